# Optimizing a Trainium2 kernel written in Bass

```python
import math
import jax, jax.numpy as jnp
from jax import lax
import numpy as np

D_MODEL = 4096
BATCH = 4
SEQ = 4096
DEPTH = 2

CHUNK = 64
Q_BLOCK = 128
D_MIX = D_MODEL
ATT_WIDTH = D_MIX // 2
ATT_HEAD_DIM = 128
ATT_HEADS = ATT_WIDTH // (2 * ATT_HEAD_DIM)
SSD_WIDTH = D_MIX - ATT_WIDTH
SSD_HEAD_DIM = 64
SSD_HEADS = SSD_WIDTH // SSD_HEAD_DIM
SSD_GROUPS = 8
SSD_STATE = 128
SSD_CONV = 4
SSD_CONV_DIM = SSD_WIDTH + 2 * SSD_GROUPS * SSD_STATE
D_FF = 11008
FFN_CONV = 3
LN_EPS = 1e-5
RMS_EPS = 1e-5
N_IN = 3 * ATT_WIDTH + SSD_WIDTH + SSD_CONV_DIM + SSD_HEADS
IN_SPLITS = [ATT_WIDTH, 2 * ATT_WIDTH, 3 * ATT_WIDTH,
             3 * ATT_WIDTH + SSD_WIDTH,
             3 * ATT_WIDTH + SSD_WIDTH + SSD_CONV_DIM]

kernel_name = 'chunk_causal_hybrid_diffattn_ssd'


def layer_norm(x, g, b):
    xf = x.astype(jnp.float32)
    mu = jnp.mean(xf, axis=-1, keepdims=True)
    var = jnp.mean(jnp.square(xf - mu), axis=-1, keepdims=True)
    y = (xf - mu) * lax.rsqrt(var + LN_EPS)
    return (y * g.astype(jnp.float32) + b.astype(jnp.float32)).astype(x.dtype)


def rms_norm(x, w):
    xf = x.astype(jnp.float32)
    y = xf * lax.rsqrt(jnp.mean(jnp.square(xf), axis=-1, keepdims=True) + RMS_EPS)
    return (y * w.astype(jnp.float32)).astype(x.dtype)


def causal_depthwise_conv(x, w, b):
    k_width = w.shape[0]
    seq = x.shape[1]
    xp = jnp.pad(x, ((0, 0), (k_width - 1, 0), (0, 0)))
    y = b + w[0] * xp[:, 0:seq]
    for j in range(1, k_width):
        y = y + w[j] * xp[:, j:j + seq]
    return y


def alibi_slopes():
    return jnp.asarray(np.array([2.0 ** (-8.0 * (h + 1) / ATT_HEADS)
                                 for h in range(ATT_HEADS)], dtype=np.float32))


def diff_attention(q, k, v, lam_params, norm_w, layer_idx):
    b, seq, _ = q.shape
    q = q.reshape(b, seq, ATT_HEADS, 2, ATT_HEAD_DIM)
    k = k.reshape(b, seq, ATT_HEADS, 2, ATT_HEAD_DIM)
    v = v.reshape(b, seq, ATT_HEADS, 2 * ATT_HEAD_DIM)
    lam_init = 0.8 - 0.6 * math.exp(-0.3 * layer_idx)
    lp = lam_params.astype(jnp.float32)
    lam = jnp.exp(jnp.sum(lp[0] * lp[1])) - jnp.exp(jnp.sum(lp[2] * lp[3])) + lam_init
    slopes = alibi_slopes()
    pos = jnp.arange(seq)
    scale = ATT_HEAD_DIM ** -0.5
    outs = []
    for i in range(seq // Q_BLOCK):
        q0 = i * Q_BLOCK
        k_end = q0 + Q_BLOCK
        qb = q[:, q0:k_end].astype(jnp.float32) * scale
        kb = k[:, :k_end].astype(jnp.float32)
        s = jnp.einsum('bqhjd,bkhjd->bhjqk', qb, kb)
        tq = pos[q0:k_end]
        tk = pos[:k_end]
        dist = jnp.abs(tq[:, None] - tk[None, :]).astype(jnp.float32)
        allowed = (tk // CHUNK)[None, :] <= (tq // CHUNK)[:, None]
        bias = jnp.where(allowed[None], -slopes[:, None, None] * dist[None], -jnp.inf)
        p = jax.nn.softmax(s + bias[None, :, None], axis=-1)
        a = p[:, :, 0] - lam * p[:, :, 1]
        outs.append(jnp.einsum('bhqk,bkhe->bqhe', a.astype(v.dtype), v[:, :k_end]))
    o = jnp.concatenate(outs, axis=1)
    o = rms_norm(o, norm_w) * (1.0 - lam_init)
    return o.reshape(b, seq, ATT_WIDTH)


def ssd_chunked_scan(xdt, da, bm, cm):
    b, seq, n_heads, p_dim = xdt.shape
    nc = seq // CHUNK
    r = n_heads // SSD_GROUPS
    x = xdt.reshape(b, nc, CHUNK, SSD_GROUPS, r, p_dim)
    bm = bm.reshape(b, nc, CHUNK, SSD_GROUPS, SSD_STATE)
    cm = cm.reshape(b, nc, CHUNK, SSD_GROUPS, SSD_STATE)
    da = da.reshape(b, nc, CHUNK, SSD_GROUPS, r).transpose(0, 3, 4, 1, 2)
    a_cs = jnp.cumsum(da, axis=-1)
    li = jnp.arange(CHUNK)
    causal = li[:, None] >= li[None, :]
    seg = a_cs[..., :, None] - a_cs[..., None, :]
    decay_in = jnp.exp(jnp.where(causal, seg, -jnp.inf))
    cb = jnp.einsum('bclgn,bcsgn->bgcls', cm, bm)
    y_diag = jnp.einsum('bgrcls,bcsgrp->bclgrp', cb[:, :, None] * decay_in, x)
    decay_states = jnp.exp(a_cs[..., -1:] - a_cs).transpose(0, 3, 4, 1, 2)
    states = jnp.einsum('bcsgn,bcsgrp->bcgrpn', bm, x * decay_states[..., None])
    chunk_decay = jnp.exp(a_cs[..., -1])

    def step(h, inp):
        st, dec = inp
        return dec[..., None, None] * h + st, h

    h0 = jnp.zeros((b, SSD_GROUPS, r, p_dim, SSD_STATE), dtype=jnp.float32)
    _, prev = lax.scan(step, h0, (states.transpose(1, 0, 2, 3, 4, 5),
                                  chunk_decay.transpose(3, 0, 1, 2)))
    prev = prev.transpose(1, 0, 2, 3, 4, 5)
    state_decay = jnp.exp(a_cs).transpose(0, 3, 4, 1, 2)
    y_off = jnp.einsum('bclgn,bcgrpn->bclgrp', cm, prev) * state_decay[..., None]
    return (y_diag + y_off).reshape(b, seq, n_heads, p_dim)


def ssd_mixer(z, xbc, dt_raw, conv_w, conv_b, dt_bias, a_log, d_skip, norm_w):
    b, seq, _ = z.shape
    xbc = jax.nn.silu(causal_depthwise_conv(xbc, conv_w, conv_b))
    xs, bm, cm = jnp.split(xbc, [SSD_WIDTH, SSD_WIDTH + SSD_GROUPS * SSD_STATE], axis=-1)
    xs = xs.reshape(b, seq, SSD_HEADS, SSD_HEAD_DIM).astype(jnp.float32)
    bm = bm.reshape(b, seq, SSD_GROUPS, SSD_STATE).astype(jnp.float32)
    cm = cm.reshape(b, seq, SSD_GROUPS, SSD_STATE).astype(jnp.float32)
    dt = jax.nn.softplus(dt_raw.astype(jnp.float32) + dt_bias.astype(jnp.float32))
    a = -jnp.exp(a_log.astype(jnp.float32))
    y = ssd_chunked_scan(xs * dt[..., None], dt * a, bm, cm)
    y = y + d_skip.astype(jnp.float32)[:, None] * xs
    y = y.reshape(b, seq, SSD_WIDTH) * jax.nn.silu(z.astype(jnp.float32))
    y = rms_norm(y.reshape(b, seq, SSD_GROUPS, SSD_WIDTH // SSD_GROUPS),
                 norm_w.reshape(SSD_GROUPS, SSD_WIDTH // SSD_GROUPS))
    return y.reshape(b, seq, SSD_WIDTH).astype(z.dtype)


def conv_glu_ffn(h, w_gate, w_up, conv_w, conv_b, w_down):
    g = causal_depthwise_conv(h @ w_gate, conv_w, conv_b)
    return (jax.nn.silu(g) * (h @ w_up)) @ w_down


def setup_inputs(seed: int = 0) -> dict:
    key = jax.random.key(seed)
    ks = jax.random.split(key, 24)
    f32 = jnp.float32
    beta = (8.0 * DEPTH) ** -0.25
    nrm = lambda k, shape, s: jax.random.normal(k, shape, f32) * s
    dt0 = jnp.exp(jax.random.uniform(ks[9], (DEPTH, SSD_HEADS), f32,
                                     math.log(1e-3), math.log(1e-1)))
    return {
        'x': nrm(ks[0], (BATCH, SEQ, D_MODEL), 1.0),
        'c': nrm(ks[1], (BATCH, D_MODEL), 1.0),
        'w_mod': nrm(ks[2], (DEPTH, D_MODEL, 6 * D_MODEL), D_MODEL ** -0.5),
        'b_mod': nrm(ks[3], (DEPTH, 6 * D_MODEL), 0.02),
        'w_in': nrm(ks[4], (DEPTH, D_MODEL, N_IN), D_MODEL ** -0.5),
        'diff_lambda': nrm(ks[5], (DEPTH, 4, ATT_HEAD_DIM), 0.1),
        'diff_norm_w': 1.0 + nrm(ks[6], (DEPTH, 2 * ATT_HEAD_DIM), 0.02),
        'ssd_conv_w': nrm(ks[7], (DEPTH, SSD_CONV, SSD_CONV_DIM), SSD_CONV ** -0.5),
        'ssd_conv_b': nrm(ks[8], (DEPTH, SSD_CONV_DIM), 0.02),
        'ssd_dt_bias': dt0 + jnp.log(-jnp.expm1(-dt0)),
        'ssd_a_log': jnp.log(jax.random.uniform(ks[10], (DEPTH, SSD_HEADS), f32, 1.0, 16.0)),
        'ssd_d': 1.0 + nrm(ks[11], (DEPTH, SSD_HEADS), 0.02),
        'ssd_norm_w': 1.0 + nrm(ks[12], (DEPTH, SSD_WIDTH), 0.02),
        'w_out': nrm(ks[13], (DEPTH, D_MIX, D_MODEL), beta * D_MIX ** -0.5),
        'ln1_g': 1.0 + nrm(ks[14], (DEPTH, D_MODEL), 0.02),
        'ln1_b': nrm(ks[15], (DEPTH, D_MODEL), 0.02),
        'w_gate': nrm(ks[16], (DEPTH, D_MODEL, D_FF), D_MODEL ** -0.5),
        'w_up': nrm(ks[17], (DEPTH, D_MODEL, D_FF), D_MODEL ** -0.5),
        'ffn_conv_w': nrm(ks[18], (DEPTH, FFN_CONV, D_FF), FFN_CONV ** -0.5),
        'ffn_conv_b': nrm(ks[19], (DEPTH, D_FF), 0.02),
        'w_down': nrm(ks[20], (DEPTH, D_FF, D_MODEL), beta * D_FF ** -0.5),
        'ln2_g': 1.0 + nrm(ks[21], (DEPTH, D_MODEL), 0.02),
        'ln2_b': nrm(ks[22], (DEPTH, D_MODEL), 0.02),
    }


def reference(x, c, w_mod, b_mod, w_in, diff_lambda, diff_norm_w, ssd_conv_w, ssd_conv_b,
              ssd_dt_bias, ssd_a_log, ssd_d, ssd_norm_w, w_out, ln1_g, ln1_b,
              w_gate, w_up, ffn_conv_w, ffn_conv_b, w_down, ln2_g, ln2_b):
    alpha = (2.0 * DEPTH) ** 0.25
    c_act = jax.nn.silu(c)
    for l in range(DEPTH):
        mod = (c_act @ w_mod[l] + b_mod[l])[:, None, :]
        shift1, scale1, gate1, shift2, scale2, gate2 = jnp.split(mod, 6, axis=-1)
        h = x * (1.0 + scale1) + shift1
        proj = h @ w_in[l]
        q, k, v, z, xbc, dt_raw = jnp.split(proj, IN_SPLITS, axis=-1)
        y_att = diff_attention(q, k, v, diff_lambda[l], diff_norm_w[l], l)
        y_ssd = ssd_mixer(z, xbc, dt_raw, ssd_conv_w[l], ssd_conv_b[l], ssd_dt_bias[l],
                          ssd_a_log[l], ssd_d[l], ssd_norm_w[l])
        m = jnp.concatenate([y_att, y_ssd], axis=-1) @ w_out[l]
        x = layer_norm(alpha * x + gate1 * m, ln1_g[l], ln1_b[l])
        h = x * (1.0 + scale2) + shift2
        f = conv_glu_ffn(h, w_gate[l], w_up[l], ffn_conv_w[l], ffn_conv_b[l], w_down[l])
        x = layer_norm(alpha * x + gate2 * f, ln2_g[l], ln2_b[l])
    return x
```

```python
import math
import os
from contextlib import ExitStack

import numpy as np
import concourse.bass as bass
import concourse.mybir as mybir
from concourse.bass_utils import run_bass_kernel_spmd

F32 = mybir.dt.float32
BF16 = mybir.dt.bfloat16
AF = mybir.ActivationFunctionType
ALU = mybir.AluOpType
AX = mybir.AxisListType

LN_EPS = 1e-5
RMS_EPS = 1e-5


class Cfg:
    def __init__(self, D=4096, S=4096, NH=8, NG=8, DFF=11008, L=2):
        self.D, self.S, self.NH, self.NG, self.DFF, self.L = D, S, NH, NG, DFF, L
        self.AW = NH * 256
        self.SW = NG * 256
        self.NHS = NG * 4
        self.DMIX = self.AW + self.SW
        self.XBC = self.SW + 2 * NG * 128
        self.NIN = 3 * self.AW + self.SW + self.XBC + self.NHS
        self.KC = D // 128
        self.alpha = (2.0 * L) ** 0.25


class Sy:
    def __init__(self, nc, es):
        self.nc = nc
        self.es = es
        self.engs = {"pe": nc.tensor, "dve": nc.vector, "act": nc.scalar,
                     "pool": nc.gpsimd, "sp": nc.sync}
        self.sem = {}
        self.cnt = {}
        for e in ("pe", "dve", "act", "pool"):
            self.sem[e] = es.enter_context(nc.semaphore("s_" + e))
            self.cnt[e] = 0
        self.waited = {}
        self.st = {}
        self.pend_r = []
        self.pend_w = []
        self.nwait = 0

    def _lane(self, name):
        if name not in self.sem:
            self.sem[name] = self.es.enter_context(self.nc.semaphore("l_" + name))
            self.cnt[name] = 0
        return name

    def _wait(self, e, tok):
        sk, val = tok
        if e == "pe" and sk == "pe":
            return
        k = (e, sk)
        if self.waited.get(k, 0) >= val:
            return
        self.waited[k] = val
        self.engs[e].wait_ge(self.sem[sk], val)
        self.nwait += 1

    def _deps(self, e, reads, writes, own=None):
        for k in reads:
            s = self.st.get(k)
            if s and s[0] and s[0][0] != own:
                self._wait(e, s[0])
        for k in writes:
            s = self.st.get(k)
            if s:
                if s[0] and s[0][0] != own:
                    self._wait(e, s[0])
                for t in s[1]:
                    if t[0] != own:
                        self._wait(e, t)

    def _commit(self, tok, reads, writes):
        for k in reads:
            s = self.st.setdefault(k, [None, []])
            s[1] = [t for t in s[1] if t[0] != tok[0]] + [tok]
        for k in writes:
            self.st[k] = [tok, []]

    def op(self, e, fn, reads=(), writes=(), signal=True):
        self._deps(e, reads, writes)
        ins = fn(self.engs[e])
        if e == "pe" and not signal:
            self.pend_r += list(reads)
            self.pend_w += list(writes)
            return ins
        self.cnt[e] += 1
        ins.then_inc(self.sem[e], 1)
        tok = (e, self.cnt[e])
        if e == "pe":
            reads = list(reads) + self.pend_r
            writes = list(writes) + self.pend_w
            self.pend_r, self.pend_w = [], []
        self._commit(tok, reads, writes)
        return ins

    def dma(self, q, out, in_, lane, reads=(), writes=()):
        self._lane(lane)
        self._deps(q, reads, writes, own=lane)
        ins = self.engs[q].dma_start(out=out, in_=in_)
        self.cnt[lane] += 16
        ins.then_inc(self.sem[lane], 16)
        self._commit((lane, self.cnt[lane]), reads, writes)
        return ins

    def barrier(self):
        for e in ("pe", "dve", "act", "pool", "sp"):
            for sk, c in self.cnt.items():
                if c > 0 and sk != e:
                    self._wait(e, (sk, c))
        self.st = {}


def _chunks(n, c):
    return [(i, min(c, n - i)) for i in range(0, n, c)]


class Prog:
    def __init__(self, cfg, debug=False, stop_after=None):
        self.cfg = cfg
        self.debug = debug
        self.stop_after = stop_after
        self.nc = bass.Bass("TRN2", target_bir_lowering=False)
        self.dbg_outs = []

    def din(self, name, shape, dt=F32):
        return self.nc.dram_tensor(name, list(shape), dt, kind="ExternalInput").ap()

    def dscr(self, name, shape, dt):
        kind = "ExternalOutput" if self.debug else "Internal"
        if self.debug:
            self.dbg_outs.append(name)
        return self.nc.dram_tensor(name, list(shape), dt, kind=kind).ap()

    def sb(self, es, name, shape, dt):
        self._nsb = getattr(self, "_nsb", 0) + 1
        return es.enter_context(self.nc.sbuf_tensor("%s_%d" % (name, self._nsb), list(shape), dt))

    def build(self):
        cfg, nc = self.cfg, self.nc
        D, S, L = cfg.D, cfg.S, cfg.L
        self.i_x = self.din("x", [S, D])
        self.i_c = self.din("c", [cfg.KC, 128])
        self.i_wmod = self.din("w_mod", [L, D, 6 * D])
        self.i_bmod = self.din("b_mod", [L, 6 * cfg.KC, 128])
        self.i_win = self.din("w_in", [L, D, cfg.NIN])
        self.i_lam = self.din("diff_lambda", [L, 1, 512])
        self.i_dnw = self.din("diff_norm_w", [L, 1, 256])
        self.i_scw = self.din("ssd_conv_w", [L, 4 * cfg.XBC // 128, 128])
        self.i_scb = self.din("ssd_conv_b", [L, cfg.XBC // 128, 128])
        self.i_dtb = self.din("ssd_dt_bias", [L, 1, cfg.NHS])
        self.i_alog = self.din("ssd_a_log", [L, 1, cfg.NHS])
        self.i_sd = self.din("ssd_d", [L, 1, cfg.NHS])
        self.i_snw = self.din("ssd_norm_w", [L, 1, cfg.SW])
        self.i_wout = self.din("w_out", [L, cfg.DMIX, D])
        self.i_ln1g = self.din("ln1_g", [L, 1, D])
        self.i_ln1b = self.din("ln1_b", [L, 1, D])
        self.i_wg = self.din("w_gate", [L, D, cfg.DFF])
        self.i_wu = self.din("w_up", [L, D, cfg.DFF])
        self.i_fcw = self.din("ffn_conv_w", [L, 3 * cfg.DFF // 128, 128])
        self.i_fcb = self.din("ffn_conv_b", [L, cfg.DFF // 128, 128])
        self.i_wd = self.din("w_down", [L, cfg.DFF, D])
        self.i_ln2g = self.din("ln2_g", [L, 1, D])
        self.i_ln2b = self.din("ln2_b", [L, 1, D])
        self.o_out = nc.dram_tensor("out", [S, D], F32, kind="ExternalOutput").ap()

        self.w_in = self.dscr("wb_in", [L, D, cfg.NIN], BF16)
        self.w_out = self.dscr("wb_out", [L, cfg.DMIX, D], BF16)
        self.w_g = self.dscr("wb_g", [L, D, cfg.DFF], BF16)
        self.w_u = self.dscr("wb_u", [L, D, cfg.DFF], BF16)
        self.w_d = self.dscr("wb_d", [L, cfg.DFF, D], BF16)
        self.qT = self.dscr("s_qT", [cfg.AW, S], BF16)
        self.kT = self.dscr("s_kT", [cfg.AW, S], BF16)
        self.v = self.dscr("s_v", [S, cfg.AW], BF16)
        self.sz = self.dscr("s_sz", [S, cfg.SW], F32)
        self.xsT = self.dscr("s_xsT", [cfg.SW, S], F32)
        self.BT = self.dscr("s_BT", [cfg.NG * 128, S], BF16)
        self.CT = self.dscr("s_CT", [cfg.NG * 128, S], BF16)
        self.dt = self.dscr("s_dt", [S, cfg.NHS], F32)
        self.mix = self.dscr("s_mix", [S, cfg.DMIX], BF16)
        self.x1 = self.dscr("s_x1", [S, D], F32)
        self.actT = self.dscr("s_actT", [cfg.DFF, S], BF16)
        self.xl = self.dscr("s_xl", [S, D], F32)

        with ExitStack() as es:
            self.sy = Sy(nc, es)
            self.ps = [es.enter_context(nc.psum_tensor("ps%d" % i, [128, 512], F32))
                       for i in range(8)]
            self.consts(es)
            self.cast_weights()
            for l in range(L):
                self.phase_mod(l)
            self.sy.barrier()
            for l in range(L):
                xin = self.i_x if l == 0 else self.xl
                xout = self.o_out if l == L - 1 else self.xl
                for pi, ph in enumerate((self.phase_inproj, self.phase_attn, self.phase_ssd,
                                         self.phase_outproj, self.phase_ffn1, self.phase_ffn2)):
                    if self.stop_after is not None and (l, pi) > self.stop_after:
                        continue
                    ph(l, xin, xout)
                    self.sy.barrier()
            self.sy.barrier()
        return nc

    def consts(self, es):
        sy, nc, cfg = self.sy, self.nc, self.cfg
        self.identf = self.sb(es, "identf", [128, 128], F32)
        self.identb = self.sb(es, "identb", [128, 128], BF16)
        self.onesf = self.sb(es, "onesf", [128, 128], F32)
        self.tri = self.sb(es, "tri", [128, 128], F32)
        self.ltri = self.sb(es, "ltri", [128, 128], F32)
        self.modT = [self.sb(es, "modT%d" % l, [128, 6 * cfg.KC], F32) for l in range(cfg.L)]
        self.sc1p = [self.sb(es, "sc1p%d" % l, [128, 2 * cfg.KC], F32) for l in range(cfg.L)]
        sy.op("pool", lambda e: e.memset(self.onesf[:], 1.0), writes=["onesf"])
        sy.op("pool", lambda e: e.affine_select(
            out=self.identf[:], in_=self.onesf[:], pattern=[[-1, 128]],
            compare_op=ALU.is_equal, fill=0.0, base=0, channel_multiplier=1),
            reads=["onesf"], writes=["identf"])
        sy.op("pool", lambda e: e.affine_select(
            out=self.tri[:], in_=self.onesf[:], pattern=[[1, 128]],
            compare_op=ALU.is_ge, fill=0.0, base=0, channel_multiplier=-1),
            reads=["onesf"], writes=["tri"])
        sy.op("pool", lambda e: e.affine_select(
            out=self.ltri[:], in_=self.onesf[:], pattern=[[-1, 128]],
            compare_op=ALU.is_gt, fill=0.0, base=0, channel_multiplier=1),
            reads=["onesf"], writes=["ltri"])
        sy.op("dve", lambda e: e.tensor_copy(self.identb[:], self.identf[:]),
              reads=["identf"], writes=["identb"])
        self.epsr = self.sb(es, "epsr", [128, 1], F32)
        self.epsl = self.sb(es, "epsl", [128, 1], F32)
        self.zeroc = self.sb(es, "zeroc", [128, 1], F32)
        sy.op("pool", lambda e: e.memset(self.zeroc[:], 0.0), writes=["zeroc"])
        sy.op("pool", lambda e: e.memset(self.epsr[:], RMS_EPS), writes=["epsr"])
        sy.op("pool", lambda e: e.memset(self.epsl[:], LN_EPS), writes=["epsl"])

    def cast_weights(self):
        sy, cfg = self.sy, self.cfg
        n = 0
        for l in range(cfg.L):
            for src, dst, rows in ((self.i_win, self.w_in, cfg.D), (self.i_wout, self.w_out, cfg.DMIX),
                                   (self.i_wg, self.w_g, cfg.D), (self.i_wu, self.w_u, cfg.D),
                                   (self.i_wd, self.w_d, cfg.DFF)):
                for r0, nr in _chunks(rows, 1024):
                    sy.dma("pool", dst[l, r0:r0 + nr, :], src[l, r0:r0 + nr, :], "wcast%d" % (n % 4))
                    n += 1

    def load_cols(self, es, dst, dst_key, src_rows, n, tag):
        sy = self.sy
        tmp = self.sb(es, "lc_" + tag, [128, 128], F32)
        for j0, nj in _chunks(n, 128):
            sy.dma("sp", tmp[0:nj, :], src_rows[j0:j0 + nj, :], "lc", writes=["lc_tmp"])
            bank = self.ps[7]
            sy.op("pe", lambda e: e.transpose(out=bank[:, 0:nj], in_=tmp[0:nj, :], identity=self.identf[0:nj, 0:nj]),
                  reads=["lc_tmp", "identf"], writes=["ps7"])
            sy.op("dve", lambda e: e.tensor_copy(dst[:, j0:j0 + nj], bank[:, 0:nj]),
                  reads=["ps7"], writes=[dst_key])

    def phase_mod(self, l):
        sy, cfg = self.sy, self.cfg
        KC, D = cfg.KC, cfg.D
        NJ = 6 * KC
        with ExitStack() as es:
            ca = self.sb(es, "ca", [128, KC], F32)
            cas = self.sb(es, "cas", [128, KC], F32)
            bm = self.sb(es, "bm", [128, NJ], F32)
            wsl = [self.sb(es, "wms%d" % i, [128, KC, 256], F32) for i in range(2)]
            self.load_cols(es, ca, "ca", self.i_c, KC, "c")
            self.load_cols(es, bm, "bm", self.i_bmod[l], NJ, "b")
            sy.op("act", lambda e: e.activation(out=cas[:], in_=ca[:], func=AF.Silu),
                  reads=["ca"], writes=["cas"])
            psm = self.ps[6]
            blocks = _chunks(6 * D, 256)
            for bi, (c0, n) in enumerate(blocks):
                w = wsl[bi % 2]
                sy.dma("sp", w[:, :, 0:n],
                       self.i_wmod[l, :, c0:c0 + n].rearrange("(k p) n -> p k n", p=128),
                       "wms%d" % (bi % 2), writes=["wms%d" % (bi % 2)])
                for jj in range(n // 128):
                    j = c0 // 128 + jj
                    for k in range(KC):
                        sy.op("pe", lambda e: e.matmul(psm[:, j:j + 1], w[:, k, jj * 128:(jj + 1) * 128],
                                                       cas[:, k:k + 1], start=(k == 0), stop=(k == KC - 1)),
                              reads=["wms%d" % (bi % 2), "cas"], writes=["ps6"],
                              signal=(k == KC - 1))
            sy.op("dve", lambda e: e.tensor_tensor(self.modT[l][:], psm[:, 0:NJ], bm[:], ALU.add),
                  reads=["ps6", "bm"], writes=["modT%d" % l])
            sy.op("dve", lambda e: e.tensor_scalar_add(self.sc1p[l][:, 0:KC], self.modT[l][:, KC:2 * KC], 1.0),
                  reads=["modT%d" % l], writes=["sc1pa%d" % l])
            sy.op("dve", lambda e: e.tensor_scalar_add(self.sc1p[l][:, KC:2 * KC], self.modT[l][:, 4 * KC:5 * KC], 1.0),
                  reads=["modT%d" % l], writes=["sc1pb%d" % l])
            sy.barrier()

    def build_hT(self, xin, t0, T, hT, xst, scale_ap, shift_ap, tag):
        sy, cfg = self.sy, self.cfg
        KC = cfg.KC
        G = min(4, KC)
        nb = 0
        for sub in range(T // 128):
            xs = xst[sub % 2]
            xk = "xst%d" % (sub % 2)
            sy.dma("sp", xs[:], xin[t0 + sub * 128:t0 + (sub + 1) * 128, :], xk, writes=[xk])
            for kg in range(KC // G):
                bi = nb % 2
                nb += 1
                bank = self.ps[bi]
                for kk in range(G):
                    k = kg * G + kk
                    sy.op("pe", lambda e: e.transpose(out=bank[:, kk * 128:(kk + 1) * 128],
                                                      in_=xs[:, k * 128:(k + 1) * 128], identity=self.identf[:]),
                          reads=[xk], writes=["ps%d" % bi], signal=(kk == G - 1))
                for kk in range(G):
                    k = kg * G + kk
                    sy.op("act", lambda e: e.activation(out=hT[:, k, sub * 128:(sub + 1) * 128],
                                                        in_=bank[:, kk * 128:(kk + 1) * 128], func=AF.Identity,
                                                        bias=shift_ap[:, k:k + 1], scale=scale_ap[:, k:k + 1]),
                          reads=["ps%d" % bi], writes=[tag])

    def phase_inproj(self, l, xin, xout):
        sy, cfg = self.sy, self.cfg
        KC, D, S = cfg.KC, cfg.D, cfg.S
        T = min(512, S)
        NXC = cfg.XBC // 128
        AW, SW = cfg.AW, cfg.SW
        segs = [("q", 0, AW, "a"), ("k", AW, AW, "a"), ("v", 2 * AW, AW, "b"),
                ("z", 3 * AW, SW, "b"), ("xbc", 3 * AW + SW, cfg.XBC, "a"),
                ("dt", 3 * AW + SW + cfg.XBC, cfg.NHS, "b")]
        blocks = []
        for name, c0, n, form in segs:
            for b0, bn in _chunks(n, 512):
                blocks.append((name, c0, b0, bn, form))
        with ExitStack() as es:
            hT = self.sb(es, "hT", [128, KC, T], BF16)
            wb = [self.sb(es, "wb%d" % i, [128, KC, 512], BF16) for i in range(2)]
            xst = [self.sb(es, "xst%d" % i, [128, D], F32) for i in range(2)]
            ost = [self.sb(es, "ost%d" % i, [128, 512], F32) for i in range(3)]
            obf = [self.sb(es, "obf%d" % i, [128, 512], BF16) for i in range(3)]
            xpad = [self.sb(es, "xpad%d" % i, [128, 3 + 512], F32) for i in range(2)]
            cvt = [self.sb(es, "cvt%d" % i, [128, 512], F32) for i in range(2)]
            halo = self.sb(es, "halo", [128, NXC, 3], F32)
            cw = self.sb(es, "cw", [128, 4 * NXC], F32)
            cb = self.sb(es, "cb", [128, NXC], F32)
            dtb = self.sb(es, "dtb", [128, cfg.NHS], F32)
            sp1 = self.sb(es, "sp1", [128, cfg.NHS], F32)
            sp2 = self.sb(es, "sp2", [128, cfg.NHS], F32)
            sp3 = self.sb(es, "sp3", [128, cfg.NHS], F32)
            self.load_cols(es, cw, "cw", self.i_scw[l], 4 * NXC, "cw")
            self.load_cols(es, cb, "cb", self.i_scb[l], NXC, "cb")
            sy.dma("sp", dtb[:], self.i_dtb[l].to_broadcast([128, cfg.NHS]), "dtb", writes=["dtb"])
            sy.op("pool", lambda e: e.memset(halo[:], 0.0), writes=["halo%d" % i for i in range(NXC)])
            scale_ap = self.sc1p[l][:, 0:KC]
            shift_ap = self.modT[l][:, 0:KC]
            nev = 0
            nst = 0
            for tt in range(S // T):
                t0 = tt * T
                self.build_hT(xin, t0, T, hT, xst, scale_ap, shift_ap, "hT")
                for bi, (name, c0, b0, bn, form) in enumerate(blocks):
                    gi = tt * len(blocks) + bi
                    w = wb[gi % 2]
                    wk = "wb%d" % (gi % 2)
                    sy.dma("sp", w[:, :, 0:bn],
                           self.w_in[l, :, c0 + b0:c0 + b0 + bn].rearrange("(k p) n -> p k n", p=128),
                           wk, writes=[wk])
                    if form == "a":
                        for oc in range(bn // 128):
                            pb = 2 + nev % 4
                            nev += 1
                            bank = self.ps[pb]
                            pk = "ps%d" % pb
                            for k in range(KC):
                                sy.op("pe", lambda e: e.matmul(bank[:, 0:T], w[:, k, oc * 128:(oc + 1) * 128],
                                                               hT[:, k, :], start=(k == 0), stop=(k == KC - 1)),
                                      reads=[wk, "hT"], writes=[pk], signal=(k == KC - 1))
                            fc = (b0 + oc * 128) // 128
                            si = nst % 3
                            nst += 1
                            if name == "q":
                                sy.op("act", lambda e: e.activation(out=obf[si][:, 0:T], in_=bank[:, 0:T],
                                                                    func=AF.Copy, scale=1.0 / math.sqrt(128.0)),
                                      reads=[pk], writes=["obf%d" % si])
                                sy.dma("pool", self.qT[fc * 128:(fc + 1) * 128, t0:t0 + T], obf[si][:, 0:T],
                                       "obf%d" % si, reads=["obf%d" % si])
                            elif name == "k":
                                sy.op("dve", lambda e: e.tensor_copy(obf[si][:, 0:T], bank[:, 0:T]),
                                      reads=[pk], writes=["obf%d" % si])
                                sy.dma("pool", self.kT[fc * 128:(fc + 1) * 128, t0:t0 + T], obf[si][:, 0:T],
                                       "obf%d" % si, reads=["obf%d" % si])
                            else:
                                xp = xpad[fc % 2]
                                xpk = "xpad%d" % (fc % 2)
                                cv = cvt[fc % 2]
                                cvk = "cvt%d" % (fc % 2)
                                sy.op("act", lambda e: e.activation(out=xp[:, 3:3 + T], in_=bank[:, 0:T], func=AF.Copy),
                                      reads=[pk], writes=[xpk + "m"])
                                sy.op("pool", lambda e: e.tensor_copy(xp[:, 0:3], halo[:, fc, :]),
                                      reads=["halo%d" % fc], writes=[xpk + "h"])
                                sy.op("dve", lambda e: e.tensor_scalar(cv[:, 0:T], xp[:, 3:3 + T],
                                                                       cw[:, 3 * NXC + fc:3 * NXC + fc + 1],
                                                                       cb[:, fc:fc + 1], ALU.mult, ALU.add),
                                      reads=[xpk + "m", xpk + "h", "cw", "cb"], writes=[cvk])
                                for j in range(3):
                                    sy.op("dve", lambda e: e.scalar_tensor_tensor(
                                        cv[:, 0:T], xp[:, j:j + T], cw[:, j * NXC + fc:j * NXC + fc + 1],
                                        cv[:, 0:T], ALU.mult, ALU.add),
                                        reads=[xpk + "m", xpk + "h", cvk], writes=[cvk])
                                sy.op("pool", lambda e: e.tensor_copy(halo[:, fc, :], xp[:, T:T + 3]),
                                      reads=[xpk + "m"], writes=["halo%d" % fc])
                                if fc < SW // 128:
                                    sy.op("act", lambda e: e.activation(out=ost[si][:, 0:T], in_=cv[:, 0:T], func=AF.Silu),
                                          reads=[cvk], writes=["ost%d" % si])
                                    sy.dma("pool", self.xsT[fc * 128:(fc + 1) * 128, t0:t0 + T], ost[si][:, 0:T],
                                           "ost%d" % si, reads=["ost%d" % si])
                                else:
                                    sy.op("act", lambda e: e.activation(out=obf[si][:, 0:T], in_=cv[:, 0:T], func=AF.Silu),
                                          reads=[cvk], writes=["obf%d" % si])
                                    g = fc - SW // 128
                                    dst = self.BT if g < cfg.NG else self.CT
                                    g = g % cfg.NG
                                    sy.dma("pool", dst[g * 128:(g + 1) * 128, t0:t0 + T], obf[si][:, 0:T],
                                           "obf%d" % si, reads=["obf%d" % si])
                    else:
                        for sub in range(T // 128):
                            pb = 2 + nev % 4
                            nev += 1
                            bank = self.ps[pb]
                            pk = "ps%d" % pb
                            for k in range(KC):
                                sy.op("pe", lambda e: e.matmul(bank[:, 0:bn], hT[:, k, sub * 128:(sub + 1) * 128],
                                                               w[:, k, 0:bn], start=(k == 0), stop=(k == KC - 1)),
                                      reads=[wk, "hT"], writes=[pk], signal=(k == KC - 1))
                            r0 = t0 + sub * 128
                            si = nst % 3
                            nst += 1
                            if name == "v":
                                sy.op("dve", lambda e: e.tensor_copy(obf[si][:, 0:bn], bank[:, 0:bn]),
                                      reads=[pk], writes=["obf%d" % si])
                                sy.dma("pool", self.v[r0:r0 + 128, b0:b0 + bn], obf[si][:, 0:bn],
                                       "obf%d" % si, reads=["obf%d" % si])
                            elif name == "z":
                                sy.op("act", lambda e: e.activation(out=ost[si][:, 0:bn], in_=bank[:, 0:bn], func=AF.Silu),
                                      reads=[pk], writes=["ost%d" % si])
                                sy.dma("pool", self.sz[r0:r0 + 128, b0:b0 + bn], ost[si][:, 0:bn],
                                       "ost%d" % si, reads=["ost%d" % si])
                            else:
                                n = bn
                                sy.op("dve", lambda e: e.tensor_tensor(sp1[:, 0:n], bank[:, 0:n], dtb[:, 0:n], ALU.add),
                                      reads=[pk, "dtb"], writes=["sp1"])
                                sy.op("dve", lambda e: e.scalar_tensor_tensor(sp2[:, 0:n], sp1[:, 0:n], -1.0, sp1[:, 0:n], ALU.mult, ALU.max),
                                      reads=["sp1"], writes=["sp2"])
                                sy.op("act", lambda e: e.activation(out=sp3[:, 0:n], in_=sp2[:, 0:n], func=AF.Exp, scale=-1.0),
                                      reads=["sp2"], writes=["sp3"])
                                sy.op("act", lambda e: e.activation(out=sp2[:, 0:n], in_=sp3[:, 0:n], func=AF.Ln, bias=1.0),
                                      reads=["sp3"], writes=["sp2"])
                                sy.op("dve", lambda e: e.scalar_tensor_tensor(ost[si][:, 0:n], sp1[:, 0:n], 0.0, sp2[:, 0:n],
                                                                              ALU.max, ALU.add),
                                      reads=["sp1", "sp2"], writes=["ost%d" % si])
                                sy.dma("pool", self.dt[r0:r0 + 128, :], ost[si][:, 0:n],
                                       "ost%d" % si, reads=["ost%d" % si])

    def bcast_load(self, dst, src_row, n, lane, key):
        self.sy.dma("sp", dst[:, 0:n], src_row.to_broadcast([128, n]), lane, writes=[key])

    def psb(self, i):
        return self.ps[i][:].bitcast(BF16)

    def phase_attn(self, l, xin, xout):
        sy, cfg = self.sy, self.cfg
        S, NH = cfg.S, cfg.NH
        NB = S // 128
        NSTR = 2
        lam_init = 0.8 - 0.6 * math.exp(-0.3 * l)
        with ExitStack() as es:
            Dm = self.sb(es, "Dm", [128, S], F32)
            M2 = self.sb(es, "M2", [128, 128], F32)
            lamb = self.sb(es, "lamb", [128, 512], F32)
            lt = self.sb(es, "lt", [128, 256], F32)
            lsm = self.sb(es, "lsm", [128, 2], F32)
            lex = self.sb(es, "lex", [128, 2], F32)
            nlam = self.sb(es, "nlam", [128, 1], F32)
            nw = self.sb(es, "nw", [128, 256], F32)
            nws = self.sb(es, "nws", [128, 256], F32)
            kTs = [self.sb(es, "kTs%d" % j, [128, S], BF16) for j in range(2)]
            qTs = [self.sb(es, "qTs%d" % j, [128, S], BF16) for j in range(2)]
            V = self.sb(es, "V", [128, NB, 256], BF16)
            ob = [self.sb(es, "ob%d" % i, [128, 256], BF16) for i in range(2)]
            st = []
            for t in range(NSTR):
                d = {}
                d["sc"] = [self.sb(es, "sc%d" % j, [128, S], F32) for j in range(2)]
                d["Ab"] = self.sb(es, "Ab", [128, S], BF16)
                d["AT"] = self.sb(es, "AT", [128, NB, 128], BF16)
                for nm, w in (("mx", 2), ("nmx", 2), ("sm", 2), ("rs", 2), ("t1", 1), ("cc", 1), ("ss", 1), ("rstd", 1)):
                    d[nm] = self.sb(es, nm, [128, w], F32)
                d["o"] = self.sb(es, "o", [128, 256], F32)
                d["junk"] = self.sb(es, "junk", [128, 256], F32)
                st.append(d)
            sy.op("pool", lambda e: e.iota(Dm[:], pattern=[[-1, S]], base=S - 128, channel_multiplier=1,
                                           allow_small_or_imprecise_dtypes=True), writes=["Dm"])
            sy.op("dve", lambda e: e.scalar_tensor_tensor(Dm[:], Dm[:], -1.0, Dm[:], ALU.mult, ALU.min),
                  reads=["Dm"], writes=["Dm"])
            sy.op("pool", lambda e: e.memset(M2[:], 0.0), writes=["M2"])
            sy.op("pool", lambda e: e.memset(M2[0:64, 64:128], -30000.0), reads=["M2"], writes=["M2"])
            self.bcast_load(lamb, self.i_lam[l], 512, "lamb", "lamb")
            self.bcast_load(nw, self.i_dnw[l], 256, "nw", "nw")
            sy.op("dve", lambda e: e.tensor_tensor(lt[:, 0:128], lamb[:, 0:128], lamb[:, 128:256], ALU.mult),
                  reads=["lamb"], writes=["lt"])
            sy.op("dve", lambda e: e.tensor_tensor(lt[:, 128:256], lamb[:, 256:384], lamb[:, 384:512], ALU.mult),
                  reads=["lamb", "lt"], writes=["lt"])
            sy.op("dve", lambda e: e.reduce_sum(lsm[:, 0:1], lt[:, 0:128], AX.X), reads=["lt"], writes=["lsm"])
            sy.op("dve", lambda e: e.reduce_sum(lsm[:, 1:2], lt[:, 128:256], AX.X), reads=["lt", "lsm"], writes=["lsm"])
            sy.op("act", lambda e: e.activation(out=lex[:], in_=lsm[:], func=AF.Exp), reads=["lsm"], writes=["lex"])
            sy.op("dve", lambda e: e.scalar_tensor_tensor(nlam[:], lex[:, 1:2], -lam_init, lex[:, 0:1], ALU.add, ALU.subtract),
                  reads=["lex"], writes=["nlam"])
            sy.op("dve", lambda e: e.tensor_scalar_mul(nws[:], nw[:], 1.0 - lam_init), reads=["nw"], writes=["nws"])
            cnt = {"sb": 0, "ob": 0}

            def qblock(t, h, i, slope):
                d = st[t]
                sc, Ab, AT = d["sc"], d["Ab"], d["AT"]
                K_ = lambda nm: "%s_%d" % (nm, t)
                q0 = i * 128
                kend = q0 + 128
                off = S - 128 - q0
                for j in range(2):
                    sk = K_("sc%d" % j)
                    for c0, n in _chunks(kend, 512):
                        pb = 2 * t + (cnt["sb"] % 2)
                        cnt["sb"] += 1
                        bank = self.ps[pb]
                        sy.op("pe", lambda e: e.matmul(bank[:, 0:n], qTs[j][:, q0:q0 + 128], kTs[j][:, c0:c0 + n],
                                                       start=True, stop=True),
                              reads=["qTs%d" % j, "kTs%d" % j], writes=["ps%d" % pb])
                        sy.op("dve", lambda e: e.scalar_tensor_tensor(sc[j][:, c0:c0 + n], Dm[:, off + c0:off + c0 + n],
                                                                      slope, bank[:, 0:n], ALU.mult, ALU.add),
                              reads=["ps%d" % pb, "Dm"], writes=[sk])
                    sy.op("dve", lambda e: e.tensor_tensor(sc[j][:, q0:kend], sc[j][:, q0:kend], M2[:], ALU.add),
                          reads=[sk, "M2"], writes=[sk])
                    sy.op("dve", lambda e: e.reduce_max(d["mx"][:, j:j + 1], sc[j][:, 0:kend], AX.X),
                          reads=[sk], writes=[K_("mx%d" % j)])
                    yield
                sy.op("dve", lambda e: e.tensor_scalar_mul(d["nmx"][:], d["mx"][:], -1.0),
                      reads=[K_("mx0"), K_("mx1")], writes=[K_("nmx")])
                sy.op("dve", lambda e: e.memset(d["sm"][:], 0.0), writes=[K_("sm")])
                for j in range(2):
                    sk = K_("sc%d" % j)
                    sy.op("act", lambda e: e.activation(out=sc[j][:, 0:kend], in_=sc[j][:, 0:kend], func=AF.Exp,
                                                        bias=d["nmx"][:, j:j + 1], scale=1.0, accum_out=d["sm"][:, j:j + 1]),
                          reads=[sk, K_("nmx"), K_("sm")], writes=[sk, K_("sm")])
                yield
                sy.op("dve", lambda e: e.reciprocal(d["rs"][:], d["sm"][:]), reads=[K_("sm")], writes=[K_("rs")])
                sy.op("dve", lambda e: e.tensor_tensor(d["t1"][:], d["sm"][:, 0:1], d["rs"][:, 1:2], ALU.mult),
                      reads=[K_("sm"), K_("rs")], writes=[K_("t1")])
                sy.op("dve", lambda e: e.tensor_tensor(d["cc"][:], d["t1"][:], nlam[:], ALU.mult),
                      reads=[K_("t1"), "nlam"], writes=[K_("cc")])
                sy.op("dve", lambda e: e.scalar_tensor_tensor(Ab[:, 0:kend], sc[1][:, 0:kend], d["cc"][:, 0:1],
                                                              sc[0][:, 0:kend], ALU.mult, ALU.add),
                      reads=[K_("sc0"), K_("sc1"), K_("cc")], writes=[K_("Ab")])
                yield
                for g0, gn in _chunks(i + 1, 8):
                    tb = 4 + t
                    pbb = self.psb(tb)
                    for kk in range(gn):
                        kb = g0 + kk
                        sy.op("pe", lambda e: e.transpose(out=pbb[:, kk * 128:(kk + 1) * 128],
                                                          in_=Ab[:, kb * 128:(kb + 1) * 128], identity=self.identb[:]),
                              reads=[K_("Ab")], writes=["ps%d" % tb], signal=(kk == gn - 1))
                    sy.op("act", lambda e: e.activation(out=AT[:, g0:g0 + gn, :].rearrange("p a b -> p (a b)"),
                                                        in_=pbb[:, 0:gn * 128], func=AF.Copy),
                          reads=["ps%d" % tb], writes=[K_("AT")])
                    yield
                ob_i = 6 + t
                po = self.ps[ob_i]
                for kb in range(i + 1):
                    sy.op("pe", lambda e: e.matmul(po[:, 0:256], AT[:, kb, :], V[:, kb, :],
                                                   start=(kb == 0), stop=(kb == i)),
                          reads=[K_("AT"), "V"], writes=["ps%d" % ob_i], signal=(kb == i))
                yield
                sy.op("dve", lambda e: e.tensor_scalar_mul(d["o"][:], po[:, 0:256], d["rs"][:, 0:1]),
                      reads=["ps%d" % ob_i, K_("rs")], writes=[K_("o")])
                sy.op("dve", lambda e: e.memset(d["ss"][:], 0.0), writes=[K_("ss")])
                sy.op("act", lambda e: e.activation(out=d["junk"][:], in_=d["o"][:], func=AF.Square, accum_out=d["ss"][:]),
                      reads=[K_("o"), K_("ss")], writes=[K_("junk"), K_("ss")])
                sy.op("act", lambda e: e.activation(out=d["rstd"][:], in_=d["ss"][:], func=AF.Sqrt, bias=self.epsr[:, 0:1],
                                                    scale=1.0 / 256.0),
                      reads=[K_("ss")], writes=[K_("rstd")])
                sy.op("dve", lambda e: e.reciprocal(d["rstd"][:], d["rstd"][:]), reads=[K_("rstd")], writes=[K_("rstd")])
                obb = ob[cnt["ob"] % 2]
                obk = "ob%d" % (cnt["ob"] % 2)
                cnt["ob"] += 1
                sy.op("dve", lambda e: e.scalar_tensor_tensor(obb[:], d["o"][:], d["rstd"][:, 0:1], nws[:], ALU.mult, ALU.mult),
                      reads=[K_("o"), K_("rstd"), "nws"], writes=[obk])
                sy.dma("pool", self.mix[q0:q0 + 128, h * 256:(h + 1) * 256], obb[:], obk, reads=[obk])

            for h in range(NH):
                slope = 2.0 ** (-8.0 * (h + 1) / NH)
                for j in range(2):
                    r0 = (2 * h + j) * 128
                    sy.dma("sp", kTs[j][:], self.kT[r0:r0 + 128, :], "kTs%d" % j, writes=["kTs%d" % j])
                    sy.dma("sp", qTs[j][:], self.qT[r0:r0 + 128, :], "qTs%d" % j, writes=["qTs%d" % j])
                sy.dma("sp", V[:], self.v[:, h * 256:(h + 1) * 256].rearrange("(b p) e -> p b e", p=128),
                       "V", writes=["V"])
                gens = []
                free = list(range(NSTR))
                nxt = 0
                while nxt < NB or gens:
                    if nxt < NB and free:
                        t_ = free.pop(0)
                        gens.append((t_, qblock(t_, h, nxt, slope)))
                        nxt += 1
                    for item in list(gens):
                        try:
                            next(item[1])
                        except StopIteration:
                            gens.remove(item)
                            free.append(item[0])

    def phase_ssd(self, l, xin, xout):
        sy, cfg = self.sy, self.cfg
        S, NG, NHS, SW, AW = cfg.S, cfg.NG, cfg.NHS, cfg.SW, cfg.AW
        NB = S // 128
        with ExitStack() as es:
            xsTg = self.sb(es, "xsTg", [128, 2, S], F32)
            BTg = self.sb(es, "BTg", [128, S], BF16)
            CTg = self.sb(es, "CTg", [128, S], BF16)
            dta = self.sb(es, "dta", [128, NB, NHS], F32)
            Arow = self.sb(es, "Arow", [128, NHS], F32)
            Drow = self.sb(es, "Drow", [128, NHS], F32)
            Dfull = self.sb(es, "Dfull", [128, NHS, 64], F32)
            normw = self.sb(es, "normw", [128, SW], F32)
            prev = self.sb(es, "prev", [128, 256], F32)
            prevb = self.sb(es, "prevb", [128, 256], BF16)
            P = []
            for par in range(2):
                d = {}
                d["xs_t"] = self.sb(es, "xs_t", [128, 256], F32)
                d["Bt"] = self.sb(es, "Bt", [128, 128], BF16)
                for nm in ("da", "nacs", "dst", "w2"):
                    d[nm] = self.sb(es, nm, [128, 4], F32)
                for nm in ("daB", "eRow", "E", "E2"):
                    d[nm] = self.sb(es, nm, [128, 4, 128], F32)
                d["CBm"] = self.sb(es, "CBm", [128, 128], F32)
                d["LT"] = self.sb(es, "LT", [128, 4, 128], BF16)
                d["CeT"] = self.sb(es, "CeT", [128, 4, 128], BF16)
                d["xdt"] = self.sb(es, "xdt", [128, 4, 64], BF16)
                d["xdd"] = self.sb(es, "xdd", [128, 4, 64], BF16)
                P.append(d)
            tt1 = self.sb(es, "tt1", [128, 256], F32)
            yd = self.sb(es, "yd", [128, 256], F32)
            y1 = self.sb(es, "y1", [128, 256], F32)
            junk = self.sb(es, "junk", [128, 256], F32)
            szt = [self.sb(es, "szt%d" % i, [128, 256], F32) for i in range(2)]
            ss = self.sb(es, "ss", [128, 1], F32)
            rstd = self.sb(es, "rstd", [128, 1], F32)
            ob = [self.sb(es, "ob%d" % i, [128, 256], BF16) for i in range(2)]
            self.bcast_load(Arow, self.i_alog[l], NHS, "Arow", "Arow")
            sy.op("act", lambda e: e.activation(out=Arow[:], in_=Arow[:], func=AF.Exp), reads=["Arow"], writes=["Arow"])
            sy.op("dve", lambda e: e.tensor_scalar_mul(Arow[:], Arow[:], -1.0), reads=["Arow"], writes=["Arow"])
            self.bcast_load(Drow, self.i_sd[l], NHS, "Drow", "Drow")
            sy.op("dve", lambda e: e.tensor_copy(Dfull[:], Drow[:].unsqueeze(2).to_broadcast([128, NHS, 64])),
                  reads=["Drow"], writes=["Dfull"])
            self.bcast_load(normw, self.i_snw[l], SW, "normw", "normw")
            sy.dma("sp", dta[:], self.dt.rearrange("(b p) h -> p b h", p=128), "dta", writes=["dta"])

            def stage1(g, c):
                d = P[c % 2]
                K_ = lambda nm: "%s_%d" % (nm, c % 2)
                cs = slice(c * 128, (c + 1) * 128)
                dtc = dta[:, c, 4 * g:4 * g + 4]
                xs_t, Bt, da, daB, nacs, dst, w2 = d["xs_t"], d["Bt"], d["da"], d["daB"], d["nacs"], d["dst"], d["w2"]
                eRow, E, E2, CBm, LT, CeT, xdt, xdd = d["eRow"], d["E"], d["E2"], d["CBm"], d["LT"], d["CeT"], d["xdt"], d["xdd"]
                for f in range(2):
                    sy.op("pe", lambda e: e.transpose(out=self.ps[0][:, f * 128:(f + 1) * 128], in_=xsTg[:, f, cs],
                                                      identity=self.identf[:]),
                          reads=["xsTg"], writes=["ps0"], signal=(f == 1))
                sy.op("act", lambda e: e.activation(out=xs_t[:], in_=self.ps[0][:, 0:256], func=AF.Copy),
                      reads=["ps0"], writes=[K_("xs_t")])
                pb1 = self.psb(1)
                sy.op("pe", lambda e: e.transpose(out=pb1[:, 0:128], in_=BTg[:, cs], identity=self.identb[:]),
                      reads=["BTg"], writes=["ps1"])
                sy.op("dve", lambda e: e.tensor_copy(Bt[:], pb1[:, 0:128]), reads=["ps1"], writes=[K_("Bt")])
                sy.op("dve", lambda e: e.tensor_tensor(da[:], dtc, Arow[:, 4 * g:4 * g + 4], ALU.mult),
                      reads=["dta", "Arow"], writes=[K_("da")])
                sy.op("dve", lambda e: e.tensor_copy(daB[:], da[:].unsqueeze(2).to_broadcast([128, 4, 128])),
                      reads=[K_("da")], writes=[K_("daB")])
                for h in range(4):
                    sy.op("pe", lambda e: e.matmul(self.ps[2][:, h * 128:(h + 1) * 128], daB[:, h, :], self.tri[:],
                                                   start=True, stop=True),
                          reads=[K_("daB"), "tri"], writes=["ps2"], signal=(h == 3))
                sy.op("pe", lambda e: e.matmul(self.ps[3][:, 0:4], self.tri[:], da[:], start=True, stop=True),
                      reads=[K_("da"), "tri"], writes=["ps3"], signal=False)
                sy.op("pe", lambda e: e.matmul(self.ps[3][:, 4:8], self.ltri[:], da[:], start=True, stop=True),
                      reads=[K_("da"), "ltri"], writes=["ps3"])
                sy.op("pe", lambda e: e.matmul(self.ps[4][:, 0:128], BTg[:, cs], CTg[:, cs], start=True, stop=True),
                      reads=["BTg", "CTg"], writes=["ps4"])
                sy.op("dve", lambda e: e.tensor_scalar_mul(nacs[:], self.ps[3][:, 0:4], -1.0),
                      reads=["ps3"], writes=[K_("nacs")])
                sy.op("act", lambda e: e.activation(out=dst[:], in_=self.ps[3][:, 4:8], func=AF.Exp),
                      reads=["ps3"], writes=[K_("dst")])
                sy.op("act", lambda e: e.activation(out=eRow[:].rearrange("p a b -> p (a b)"), in_=self.ps[2][:, 0:512],
                                                    func=AF.Exp),
                      reads=["ps2"], writes=[K_("eRow")])
                for h in range(4):
                    sy.op("act", lambda e: e.activation(out=E[:, h, :], in_=self.ps[2][:, h * 128:(h + 1) * 128], func=AF.Exp,
                                                        bias=nacs[:, h:h + 1], scale=1.0),
                          reads=["ps2", K_("nacs")], writes=[K_("E")])
                sy.op("dve", lambda e: e.tensor_scalar_min(E2[:].rearrange("p a b -> p (a b)"),
                                                           E[:].rearrange("p a b -> p (a b)"), 1.0),
                      reads=[K_("E")], writes=[K_("E2")])
                sy.op("dve", lambda e: e.tensor_tensor(CBm[:], self.ps[4][:, 0:128], self.tri[:], ALU.mult),
                      reads=["ps4", "tri"], writes=[K_("CBm")])
                sy.op("dve", lambda e: e.tensor_tensor(LT[:], E2[:], CBm[:].unsqueeze(1).to_broadcast([128, 4, 128]), ALU.mult),
                      reads=[K_("E2"), K_("CBm")], writes=[K_("LT")])
                sy.op("pool", lambda e: e.tensor_tensor(CeT[:], eRow[:], CTg[:, cs].unsqueeze(1).to_broadcast([128, 4, 128]),
                                                        ALU.mult),
                      reads=[K_("eRow"), "CTg"], writes=[K_("CeT")])
                xv = xs_t[:].rearrange("p (h e) -> p h e", h=4)
                sy.op("dve", lambda e: e.tensor_tensor(xdt[:], xv, dtc.unsqueeze(2).to_broadcast([128, 4, 64]), ALU.mult),
                      reads=[K_("xs_t"), "dta"], writes=[K_("xdt")])
                sy.op("dve", lambda e: e.tensor_tensor(w2[:], dtc, dst[:], ALU.mult), reads=["dta", K_("dst")], writes=[K_("w2")])
                sy.op("dve", lambda e: e.tensor_tensor(xdd[:], xv, w2[:].unsqueeze(2).to_broadcast([128, 4, 64]), ALU.mult),
                      reads=[K_("xs_t"), K_("w2")], writes=[K_("xdd")])

            def stage2(g, c):
                d = P[c % 2]
                K_ = lambda nm: "%s_%d" % (nm, c % 2)
                xs_t, Bt, eRow, LT, CeT, xdt, xdd = d["xs_t"], d["Bt"], d["eRow"], d["LT"], d["CeT"], d["xdt"], d["xdd"]
                szb = szt[c % 2]
                szk = "szt%d" % (c % 2)
                sy.dma("sp", szb[:], self.sz[c * 128:(c + 1) * 128, g * 256:(g + 1) * 256], szk, writes=[szk])
                for h in range(4):
                    sy.op("pe", lambda e: e.matmul(self.ps[5][:, h * 64:(h + 1) * 64], LT[:, h, :], xdt[:, h, :],
                                                   start=True, stop=False),
                          reads=[K_("LT"), K_("xdt")], writes=["ps5"], signal=False)
                    sy.op("pe", lambda e: e.matmul(self.ps[5][:, h * 64:(h + 1) * 64], CeT[:, h, :],
                                                   prevb[:, h * 64:(h + 1) * 64], start=False, stop=True),
                          reads=[K_("CeT"), "prevb"], writes=["ps5"], signal=(h == 3))
                sy.op("pe", lambda e: e.matmul(self.ps[6][:, 0:256], Bt[:], xdd[:].rearrange("p h e -> p (h e)"),
                                               start=True, stop=True),
                      reads=[K_("Bt"), K_("xdd")], writes=["ps6"])
                sy.op("dve", lambda e: e.tensor_tensor(tt1[:].rearrange("p (h e) -> p h e", h=4),
                                                       prev[:].rearrange("p (h e) -> p h e", h=4),
                                                       eRow[:, :, 127:128].to_broadcast([128, 4, 64]), ALU.mult),
                      reads=["prev", K_("eRow")], writes=["tt1"])
                sy.op("dve", lambda e: e.tensor_tensor(prev[:], tt1[:], self.ps[6][:, 0:256], ALU.add),
                      reads=["tt1", "ps6"], writes=["prev"])
                sy.op("act", lambda e: e.activation(out=prevb[:], in_=prev[:], func=AF.Copy),
                      reads=["prev"], writes=["prevb"])
                sy.op("pool", lambda e: e.tensor_tensor(yd[:], xs_t[:],
                                                        Dfull[:, 4 * g:4 * g + 4, :].rearrange("p h e -> p (h e)"), ALU.mult),
                      reads=[K_("xs_t"), "Dfull"], writes=["yd"])
                sy.op("dve", lambda e: e.tensor_tensor(y1[:], self.ps[5][:, 0:256], yd[:], ALU.add),
                      reads=["ps5", "yd"], writes=["y1"])
                sy.op("dve", lambda e: e.tensor_tensor(y1[:], y1[:], szb[:], ALU.mult),
                      reads=["y1", szk], writes=["y1"])
                sy.op("dve", lambda e: e.memset(ss[:], 0.0), writes=["ss"])
                sy.op("act", lambda e: e.activation(out=junk[:], in_=y1[:], func=AF.Square, accum_out=ss[:]),
                      reads=["y1", "ss"], writes=["junk", "ss"])
                sy.op("act", lambda e: e.activation(out=rstd[:], in_=ss[:], func=AF.Sqrt, bias=self.epsr[:, 0:1], scale=1.0 / 256.0),
                      reads=["ss"], writes=["rstd"])
                sy.op("dve", lambda e: e.reciprocal(rstd[:], rstd[:]), reads=["rstd"], writes=["rstd"])
                obb = ob[c % 2]
                obk = "ob%d" % (c % 2)
                sy.op("dve", lambda e: e.scalar_tensor_tensor(obb[:], y1[:], rstd[:, 0:1], normw[:, g * 256:(g + 1) * 256],
                                                              ALU.mult, ALU.mult),
                      reads=["y1", "rstd", "normw"], writes=[obk])
                sy.dma("pool", self.mix[c * 128:(c + 1) * 128, AW + g * 256:AW + (g + 1) * 256], obb[:], obk, reads=[obk])

            for g in range(NG):
                for f in range(2):
                    sy.dma("sp", xsTg[:, f, :], self.xsT[(2 * g + f) * 128:(2 * g + f + 1) * 128, :], "xsTg", writes=["xsTg"])
                sy.dma("sp", BTg[:], self.BT[g * 128:(g + 1) * 128, :], "BTg", writes=["BTg"])
                sy.dma("sp", CTg[:], self.CT[g * 128:(g + 1) * 128, :], "CTg", writes=["CTg"])
                sy.op("dve", lambda e: e.memset(prev[:], 0.0), writes=["prev"])
                sy.op("dve", lambda e: e.memset(prevb[:], 0.0), writes=["prevb"])
                stage1(g, 0)
                for c in range(NB):
                    if c + 1 < NB:
                        stage1(g, c + 1)
                    stage2(g, c)

    def proj_ln(self, l, KA, load_A, w_d, resid, gate_col0, lng_in, lnb_in, dst, T):
        sy, cfg = self.sy, self.cfg
        D, S, KC = cfg.D, cfg.S, cfg.KC
        KS = 16
        kslabs = _chunks(KA, KS)
        nD = (D + 511) // 512
        with ExitStack() as es:
            AT = self.sb(es, "plA", [128, KA, T], BF16)
            wb = [self.sb(es, "plw%d" % i, [128, KS, 512], BF16) for i in range(2)]
            res = [self.sb(es, "plres%d" % i, [128, D], F32) for i in range(T // 128)]
            gate_b = self.sb(es, "plgate", [128, D], F32)
            lng = self.sb(es, "pllng", [128, D], F32)
            lnb = self.sb(es, "pllnb", [128, D], F32)
            dg = self.sb(es, "pldg", [128, 128], F32)
            tmp = [self.sb(es, "pltmp%d" % i, [128, 512], F32) for i in range(2)]
            st = self.sb(es, "plst", [128, nD, 6], F32)
            mv = self.sb(es, "plmv", [128, 2], F32)
            rstd = self.sb(es, "plrstd", [128, 1], F32)
            nmr = self.sb(es, "plnmr", [128, 1], F32)
            self.bcast_load(lng, lng_in, D, "pllng", "lng")
            self.bcast_load(lnb, lnb_in, D, "pllnb", "lnb")
            for k in range(KC):
                sy.op("dve", lambda e: e.tensor_scalar_mul(dg[:], self.identf[:], self.modT[l][:, gate_col0 + k:gate_col0 + k + 1]),
                      reads=["identf"], writes=["dg"])
                sy.op("pe", lambda e: e.matmul(self.ps[7][:, 0:128], self.onesf[:], dg[:], start=True, stop=True),
                      reads=["dg", "onesf"], writes=["ps7"])
                sy.op("act", lambda e: e.activation(out=gate_b[:, k * 128:(k + 1) * 128], in_=self.ps[7][:, 0:128], func=AF.Copy),
                      reads=["ps7"], writes=["gate_b"])
            nw = 0
            nt = 0
            for tt in range(S // T):
                t0 = tt * T
                load_A(AT, t0, T, es)
                for sub in range(T // 128):
                    sy.dma("sp", res[sub][:], resid[t0 + sub * 128:t0 + (sub + 1) * 128, :], "plres%d" % sub,
                           writes=["res%d" % sub])
                for c0, n in _chunks(D, 512):
                    for si, (k0, kn) in enumerate(kslabs):
                        w = wb[nw % 2]
                        wk = "plw%d" % (nw % 2)
                        nw += 1
                        sy.dma("sp", w[:, 0:kn, 0:n],
                               w_d[k0 * 128:(k0 + kn) * 128, c0:c0 + n].rearrange("(k p) n -> p k n", p=128),
                               wk, writes=[wk])
                        for sub in range(T // 128):
                            bank = self.ps[sub]
                            for kk in range(kn):
                                k = k0 + kk
                                sy.op("pe", lambda e: e.matmul(bank[:, 0:n], AT[:, k, sub * 128:(sub + 1) * 128], w[:, kk, 0:n],
                                                               start=(k == 0), stop=(k == KA - 1)),
                                      reads=[wk, "plA"], writes=["ps%d" % sub],
                                      signal=(kk == kn - 1))
                    for sub in range(T // 128):
                        bank = self.ps[sub]
                        tb = tmp[nt % 2]
                        tk = "pltmp%d" % (nt % 2)
                        nt += 1
                        sy.op("dve", lambda e: e.tensor_tensor(tb[:, 0:n], bank[:, 0:n], gate_b[:, c0:c0 + n], ALU.mult),
                              reads=["ps%d" % sub, "gate_b"], writes=[tk])
                        sy.op("dve", lambda e: e.scalar_tensor_tensor(res[sub][:, c0:c0 + n], res[sub][:, c0:c0 + n],
                                                                      cfg.alpha, tb[:, 0:n], ALU.mult, ALU.add),
                              reads=[tk, "res%d" % sub], writes=["res%d" % sub])
                for sub in range(T // 128):
                    r = res[sub]
                    rk = "res%d" % sub
                    for i, (c0, n) in enumerate(_chunks(D, 512)):
                        sy.op("dve", lambda e: e.bn_stats(st[:, i, :], r[:, c0:c0 + n]), reads=[rk], writes=["plst"])
                    sy.op("dve", lambda e: e.bn_aggr(mv[:], st[:].rearrange("p a b -> p (a b)")), reads=["plst"], writes=["mv"])
                    sy.op("act", lambda e: e.activation(out=rstd[:], in_=mv[:, 1:2], func=AF.Sqrt, bias=self.epsl[:, 0:1], scale=1.0),
                          reads=["mv"], writes=["rstd"])
                    sy.op("dve", lambda e: e.reciprocal(rstd[:], rstd[:]), reads=["rstd"], writes=["rstd"])
                    sy.op("dve", lambda e: e.scalar_tensor_tensor(nmr[:], mv[:, 0:1], -1.0, rstd[:], ALU.mult, ALU.mult),
                          reads=["mv", "rstd"], writes=["nmr"])
                    sy.op("act", lambda e: e.activation(out=r[:], in_=r[:], func=AF.Identity, bias=nmr[:, 0:1], scale=rstd[:, 0:1]),
                          reads=[rk, "nmr", "rstd"], writes=[rk])
                    sy.op("dve", lambda e: e.tensor_tensor(r[:], r[:], lng[:], ALU.mult), reads=[rk, "lng"], writes=[rk])
                    sy.op("pool", lambda e: e.tensor_tensor(r[:], r[:], lnb[:], ALU.add), reads=[rk, "lnb"], writes=[rk])
                    sy.dma("pool", dst[t0 + sub * 128:t0 + (sub + 1) * 128, :], r[:], "plsto%d" % sub, reads=[rk])

    def phase_outproj(self, l, xin, xout):
        sy, cfg = self.sy, self.cfg
        KA = cfg.DMIX // 128
        T = min(256, cfg.S)
        holder = {}

        def load_A(AT, t0, T_, es):
            if "mst" not in holder:
                holder["mst"] = [self.sb(es, "mst%d" % i, [128, cfg.DMIX], BF16) for i in range(2)]
                holder["n"] = 0
            for sub in range(T_ // 128):
                m = holder["mst"][sub % 2]
                mk = "mst%d" % (sub % 2)
                sy.dma("sp", m[:], self.mix[t0 + sub * 128:t0 + (sub + 1) * 128, :], mk, writes=[mk])
                for g0, gn in _chunks(KA, 8):
                    tb = 4 + holder["n"] % 2
                    holder["n"] += 1
                    pbb = self.psb(tb)
                    for kk in range(gn):
                        k = g0 + kk
                        sy.op("pe", lambda e: e.transpose(out=pbb[:, kk * 128:(kk + 1) * 128], in_=m[:, k * 128:(k + 1) * 128],
                                                          identity=self.identb[:]),
                              reads=[mk], writes=["ps%d" % tb], signal=(kk == gn - 1))
                    for kk in range(gn):
                        k = g0 + kk
                        eng = "act" if kk % 2 == 0 else "dve"
                        if eng == "act":
                            sy.op("act", lambda e: e.activation(out=AT[:, k, sub * 128:(sub + 1) * 128],
                                                                in_=pbb[:, kk * 128:(kk + 1) * 128], func=AF.Copy),
                                  reads=["ps%d" % tb], writes=["plA"])
                        else:
                            sy.op("dve", lambda e: e.tensor_copy(AT[:, k, sub * 128:(sub + 1) * 128], pbb[:, kk * 128:(kk + 1) * 128]),
                                  reads=["ps%d" % tb], writes=["plA"])

        self.proj_ln(l, KA, load_A, self.w_out[l], xin, 2 * cfg.KC, self.i_ln1g[l], self.i_ln1b[l], self.x1, T)

    def phase_ffn1(self, l, xin, xout):
        sy, cfg = self.sy, self.cfg
        KC, D, S, DFF = cfg.KC, cfg.D, cfg.S, cfg.DFF
        T = min(512, S)
        NFC = DFF // 128
        with ExitStack() as es:
            hT = self.sb(es, "h2T", [128, KC, T], BF16)
            wg = [self.sb(es, "wg%d" % i, [128, KC, 256], BF16) for i in range(2)]
            wu = [self.sb(es, "wu%d" % i, [128, KC, 256], BF16) for i in range(2)]
            xst = [self.sb(es, "xst%d" % i, [128, D], F32) for i in range(2)]
            gpad = [self.sb(es, "gpad%d" % i, [128, 2 + 512], F32) for i in range(2)]
            cvt = [self.sb(es, "cvt%d" % i, [128, 512], F32) for i in range(2)]
            sgt = [self.sb(es, "sgt%d" % i, [128, 512], F32) for i in range(2)]
            obf = [self.sb(es, "obf%d" % i, [128, 512], BF16) for i in range(3)]
            halo = self.sb(es, "fhalo", [128, NFC, 2], F32)
            cw = self.sb(es, "fcw", [128, 3 * NFC], F32)
            cb = self.sb(es, "fcb", [128, NFC], F32)
            self.load_cols(es, cw, "cw", self.i_fcw[l], 3 * NFC, "fcw")
            self.load_cols(es, cb, "cb", self.i_fcb[l], NFC, "fcb")
            sy.op("pool", lambda e: e.memset(halo[:], 0.0), writes=["halo%d" % i for i in range(NFC)])
            scale_ap = self.sc1p[l][:, KC:2 * KC]
            shift_ap = self.modT[l][:, 3 * KC:4 * KC]
            blocks = _chunks(DFF, 256)
            nev = 0
            nst = 0
            for tt in range(S // T):
                t0 = tt * T
                self.build_hT(self.x1, t0, T, hT, xst, scale_ap, shift_ap, "hT")
                for bi, (c0, bn) in enumerate(blocks):
                    gi = tt * len(blocks) + bi
                    wgb, wub = wg[gi % 2], wu[gi % 2]
                    wgk, wuk = "wg%d" % (gi % 2), "wu%d" % (gi % 2)
                    sy.dma("sp", wgb[:, :, 0:bn], self.w_g[l, :, c0:c0 + bn].rearrange("(k p) n -> p k n", p=128), wgk, writes=[wgk])
                    sy.dma("sp", wub[:, :, 0:bn], self.w_u[l, :, c0:c0 + bn].rearrange("(k p) n -> p k n", p=128), wuk, writes=[wuk])
                    for oc in range(bn // 128):
                        fc = c0 // 128 + oc
                        pg = 2 + (nev % 3)
                        pu = 5 + (nev % 3)
                        nev += 1
                        bg, bu = self.ps[pg], self.ps[pu]
                        for k in range(KC):
                            sy.op("pe", lambda e: e.matmul(bg[:, 0:T], wgb[:, k, oc * 128:(oc + 1) * 128], hT[:, k, :],
                                                           start=(k == 0), stop=(k == KC - 1)),
                                  reads=[wgk, "hT"], writes=["ps%d" % pg], signal=(k == KC - 1))
                        for k in range(KC):
                            sy.op("pe", lambda e: e.matmul(bu[:, 0:T], wub[:, k, oc * 128:(oc + 1) * 128], hT[:, k, :],
                                                           start=(k == 0), stop=(k == KC - 1)),
                                  reads=[wuk, "hT"], writes=["ps%d" % pu], signal=(k == KC - 1))
                        gp = gpad[fc % 2]
                        gpk = "gpad%d" % (fc % 2)
                        cv = cvt[fc % 2]
                        cvk = "cvt%d" % (fc % 2)
                        sg = sgt[fc % 2]
                        sgk = "sgt%d" % (fc % 2)
                        sy.op("act", lambda e: e.activation(out=gp[:, 2:2 + T], in_=bg[:, 0:T], func=AF.Copy),
                              reads=["ps%d" % pg], writes=[gpk + "m"])
                        sy.op("pool", lambda e: e.tensor_copy(gp[:, 0:2], halo[:, fc, :]),
                              reads=["halo%d" % fc], writes=[gpk + "h"])
                        sy.op("dve", lambda e: e.tensor_scalar(cv[:, 0:T], gp[:, 2:2 + T], cw[:, 2 * NFC + fc:2 * NFC + fc + 1],
                                                               cb[:, fc:fc + 1], ALU.mult, ALU.add),
                              reads=[gpk + "m", gpk + "h", "cw", "cb"], writes=[cvk])
                        for j in range(2):
                            sy.op("dve", lambda e: e.scalar_tensor_tensor(cv[:, 0:T], gp[:, j:j + T],
                                                                          cw[:, j * NFC + fc:j * NFC + fc + 1], cv[:, 0:T],
                                                                          ALU.mult, ALU.add),
                                  reads=[gpk + "m", gpk + "h", cvk], writes=[cvk])
                        sy.op("pool", lambda e: e.tensor_copy(halo[:, fc, :], gp[:, T:T + 2]),
                              reads=[gpk + "m"], writes=["halo%d" % fc])
                        sy.op("act", lambda e: e.activation(out=sg[:, 0:T], in_=cv[:, 0:T], func=AF.Silu),
                              reads=[cvk], writes=[sgk])
                        si = nst % 3
                        nst += 1
                        sy.op("dve", lambda e: e.tensor_tensor(obf[si][:, 0:T], sg[:, 0:T], bu[:, 0:T], ALU.mult),
                              reads=[sgk, "ps%d" % pu], writes=["obf%d" % si])
                        sy.dma("pool", self.actT[fc * 128:(fc + 1) * 128, t0:t0 + T], obf[si][:, 0:T], "obf%d" % si,
                               reads=["obf%d" % si])

    def phase_ffn2(self, l, xin, xout):
        sy, cfg = self.sy, self.cfg
        KA = cfg.DFF // 128
        T = min(256, cfg.S)

        def load_A(AT, t0, T_, es):
            for k0, kn in _chunks(KA, 32):
                sy.dma("sp", AT[:, k0:k0 + kn, :],
                       self.actT[k0 * 128:(k0 + kn) * 128, t0:t0 + T_].rearrange("(k p) t -> p k t", p=128),
                       "plA", writes=["plA"])

        self.proj_ln(l, KA, load_A, self.w_d[l], self.x1, 5 * cfg.KC, self.i_ln2g[l], self.i_ln2b[l], xout, T)


def make_in_map(cfg, inputs, b):
    f = lambda a: np.ascontiguousarray(a, dtype=np.float32)
    L = cfg.L
    return {
        "x": f(inputs["x"][b]),
        "c": f(inputs["c"][b]).reshape(cfg.KC, 128),
        "w_mod": f(inputs["w_mod"]),
        "b_mod": f(inputs["b_mod"]).reshape(L, 6 * cfg.KC, 128),
        "w_in": f(inputs["w_in"]),
        "diff_lambda": f(inputs["diff_lambda"]).reshape(L, 1, 512),
        "diff_norm_w": f(inputs["diff_norm_w"]).reshape(L, 1, 256),
        "ssd_conv_w": f(inputs["ssd_conv_w"]).reshape(L, 4 * cfg.XBC // 128, 128),
        "ssd_conv_b": f(inputs["ssd_conv_b"]).reshape(L, cfg.XBC // 128, 128),
        "ssd_dt_bias": f(inputs["ssd_dt_bias"]).reshape(L, 1, cfg.NHS),
        "ssd_a_log": f(inputs["ssd_a_log"]).reshape(L, 1, cfg.NHS),
        "ssd_d": f(inputs["ssd_d"]).reshape(L, 1, cfg.NHS),
        "ssd_norm_w": f(inputs["ssd_norm_w"]).reshape(L, 1, cfg.SW),
        "w_out": f(inputs["w_out"]),
        "ln1_g": f(inputs["ln1_g"]).reshape(L, 1, cfg.D),
        "ln1_b": f(inputs["ln1_b"]).reshape(L, 1, cfg.D),
        "w_gate": f(inputs["w_gate"]),
        "w_up": f(inputs["w_up"]),
        "ffn_conv_w": f(inputs["ffn_conv_w"]).reshape(L, 3 * cfg.DFF // 128, 128),
        "ffn_conv_b": f(inputs["ffn_conv_b"]).reshape(L, cfg.DFF // 128, 128),
        "w_down": f(inputs["w_down"]),
        "ln2_g": f(inputs["ln2_g"]).reshape(L, 1, cfg.D),
        "ln2_b": f(inputs["ln2_b"]).reshape(L, 1, cfg.D),
    }


def kernel(**inputs):
    cfg = Cfg()
    B = inputs["x"].shape[0]
    prog = Prog(cfg)
    nc = prog.build()
    in_maps = [make_in_map(cfg, inputs, b) for b in range(B)]
    res = run_bass_kernel_spmd(nc, in_maps, core_ids=list(range(B)))
    return np.stack([np.asarray(r["out"], dtype=np.float32) for r in res.results], axis=0)
```

```python
import math
import os
from contextlib import ExitStack

import numpy as np
import concourse.bass as bass
import concourse.mybir as mybir
from concourse.bass_utils import run_bass_kernel_spmd

F32 = mybir.dt.float32
BF16 = mybir.dt.bfloat16
AF = mybir.ActivationFunctionType
ALU = mybir.AluOpType
AX = mybir.AxisListType

LN_EPS = 1e-5
RMS_EPS = 1e-5


class Cfg:
    def __init__(self, D=4096, S=4096, NH=8, NG=8, DFF=11008, L=2):
        self.D, self.S, self.NH, self.NG, self.DFF, self.L = D, S, NH, NG, DFF, L
        self.AW = NH * 256
        self.SW = NG * 256
        self.NHS = NG * 4
        self.DMIX = self.AW + self.SW
        self.XBC = self.SW + 2 * NG * 128
        self.NIN = 3 * self.AW + self.SW + self.XBC + self.NHS
        self.KC = D // 128
        self.alpha = (2.0 * L) ** 0.25


class Sy:
    def __init__(self, nc, es):
        self.nc = nc
        self.es = es
        self.engs = {"pe": nc.tensor, "dve": nc.vector, "act": nc.scalar,
                     "pool": nc.gpsimd, "sp": nc.sync}
        self.sem = {}
        self.cnt = {}
        for e in ("pe", "dve", "act", "pool"):
            self.sem[e] = es.enter_context(nc.semaphore("s_" + e))
            self.cnt[e] = 0
        self.waited = {}
        self.st = {}
        self.pend_r = []
        self.pend_w = []
        self.nwait = 0

    def _lane(self, name):
        if name not in self.sem:
            self.sem[name] = self.es.enter_context(self.nc.semaphore("l_" + name))
            self.cnt[name] = 0
        return name

    def _wait(self, e, tok):
        sk, val = tok
        if e == "pe" and sk == "pe":
            return
        k = (e, sk)
        if self.waited.get(k, 0) >= val:
            return
        self.waited[k] = val
        self.engs[e].wait_ge(self.sem[sk], val)
        self.nwait += 1

    def _deps(self, e, reads, writes, own=None):
        for k in reads:
            s = self.st.get(k)
            if s and s[0] and s[0][0] != own:
                self._wait(e, s[0])
        for k in writes:
            s = self.st.get(k)
            if s:
                if s[0] and s[0][0] != own:
                    self._wait(e, s[0])
                for t in s[1]:
                    if t[0] != own:
                        self._wait(e, t)

    def _commit(self, tok, reads, writes):
        for k in reads:
            s = self.st.setdefault(k, [None, []])
            s[1] = [t for t in s[1] if t[0] != tok[0]] + [tok]
        for k in writes:
            self.st[k] = [tok, []]

    def op(self, e, fn, reads=(), writes=(), signal=True):
        self._deps(e, reads, writes)
        ins = fn(self.engs[e])
        if e == "pe" and not signal:
            self.pend_r += list(reads)
            self.pend_w += list(writes)
            return ins
        self.cnt[e] += 1
        ins.then_inc(self.sem[e], 1)
        tok = (e, self.cnt[e])
        if e == "pe":
            reads = list(reads) + self.pend_r
            writes = list(writes) + self.pend_w
            self.pend_r, self.pend_w = [], []
        self._commit(tok, reads, writes)
        return ins

    def dma(self, q, out, in_, lane, reads=(), writes=()):
        self._lane(lane)
        self._deps(q, reads, writes, own=lane)
        ins = self.engs[q].dma_start(out=out, in_=in_)
        self.cnt[lane] += 16
        ins.then_inc(self.sem[lane], 16)
        self._commit((lane, self.cnt[lane]), reads, writes)
        return ins

    def barrier(self):
        for e in ("pe", "dve", "act", "pool", "sp"):
            for sk, c in self.cnt.items():
                if c > 0 and sk != e:
                    self._wait(e, (sk, c))
        self.st = {}


def _chunks(n, c):
    return [(i, min(c, n - i)) for i in range(0, n, c)]


class Prog:
    def __init__(self, cfg, debug=False, stop_after=None):
        self.cfg = cfg
        self.debug = debug
        self.stop_after = stop_after
        self.nc = bass.Bass("TRN2", target_bir_lowering=False)
        self.dbg_outs = []

    def din(self, name, shape, dt=F32):
        return self.nc.dram_tensor(name, list(shape), dt, kind="ExternalInput").ap()

    def dscr(self, name, shape, dt):
        kind = "ExternalOutput" if self.debug else "Internal"
        if self.debug:
            self.dbg_outs.append(name)
        return self.nc.dram_tensor(name, list(shape), dt, kind=kind).ap()

    def sb(self, es, name, shape, dt):
        self._nsb = getattr(self, "_nsb", 0) + 1
        return es.enter_context(self.nc.sbuf_tensor("%s_%d" % (name, self._nsb), list(shape), dt))

    def build(self):
        cfg, nc = self.cfg, self.nc
        D, S, L = cfg.D, cfg.S, cfg.L
        self.i_x = self.din("x", [S, D])
        self.i_c = self.din("c", [cfg.KC, 128])
        self.i_wmod = self.din("w_mod", [L, D, 6 * D])
        self.i_bmod = self.din("b_mod", [L, 6 * cfg.KC, 128])
        self.i_win = self.din("w_in", [L, D, cfg.NIN])
        self.i_lam = self.din("diff_lambda", [L, 1, 512])
        self.i_dnw = self.din("diff_norm_w", [L, 1, 256])
        self.i_scw = self.din("ssd_conv_w", [L, 4 * cfg.XBC // 128, 128])
        self.i_scb = self.din("ssd_conv_b", [L, cfg.XBC // 128, 128])
        self.i_dtb = self.din("ssd_dt_bias", [L, 1, cfg.NHS])
        self.i_alog = self.din("ssd_a_log", [L, 1, cfg.NHS])
        self.i_sd = self.din("ssd_d", [L, 1, cfg.NHS])
        self.i_snw = self.din("ssd_norm_w", [L, 1, cfg.SW])
        self.i_wout = self.din("w_out", [L, cfg.DMIX, D])
        self.i_ln1g = self.din("ln1_g", [L, 1, D])
        self.i_ln1b = self.din("ln1_b", [L, 1, D])
        self.i_wg = self.din("w_gate", [L, D, cfg.DFF])
        self.i_wu = self.din("w_up", [L, D, cfg.DFF])
        self.i_fcw = self.din("ffn_conv_w", [L, 3 * cfg.DFF // 128, 128])
        self.i_fcb = self.din("ffn_conv_b", [L, cfg.DFF // 128, 128])
        self.i_wd = self.din("w_down", [L, cfg.DFF, D])
        self.i_ln2g = self.din("ln2_g", [L, 1, D])
        self.i_ln2b = self.din("ln2_b", [L, 1, D])
        self.o_out = nc.dram_tensor("out", [S, D], F32, kind="ExternalOutput").ap()

        self.w_in = self.dscr("wb_in", [L, D, cfg.NIN], BF16)
        self.w_out = self.dscr("wb_out", [L, cfg.DMIX, D], BF16)
        self.w_g = self.dscr("wb_g", [L, D, cfg.DFF], BF16)
        self.w_u = self.dscr("wb_u", [L, D, cfg.DFF], BF16)
        self.w_d = self.dscr("wb_d", [L, cfg.DFF, D], BF16)
        self.qT = self.dscr("s_qT", [cfg.AW, S], BF16)
        self.kT = self.dscr("s_kT", [cfg.AW, S], BF16)
        self.v = self.dscr("s_v", [S, cfg.AW], BF16)
        self.sz = self.dscr("s_sz", [S, cfg.SW], F32)
        self.xsT = self.dscr("s_xsT", [cfg.SW, S], F32)
        self.BT = self.dscr("s_BT", [cfg.NG * 128, S], BF16)
        self.CT = self.dscr("s_CT", [cfg.NG * 128, S], BF16)
        self.dt = self.dscr("s_dt", [S, cfg.NHS], F32)
        self.mix = self.dscr("s_mix", [S, cfg.DMIX], BF16)
        self.x1 = self.dscr("s_x1", [S, D], F32)
        self.actT = self.dscr("s_actT", [cfg.DFF, S], BF16)
        self.xl = self.dscr("s_xl", [S, D], F32)

        with ExitStack() as es:
            self.sy = Sy(nc, es)
            self.ps = [es.enter_context(nc.psum_tensor("ps%d" % i, [128, 512], F32))
                       for i in range(8)]
            self.consts(es)
            self.cast_weights()
            for l in range(L):
                self.phase_mod(l)
            self.sy.barrier()
            for l in range(L):
                xin = self.i_x if l == 0 else self.xl
                xout = self.o_out if l == L - 1 else self.xl
                for pi, ph in enumerate((self.phase_inproj, self.phase_attn, self.phase_ssd,
                                         self.phase_outproj, self.phase_ffn1, self.phase_ffn2)):
                    if self.stop_after is not None and (l, pi) > self.stop_after:
                        continue
                    ph(l, xin, xout)
                    self.sy.barrier()
            self.sy.barrier()
        return nc

    def consts(self, es):
        sy, nc, cfg = self.sy, self.nc, self.cfg
        self.identf = self.sb(es, "identf", [128, 128], F32)
        self.identb = self.sb(es, "identb", [128, 128], BF16)
        self.onesf = self.sb(es, "onesf", [128, 128], F32)
        self.tri = self.sb(es, "tri", [128, 128], F32)
        self.ltri = self.sb(es, "ltri", [128, 128], F32)
        self.modT = [self.sb(es, "modT%d" % l, [128, 6 * cfg.KC], F32) for l in range(cfg.L)]
        self.sc1p = [self.sb(es, "sc1p%d" % l, [128, 2 * cfg.KC], F32) for l in range(cfg.L)]
        sy.op("pool", lambda e: e.memset(self.onesf[:], 1.0), writes=["onesf"])
        sy.op("pool", lambda e: e.affine_select(
            out=self.identf[:], in_=self.onesf[:], pattern=[[-1, 128]],
            compare_op=ALU.is_equal, fill=0.0, base=0, channel_multiplier=1),
            reads=["onesf"], writes=["identf"])
        sy.op("pool", lambda e: e.affine_select(
            out=self.tri[:], in_=self.onesf[:], pattern=[[1, 128]],
            compare_op=ALU.is_ge, fill=0.0, base=0, channel_multiplier=-1),
            reads=["onesf"], writes=["tri"])
        sy.op("pool", lambda e: e.affine_select(
            out=self.ltri[:], in_=self.onesf[:], pattern=[[-1, 128]],
            compare_op=ALU.is_gt, fill=0.0, base=0, channel_multiplier=1),
            reads=["onesf"], writes=["ltri"])
        sy.op("dve", lambda e: e.tensor_copy(self.identb[:], self.identf[:]),
              reads=["identf"], writes=["identb"])
        self.epsr = self.sb(es, "epsr", [128, 1], F32)
        self.epsl = self.sb(es, "epsl", [128, 1], F32)
        self.zeroc = self.sb(es, "zeroc", [128, 1], F32)
        sy.op("pool", lambda e: e.memset(self.zeroc[:], 0.0), writes=["zeroc"])
        sy.op("pool", lambda e: e.memset(self.epsr[:], RMS_EPS), writes=["epsr"])
        sy.op("pool", lambda e: e.memset(self.epsl[:], LN_EPS), writes=["epsl"])

    def cast_weights(self):
        sy, cfg = self.sy, self.cfg
        n = 0
        for l in range(cfg.L):
            for src, dst, rows in ((self.i_win, self.w_in, cfg.D), (self.i_wout, self.w_out, cfg.DMIX),
                                   (self.i_wg, self.w_g, cfg.D), (self.i_wu, self.w_u, cfg.D),
                                   (self.i_wd, self.w_d, cfg.DFF)):
                for r0, nr in _chunks(rows, 1024):
                    sy.dma("pool", dst[l, r0:r0 + nr, :], src[l, r0:r0 + nr, :], "wcast%d" % (n % 4))
                    n += 1

    def load_cols(self, es, dst, dst_key, src_rows, n, tag):
        sy = self.sy
        tmp = self.sb(es, "lc_" + tag, [128, 128], F32)
        for j0, nj in _chunks(n, 128):
            sy.dma("sp", tmp[0:nj, :], src_rows[j0:j0 + nj, :], "lc", writes=["lc_tmp"])
            bank = self.ps[7]
            sy.op("pe", lambda e: e.transpose(out=bank[:, 0:nj], in_=tmp[0:nj, :], identity=self.identf[0:nj, 0:nj]),
                  reads=["lc_tmp", "identf"], writes=["ps7"])
            sy.op("dve", lambda e: e.tensor_copy(dst[:, j0:j0 + nj], bank[:, 0:nj]),
                  reads=["ps7"], writes=[dst_key])

    def phase_mod(self, l):
        sy, cfg = self.sy, self.cfg
        KC, D = cfg.KC, cfg.D
        NJ = 6 * KC
        with ExitStack() as es:
            ca = self.sb(es, "ca", [128, KC], F32)
            cas = self.sb(es, "cas", [128, KC], F32)
            bm = self.sb(es, "bm", [128, NJ], F32)
            wsl = [self.sb(es, "wms%d" % i, [128, KC, 256], F32) for i in range(2)]
            self.load_cols(es, ca, "ca", self.i_c, KC, "c")
            self.load_cols(es, bm, "bm", self.i_bmod[l], NJ, "b")
            sy.op("act", lambda e: e.activation(out=cas[:], in_=ca[:], func=AF.Silu),
                  reads=["ca"], writes=["cas"])
            psm = self.ps[6]
            blocks = _chunks(6 * D, 256)
            for bi, (c0, n) in enumerate(blocks):
                w = wsl[bi % 2]
                sy.dma("sp", w[:, :, 0:n],
                       self.i_wmod[l, :, c0:c0 + n].rearrange("(k p) n -> p k n", p=128),
                       "wms%d" % (bi % 2), writes=["wms%d" % (bi % 2)])
                for jj in range(n // 128):
                    j = c0 // 128 + jj
                    for k in range(KC):
                        sy.op("pe", lambda e: e.matmul(psm[:, j:j + 1], w[:, k, jj * 128:(jj + 1) * 128],
                                                       cas[:, k:k + 1], start=(k == 0), stop=(k == KC - 1)),
                              reads=["wms%d" % (bi % 2), "cas"], writes=["ps6"],
                              signal=(k == KC - 1))
            sy.op("dve", lambda e: e.tensor_tensor(self.modT[l][:], psm[:, 0:NJ], bm[:], ALU.add),
                  reads=["ps6", "bm"], writes=["modT%d" % l])
            sy.op("dve", lambda e: e.tensor_scalar_add(self.sc1p[l][:, 0:KC], self.modT[l][:, KC:2 * KC], 1.0),
                  reads=["modT%d" % l], writes=["sc1pa%d" % l])
            sy.op("dve", lambda e: e.tensor_scalar_add(self.sc1p[l][:, KC:2 * KC], self.modT[l][:, 4 * KC:5 * KC], 1.0),
                  reads=["modT%d" % l], writes=["sc1pb%d" % l])
            sy.barrier()

    def build_hT(self, xin, t0, T, hT, xst, scale_ap, shift_ap, tag):
        sy, cfg = self.sy, self.cfg
        KC = cfg.KC
        G = min(4, KC)
        nb = 0
        for sub in range(T // 128):
            xs = xst[sub % 2]
            xk = "xst%d" % (sub % 2)
            sy.dma("sp", xs[:], xin[t0 + sub * 128:t0 + (sub + 1) * 128, :], xk, writes=[xk])
            for kg in range(KC // G):
                bi = nb % 2
                nb += 1
                bank = self.ps[bi]
                for kk in range(G):
                    k = kg * G + kk
                    sy.op("pe", lambda e: e.transpose(out=bank[:, kk * 128:(kk + 1) * 128],
                                                      in_=xs[:, k * 128:(k + 1) * 128], identity=self.identf[:]),
                          reads=[xk], writes=["ps%d" % bi], signal=(kk == G - 1))
                for kk in range(G):
                    k = kg * G + kk
                    sy.op("act", lambda e: e.activation(out=hT[:, k, sub * 128:(sub + 1) * 128],
                                                        in_=bank[:, kk * 128:(kk + 1) * 128], func=AF.Identity,
                                                        bias=shift_ap[:, k:k + 1], scale=scale_ap[:, k:k + 1]),
                          reads=["ps%d" % bi], writes=[tag])

    def phase_inproj(self, l, xin, xout):
        sy, cfg = self.sy, self.cfg
        KC, D, S = cfg.KC, cfg.D, cfg.S
        T = min(512, S)
        NXC = cfg.XBC // 128
        AW, SW = cfg.AW, cfg.SW
        segs = [("q", 0, AW, "a"), ("k", AW, AW, "a"), ("v", 2 * AW, AW, "b"),
                ("z", 3 * AW, SW, "b"), ("xbc", 3 * AW + SW, cfg.XBC, "a"),
                ("dt", 3 * AW + SW + cfg.XBC, cfg.NHS, "b")]
        blocks = []
        for name, c0, n, form in segs:
            for b0, bn in _chunks(n, 512):
                blocks.append((name, c0, b0, bn, form))
        with ExitStack() as es:
            hT = self.sb(es, "hT", [128, KC, T], BF16)
            wb = [self.sb(es, "wb%d" % i, [128, KC, 512], BF16) for i in range(2)]
            xst = [self.sb(es, "xst%d" % i, [128, D], F32) for i in range(2)]
            ost = [self.sb(es, "ost%d" % i, [128, 512], F32) for i in range(3)]
            obf = [self.sb(es, "obf%d" % i, [128, 512], BF16) for i in range(3)]
            xpad = [self.sb(es, "xpad%d" % i, [128, 3 + 512], F32) for i in range(2)]
            cvt = [self.sb(es, "cvt%d" % i, [128, 512], F32) for i in range(2)]
            halo = self.sb(es, "halo", [128, NXC, 3], F32)
            cw = self.sb(es, "cw", [128, 4 * NXC], F32)
            cb = self.sb(es, "cb", [128, NXC], F32)
            dtb = self.sb(es, "dtb", [128, cfg.NHS], F32)
            sp1 = self.sb(es, "sp1", [128, cfg.NHS], F32)
            sp2 = self.sb(es, "sp2", [128, cfg.NHS], F32)
            sp3 = self.sb(es, "sp3", [128, cfg.NHS], F32)
            self.load_cols(es, cw, "cw", self.i_scw[l], 4 * NXC, "cw")
            self.load_cols(es, cb, "cb", self.i_scb[l], NXC, "cb")
            sy.dma("sp", dtb[:], self.i_dtb[l].to_broadcast([128, cfg.NHS]), "dtb", writes=["dtb"])
            sy.op("pool", lambda e: e.memset(halo[:], 0.0), writes=["halo%d" % i for i in range(NXC)])
            scale_ap = self.sc1p[l][:, 0:KC]
            shift_ap = self.modT[l][:, 0:KC]
            nev = 0
            nst = 0
            for tt in range(S // T):
                t0 = tt * T
                self.build_hT(xin, t0, T, hT, xst, scale_ap, shift_ap, "hT")
                for bi, (name, c0, b0, bn, form) in enumerate(blocks):
                    gi = tt * len(blocks) + bi
                    w = wb[gi % 2]
                    wk = "wb%d" % (gi % 2)
                    sy.dma("sp", w[:, :, 0:bn],
                           self.w_in[l, :, c0 + b0:c0 + b0 + bn].rearrange("(k p) n -> p k n", p=128),
                           wk, writes=[wk])
                    if form == "a":
                        for oc in range(bn // 128):
                            pb = 2 + nev % 4
                            nev += 1
                            bank = self.ps[pb]
                            pk = "ps%d" % pb
                            for k in range(KC):
                                sy.op("pe", lambda e: e.matmul(bank[:, 0:T], w[:, k, oc * 128:(oc + 1) * 128],
                                                               hT[:, k, :], start=(k == 0), stop=(k == KC - 1)),
                                      reads=[wk, "hT"], writes=[pk], signal=(k == KC - 1))
                            fc = (b0 + oc * 128) // 128
                            si = nst % 3
                            nst += 1
                            if name == "q":
                                sy.op("act", lambda e: e.activation(out=obf[si][:, 0:T], in_=bank[:, 0:T],
                                                                    func=AF.Copy, scale=1.0 / math.sqrt(128.0)),
                                      reads=[pk], writes=["obf%d" % si])
                                sy.dma("pool", self.qT[fc * 128:(fc + 1) * 128, t0:t0 + T], obf[si][:, 0:T],
                                       "obf%d" % si, reads=["obf%d" % si])
                            elif name == "k":
                                sy.op("dve", lambda e: e.tensor_copy(obf[si][:, 0:T], bank[:, 0:T]),
                                      reads=[pk], writes=["obf%d" % si])
                                sy.dma("pool", self.kT[fc * 128:(fc + 1) * 128, t0:t0 + T], obf[si][:, 0:T],
                                       "obf%d" % si, reads=["obf%d" % si])
                            else:
                                xp = xpad[fc % 2]
                                xpk = "xpad%d" % (fc % 2)
                                cv = cvt[fc % 2]
                                cvk = "cvt%d" % (fc % 2)
                                sy.op("act", lambda e: e.activation(out=xp[:, 3:3 + T], in_=bank[:, 0:T], func=AF.Copy),
                                      reads=[pk], writes=[xpk + "m"])
                                sy.op("pool", lambda e: e.tensor_copy(xp[:, 0:3], halo[:, fc, :]),
                                      reads=["halo%d" % fc], writes=[xpk + "h"])
                                sy.op("dve", lambda e: e.tensor_scalar(cv[:, 0:T], xp[:, 3:3 + T],
                                                                       cw[:, 3 * NXC + fc:3 * NXC + fc + 1],
                                                                       cb[:, fc:fc + 1], ALU.mult, ALU.add),
                                      reads=[xpk + "m", xpk + "h", "cw", "cb"], writes=[cvk])
                                for j in range(3):
                                    sy.op("dve", lambda e: e.scalar_tensor_tensor(
                                        cv[:, 0:T], xp[:, j:j + T], cw[:, j * NXC + fc:j * NXC + fc + 1],
                                        cv[:, 0:T], ALU.mult, ALU.add),
                                        reads=[xpk + "m", xpk + "h", cvk], writes=[cvk])
                                sy.op("pool", lambda e: e.tensor_copy(halo[:, fc, :], xp[:, T:T + 3]),
                                      reads=[xpk + "m"], writes=["halo%d" % fc])
                                if fc < SW // 128:
                                    sy.op("act", lambda e: e.activation(out=ost[si][:, 0:T], in_=cv[:, 0:T], func=AF.Silu),
                                          reads=[cvk], writes=["ost%d" % si])
                                    sy.dma("pool", self.xsT[fc * 128:(fc + 1) * 128, t0:t0 + T], ost[si][:, 0:T],
                                           "ost%d" % si, reads=["ost%d" % si])
                                else:
                                    sy.op("act", lambda e: e.activation(out=obf[si][:, 0:T], in_=cv[:, 0:T], func=AF.Silu),
                                          reads=[cvk], writes=["obf%d" % si])
                                    g = fc - SW // 128
                                    dst = self.BT if g < cfg.NG else self.CT
                                    g = g % cfg.NG
                                    sy.dma("pool", dst[g * 128:(g + 1) * 128, t0:t0 + T], obf[si][:, 0:T],
                                           "obf%d" % si, reads=["obf%d" % si])
                    else:
                        for sub in range(T // 128):
                            pb = 2 + nev % 4
                            nev += 1
                            bank = self.ps[pb]
                            pk = "ps%d" % pb
                            for k in range(KC):
                                sy.op("pe", lambda e: e.matmul(bank[:, 0:bn], hT[:, k, sub * 128:(sub + 1) * 128],
                                                               w[:, k, 0:bn], start=(k == 0), stop=(k == KC - 1)),
                                      reads=[wk, "hT"], writes=[pk], signal=(k == KC - 1))
                            r0 = t0 + sub * 128
                            si = nst % 3
                            nst += 1
                            if name == "v":
                                sy.op("dve", lambda e: e.tensor_copy(obf[si][:, 0:bn], bank[:, 0:bn]),
                                      reads=[pk], writes=["obf%d" % si])
                                sy.dma("pool", self.v[r0:r0 + 128, b0:b0 + bn], obf[si][:, 0:bn],
                                       "obf%d" % si, reads=["obf%d" % si])
                            elif name == "z":
                                sy.op("act", lambda e: e.activation(out=ost[si][:, 0:bn], in_=bank[:, 0:bn], func=AF.Silu),
                                      reads=[pk], writes=["ost%d" % si])
                                sy.dma("pool", self.sz[r0:r0 + 128, b0:b0 + bn], ost[si][:, 0:bn],
                                       "ost%d" % si, reads=["ost%d" % si])
                            else:
                                n = bn
                                sy.op("dve", lambda e: e.tensor_tensor(sp1[:, 0:n], bank[:, 0:n], dtb[:, 0:n], ALU.add),
                                      reads=[pk, "dtb"], writes=["sp1"])
                                sy.op("dve", lambda e: e.scalar_tensor_tensor(sp2[:, 0:n], sp1[:, 0:n], -1.0, sp1[:, 0:n], ALU.mult, ALU.max),
                                      reads=["sp1"], writes=["sp2"])
                                sy.op("act", lambda e: e.activation(out=sp3[:, 0:n], in_=sp2[:, 0:n], func=AF.Exp, scale=-1.0),
                                      reads=["sp2"], writes=["sp3"])
                                sy.op("act", lambda e: e.activation(out=sp2[:, 0:n], in_=sp3[:, 0:n], func=AF.Ln, bias=1.0),
                                      reads=["sp3"], writes=["sp2"])
                                sy.op("dve", lambda e: e.scalar_tensor_tensor(ost[si][:, 0:n], sp1[:, 0:n], 0.0, sp2[:, 0:n],
                                                                              ALU.max, ALU.add),
                                      reads=["sp1", "sp2"], writes=["ost%d" % si])
                                sy.dma("pool", self.dt[r0:r0 + 128, :], ost[si][:, 0:n],
                                       "ost%d" % si, reads=["ost%d" % si])

    def bcast_load(self, dst, src_row, n, lane, key):
        self.sy.dma("sp", dst[:, 0:n], src_row.to_broadcast([128, n]), lane, writes=[key])

    def psb(self, i):
        return self.ps[i][:].bitcast(BF16)

    def phase_attn(self, l, xin, xout):
        sy, cfg = self.sy, self.cfg
        S, NH = cfg.S, cfg.NH
        NB = S // 128
        NSTR = 2
        lam_init = 0.8 - 0.6 * math.exp(-0.3 * l)
        with ExitStack() as es:
            Dm = self.sb(es, "Dm", [128, S], F32)
            M2 = self.sb(es, "M2", [128, 128], F32)
            lamb = self.sb(es, "lamb", [128, 512], F32)
            lt = self.sb(es, "lt", [128, 256], F32)
            lsm = self.sb(es, "lsm", [128, 2], F32)
            lex = self.sb(es, "lex", [128, 2], F32)
            nlam = self.sb(es, "nlam", [128, 1], F32)
            nw = self.sb(es, "nw", [128, 256], F32)
            nws = self.sb(es, "nws", [128, 256], F32)
            kTs = [self.sb(es, "kTs%d" % j, [128, S], BF16) for j in range(2)]
            qTs = [self.sb(es, "qTs%d" % j, [128, S], BF16) for j in range(2)]
            V = self.sb(es, "V", [128, NB, 256], BF16)
            ob = [self.sb(es, "ob%d" % i, [128, 256], BF16) for i in range(2)]
            st = []
            for t in range(NSTR):
                d = {}
                d["sc"] = [self.sb(es, "sc%d" % j, [128, S], F32) for j in range(2)]
                d["Ab"] = self.sb(es, "Ab", [128, S], BF16)
                d["AT"] = self.sb(es, "AT", [128, NB, 128], BF16)
                for nm, w in (("mx", 2), ("nmx", 2), ("sm", 2), ("rs", 2), ("t1", 1), ("cc", 1), ("ss", 1), ("rstd", 1)):
                    d[nm] = self.sb(es, nm, [128, w], F32)
                d["o"] = self.sb(es, "o", [128, 256], F32)
                d["junk"] = self.sb(es, "junk", [128, 256], F32)
                st.append(d)
            sy.op("pool", lambda e: e.iota(Dm[:], pattern=[[-1, S]], base=S - 128, channel_multiplier=1,
                                           allow_small_or_imprecise_dtypes=True), writes=["Dm"])
            sy.op("dve", lambda e: e.scalar_tensor_tensor(Dm[:], Dm[:], -1.0, Dm[:], ALU.mult, ALU.min),
                  reads=["Dm"], writes=["Dm"])
            sy.op("pool", lambda e: e.memset(M2[:], 0.0), writes=["M2"])
            sy.op("pool", lambda e: e.memset(M2[0:64, 64:128], -30000.0), reads=["M2"], writes=["M2"])
            self.bcast_load(lamb, self.i_lam[l], 512, "lamb", "lamb")
            self.bcast_load(nw, self.i_dnw[l], 256, "nw", "nw")
            sy.op("dve", lambda e: e.tensor_tensor(lt[:, 0:128], lamb[:, 0:128], lamb[:, 128:256], ALU.mult),
                  reads=["lamb"], writes=["lt"])
            sy.op("dve", lambda e: e.tensor_tensor(lt[:, 128:256], lamb[:, 256:384], lamb[:, 384:512], ALU.mult),
                  reads=["lamb", "lt"], writes=["lt"])
            sy.op("dve", lambda e: e.reduce_sum(lsm[:, 0:1], lt[:, 0:128], AX.X), reads=["lt"], writes=["lsm"])
            sy.op("dve", lambda e: e.reduce_sum(lsm[:, 1:2], lt[:, 128:256], AX.X), reads=["lt", "lsm"], writes=["lsm"])
            sy.op("act", lambda e: e.activation(out=lex[:], in_=lsm[:], func=AF.Exp), reads=["lsm"], writes=["lex"])
            sy.op("dve", lambda e: e.scalar_tensor_tensor(nlam[:], lex[:, 1:2], -lam_init, lex[:, 0:1], ALU.add, ALU.subtract),
                  reads=["lex"], writes=["nlam"])
            sy.op("dve", lambda e: e.tensor_scalar_mul(nws[:], nw[:], 1.0 - lam_init), reads=["nw"], writes=["nws"])
            cnt = {"sb": 0, "ob": 0}

            def qblock(t, h, i, slope):
                d = st[t]
                sc, Ab, AT = d["sc"], d["Ab"], d["AT"]
                K_ = lambda nm: "%s_%d" % (nm, t)
                q0 = i * 128
                kend = q0 + 128
                off = S - 128 - q0
                for j in range(2):
                    sk = K_("sc%d" % j)
                    for c0, n in _chunks(kend, 512):
                        pb = 2 * t + (cnt["sb"] % 2)
                        cnt["sb"] += 1
                        bank = self.ps[pb]
                        sy.op("pe", lambda e: e.matmul(bank[:, 0:n], qTs[j][:, q0:q0 + 128], kTs[j][:, c0:c0 + n],
                                                       start=True, stop=True),
                              reads=["qTs%d" % j, "kTs%d" % j], writes=["ps%d" % pb])
                        sy.op("dve", lambda e: e.scalar_tensor_tensor(sc[j][:, c0:c0 + n], Dm[:, off + c0:off + c0 + n],
                                                                      slope, bank[:, 0:n], ALU.mult, ALU.add),
                              reads=["ps%d" % pb, "Dm"], writes=[sk])
                    sy.op("dve", lambda e: e.tensor_tensor(sc[j][:, q0:kend], sc[j][:, q0:kend], M2[:], ALU.add),
                          reads=[sk, "M2"], writes=[sk])
                    sy.op("dve", lambda e: e.reduce_max(d["mx"][:, j:j + 1], sc[j][:, 0:kend], AX.X),
                          reads=[sk], writes=[K_("mx%d" % j)])
                    yield
                sy.op("dve", lambda e: e.tensor_scalar_mul(d["nmx"][:], d["mx"][:], -1.0),
                      reads=[K_("mx0"), K_("mx1")], writes=[K_("nmx")])
                sy.op("dve", lambda e: e.memset(d["sm"][:], 0.0), writes=[K_("sm")])
                for j in range(2):
                    sk = K_("sc%d" % j)
                    sy.op("act", lambda e: e.activation(out=sc[j][:, 0:kend], in_=sc[j][:, 0:kend], func=AF.Exp,
                                                        bias=d["nmx"][:, j:j + 1], scale=1.0, accum_out=d["sm"][:, j:j + 1]),
                          reads=[sk, K_("nmx"), K_("sm")], writes=[sk, K_("sm")])
                yield
                sy.op("dve", lambda e: e.reciprocal(d["rs"][:], d["sm"][:]), reads=[K_("sm")], writes=[K_("rs")])
                sy.op("dve", lambda e: e.tensor_tensor(d["t1"][:], d["sm"][:, 0:1], d["rs"][:, 1:2], ALU.mult),
                      reads=[K_("sm"), K_("rs")], writes=[K_("t1")])
                sy.op("dve", lambda e: e.tensor_tensor(d["cc"][:], d["t1"][:], nlam[:], ALU.mult),
                      reads=[K_("t1"), "nlam"], writes=[K_("cc")])
                sy.op("dve", lambda e: e.scalar_tensor_tensor(Ab[:, 0:kend], sc[1][:, 0:kend], d["cc"][:, 0:1],
                                                              sc[0][:, 0:kend], ALU.mult, ALU.add),
                      reads=[K_("sc0"), K_("sc1"), K_("cc")], writes=[K_("Ab")])
                yield
                for g0, gn in _chunks(i + 1, 8):
                    tb = 4 + t
                    pbb = self.psb(tb)
                    for kk in range(gn):
                        kb = g0 + kk
                        sy.op("pe", lambda e: e.transpose(out=pbb[:, kk * 128:(kk + 1) * 128],
                                                          in_=Ab[:, kb * 128:(kb + 1) * 128], identity=self.identb[:]),
                              reads=[K_("Ab")], writes=["ps%d" % tb], signal=(kk == gn - 1))
                    sy.op("act", lambda e: e.activation(out=AT[:, g0:g0 + gn, :].rearrange("p a b -> p (a b)"),
                                                        in_=pbb[:, 0:gn * 128], func=AF.Copy),
                          reads=["ps%d" % tb], writes=[K_("AT")])
                    yield
                ob_i = 6 + t
                po = self.ps[ob_i]
                for kb in range(i + 1):
                    sy.op("pe", lambda e: e.matmul(po[:, 0:256], AT[:, kb, :], V[:, kb, :],
                                                   start=(kb == 0), stop=(kb == i)),
                          reads=[K_("AT"), "V"], writes=["ps%d" % ob_i], signal=(kb == i))
                yield
                sy.op("dve", lambda e: e.tensor_scalar_mul(d["o"][:], po[:, 0:256], d["rs"][:, 0:1]),
                      reads=["ps%d" % ob_i, K_("rs")], writes=[K_("o")])
                sy.op("dve", lambda e: e.memset(d["ss"][:], 0.0), writes=[K_("ss")])
                sy.op("act", lambda e: e.activation(out=d["junk"][:], in_=d["o"][:], func=AF.Square, accum_out=d["ss"][:]),
                      reads=[K_("o"), K_("ss")], writes=[K_("junk"), K_("ss")])
                sy.op("act", lambda e: e.activation(out=d["rstd"][:], in_=d["ss"][:], func=AF.Sqrt, bias=self.epsr[:, 0:1],
                                                    scale=1.0 / 256.0),
                      reads=[K_("ss")], writes=[K_("rstd")])
                sy.op("dve", lambda e: e.reciprocal(d["rstd"][:], d["rstd"][:]), reads=[K_("rstd")], writes=[K_("rstd")])
                obb = ob[cnt["ob"] % 2]
                obk = "ob%d" % (cnt["ob"] % 2)
                cnt["ob"] += 1
                sy.op("dve", lambda e: e.scalar_tensor_tensor(obb[:], d["o"][:], d["rstd"][:, 0:1], nws[:], ALU.mult, ALU.mult),
                      reads=[K_("o"), K_("rstd"), "nws"], writes=[obk])
                sy.dma("pool", self.mix[q0:q0 + 128, h * 256:(h + 1) * 256], obb[:], obk, reads=[obk])

            for h in range(NH):
                slope = 2.0 ** (-8.0 * (h + 1) / NH)
                for j in range(2):
                    r0 = (2 * h + j) * 128
                    sy.dma("sp", kTs[j][:], self.kT[r0:r0 + 128, :], "kTs%d" % j, writes=["kTs%d" % j])
                    sy.dma("sp", qTs[j][:], self.qT[r0:r0 + 128, :], "qTs%d" % j, writes=["qTs%d" % j])
                sy.dma("sp", V[:], self.v[:, h * 256:(h + 1) * 256].rearrange("(b p) e -> p b e", p=128),
                       "V", writes=["V"])
                gens = []
                free = list(range(NSTR))
                nxt = 0
                while nxt < NB or gens:
                    if nxt < NB and free:
                        t_ = free.pop(0)
                        gens.append((t_, qblock(t_, h, nxt, slope)))
                        nxt += 1
                    for item in list(gens):
                        try:
                            next(item[1])
                        except StopIteration:
                            gens.remove(item)
                            free.append(item[0])

    def phase_ssd(self, l, xin, xout):
        sy, cfg = self.sy, self.cfg
        S, NG, NHS, SW, AW = cfg.S, cfg.NG, cfg.NHS, cfg.SW, cfg.AW
        NB = S // 128
        with ExitStack() as es:
            xsTg = self.sb(es, "xsTg", [128, 2, S], F32)
            BTg = self.sb(es, "BTg", [128, S], BF16)
            CTg = self.sb(es, "CTg", [128, S], BF16)
            dta = self.sb(es, "dta", [128, NB, NHS], F32)
            Arow = self.sb(es, "Arow", [128, NHS], F32)
            Drow = self.sb(es, "Drow", [128, NHS], F32)
            Dfull = self.sb(es, "Dfull", [128, NHS, 64], F32)
            normw = self.sb(es, "normw", [128, SW], F32)
            prev = self.sb(es, "prev", [128, 256], F32)
            prevb = self.sb(es, "prevb", [128, 256], BF16)
            xs_t = self.sb(es, "xs_t", [128, 256], F32)
            Bt = self.sb(es, "Bt", [128, 128], BF16)
            da = self.sb(es, "da", [128, 4], F32)
            daB = self.sb(es, "daB", [128, 4, 128], F32)
            nacs = self.sb(es, "nacs", [128, 4], F32)
            dst = self.sb(es, "dst", [128, 4], F32)
            w2 = self.sb(es, "w2", [128, 4], F32)
            eRow = self.sb(es, "eRow", [128, 4, 128], F32)
            E = self.sb(es, "E", [128, 4, 128], F32)
            E2 = self.sb(es, "E2", [128, 4, 128], F32)
            CBm = self.sb(es, "CBm", [128, 128], F32)
            LT = self.sb(es, "LT", [128, 4, 128], BF16)
            CeT = self.sb(es, "CeT", [128, 4, 128], BF16)
            xdt = self.sb(es, "xdt", [128, 4, 64], BF16)
            xdd = self.sb(es, "xdd", [128, 4, 64], BF16)
            tt1 = self.sb(es, "tt1", [128, 256], F32)
            yd = self.sb(es, "yd", [128, 256], F32)
            y1 = self.sb(es, "y1", [128, 256], F32)
            junk = self.sb(es, "junk", [128, 256], F32)
            szt = [self.sb(es, "szt%d" % i, [128, 256], F32) for i in range(2)]
            ss = self.sb(es, "ss", [128, 1], F32)
            rstd = self.sb(es, "rstd", [128, 1], F32)
            ob = [self.sb(es, "ob%d" % i, [128, 256], BF16) for i in range(2)]
            self.bcast_load(Arow, self.i_alog[l], NHS, "Arow", "Arow")
            sy.op("act", lambda e: e.activation(out=Arow[:], in_=Arow[:], func=AF.Exp), reads=["Arow"], writes=["Arow"])
            sy.op("dve", lambda e: e.tensor_scalar_mul(Arow[:], Arow[:], -1.0), reads=["Arow"], writes=["Arow"])
            self.bcast_load(Drow, self.i_sd[l], NHS, "Drow", "Drow")
            sy.op("dve", lambda e: e.tensor_copy(Dfull[:], Drow[:].unsqueeze(2).to_broadcast([128, NHS, 64])),
                  reads=["Drow"], writes=["Dfull"])
            self.bcast_load(normw, self.i_snw[l], SW, "normw", "normw")
            sy.dma("sp", dta[:], self.dt.rearrange("(b p) h -> p b h", p=128), "dta", writes=["dta"])
            nsz = 0
            for g in range(NG):
                for f in range(2):
                    sy.dma("sp", xsTg[:, f, :], self.xsT[(2 * g + f) * 128:(2 * g + f + 1) * 128, :], "xsTg", writes=["xsTg"])
                sy.dma("sp", BTg[:], self.BT[g * 128:(g + 1) * 128, :], "BTg", writes=["BTg"])
                sy.dma("sp", CTg[:], self.CT[g * 128:(g + 1) * 128, :], "CTg", writes=["CTg"])
                sy.op("dve", lambda e: e.memset(prev[:], 0.0), writes=["prev"])
                sy.op("dve", lambda e: e.memset(prevb[:], 0.0), writes=["prevb"])
                for c in range(NB):
                    cs = slice(c * 128, (c + 1) * 128)
                    dtc = dta[:, c, 4 * g:4 * g + 4]
                    szb = szt[nsz % 2]
                    szk = "szt%d" % (nsz % 2)
                    sy.dma("sp", szb[:], self.sz[c * 128:(c + 1) * 128, g * 256:(g + 1) * 256], szk, writes=[szk])
                    for f in range(2):
                        sy.op("pe", lambda e: e.transpose(out=self.ps[0][:, f * 128:(f + 1) * 128], in_=xsTg[:, f, cs],
                                                          identity=self.identf[:]),
                              reads=["xsTg"], writes=["ps0"], signal=(f == 1))
                    sy.op("act", lambda e: e.activation(out=xs_t[:], in_=self.ps[0][:, 0:256], func=AF.Copy),
                          reads=["ps0"], writes=["xs_t"])
                    pb1 = self.psb(1)
                    sy.op("pe", lambda e: e.transpose(out=pb1[:, 0:128], in_=BTg[:, cs], identity=self.identb[:]),
                          reads=["BTg"], writes=["ps1"])
                    sy.op("dve", lambda e: e.tensor_copy(Bt[:], pb1[:, 0:128]), reads=["ps1"], writes=["Bt"])
                    if float(os.environ.get('SSD_DBG', '99')) < 1:
                        continue
                    sy.op("dve", lambda e: e.tensor_tensor(da[:], dtc, Arow[:, 4 * g:4 * g + 4], ALU.mult),
                          reads=["dta", "Arow"], writes=["da"])
                    sy.op("dve", lambda e: e.tensor_copy(daB[:], da[:].unsqueeze(2).to_broadcast([128, 4, 128])),
                          reads=["da"], writes=["daB"])
                    for h in range(4):
                        sy.op("pe", lambda e: e.matmul(self.ps[2][:, h * 128:(h + 1) * 128], daB[:, h, :], self.tri[:],
                                                       start=True, stop=True),
                              reads=["daB", "tri"], writes=["ps2"], signal=(h == 3))
                    if float(os.environ.get('SSD_DBG', '99')) < 0.3:
                        continue
                    sy.op("pe", lambda e: e.matmul(self.ps[3][:, 0:4], self.tri[:], da[:], start=True, stop=True),
                          reads=["da", "tri"], writes=["ps3"], signal=False)
                    sy.op("pe", lambda e: e.matmul(self.ps[3][:, 4:8], self.ltri[:], da[:], start=True, stop=True),
                          reads=["da", "ltri"], writes=["ps3"])
                    sy.op("dve", lambda e: e.tensor_scalar_mul(nacs[:], self.ps[3][:, 0:4], -1.0),
                          reads=["ps3"], writes=["nacs"])
                    sy.op("act", lambda e: e.activation(out=dst[:], in_=self.ps[3][:, 4:8], func=AF.Exp),
                          reads=["ps3"], writes=["dst"])
                    if float(os.environ.get('SSD_DBG', '99')) < 0.6:
                        continue
                    sy.op("act", lambda e: e.activation(out=eRow[:].rearrange("p a b -> p (a b)"), in_=self.ps[2][:, 0:512],
                                                        func=AF.Exp),
                          reads=["ps2"], writes=["eRow"])
                    for h in range(4):
                        sy.op("act", lambda e: e.activation(out=E[:, h, :], in_=self.ps[2][:, h * 128:(h + 1) * 128], func=AF.Exp,
                                                            bias=nacs[:, h:h + 1], scale=1.0),
                              reads=["ps2", "nacs"], writes=["E"])
                    sy.op("dve", lambda e: e.tensor_scalar_min(E2[:].rearrange("p a b -> p (a b)"),
                                                               E[:].rearrange("p a b -> p (a b)"), 1.0),
                          reads=["E"], writes=["E2"])
                    if float(os.environ.get('SSD_DBG', '99')) < 2:
                        continue
                    sy.op("pe", lambda e: e.matmul(self.ps[4][:, 0:128], BTg[:, cs], CTg[:, cs], start=True, stop=True),
                          reads=["BTg", "CTg"], writes=["ps4"])
                    sy.op("dve", lambda e: e.tensor_tensor(CBm[:], self.ps[4][:, 0:128], self.tri[:], ALU.mult),
                          reads=["ps4", "tri"], writes=["CBm"])
                    sy.op("dve", lambda e: e.tensor_tensor(LT[:], E2[:], CBm[:].unsqueeze(1).to_broadcast([128, 4, 128]), ALU.mult),
                          reads=["E2", "CBm"], writes=["LT"])
                    sy.op("pool", lambda e: e.tensor_tensor(CeT[:], eRow[:], CTg[:, cs].unsqueeze(1).to_broadcast([128, 4, 128]),
                                                            ALU.mult),
                          reads=["eRow", "CTg"], writes=["CeT"])
                    xv = xs_t[:].rearrange("p (h e) -> p h e", h=4)
                    sy.op("dve", lambda e: e.tensor_tensor(xdt[:], xv, dtc.unsqueeze(2).to_broadcast([128, 4, 64]), ALU.mult),
                          reads=["xs_t", "dta"], writes=["xdt"])
                    sy.op("dve", lambda e: e.tensor_tensor(w2[:], dtc, dst[:], ALU.mult), reads=["dta", "dst"], writes=["w2"])
                    sy.op("dve", lambda e: e.tensor_tensor(xdd[:], xv, w2[:].unsqueeze(2).to_broadcast([128, 4, 64]), ALU.mult),
                          reads=["xs_t", "w2"], writes=["xdd"])
                    if float(os.environ.get('SSD_DBG', '99')) < 3:
                        continue
                    for h in range(4):
                        sy.op("pe", lambda e: e.matmul(self.ps[5][:, h * 64:(h + 1) * 64], LT[:, h, :], xdt[:, h, :],
                                                       start=True, stop=False),
                              reads=["LT", "xdt"], writes=["ps5"], signal=False)
                        sy.op("pe", lambda e: e.matmul(self.ps[5][:, h * 64:(h + 1) * 64], CeT[:, h, :],
                                                       prevb[:, h * 64:(h + 1) * 64], start=False, stop=True),
                              reads=["CeT", "prevb"], writes=["ps5"], signal=(h == 3))
                    if float(os.environ.get('SSD_DBG', '99')) < 4:
                        continue
                    sy.op("pe", lambda e: e.matmul(self.ps[6][:, 0:256], Bt[:], xdd[:].rearrange("p h e -> p (h e)"),
                                                   start=True, stop=True),
                          reads=["Bt", "xdd"], writes=["ps6"])
                    sy.op("dve", lambda e: e.tensor_tensor(tt1[:].rearrange("p (h e) -> p h e", h=4),
                                                           prev[:].rearrange("p (h e) -> p h e", h=4),
                                                           eRow[:, :, 127:128].to_broadcast([128, 4, 64]), ALU.mult),
                          reads=["prev", "eRow"], writes=["tt1"])
                    sy.op("dve", lambda e: e.tensor_tensor(prev[:], tt1[:], self.ps[6][:, 0:256], ALU.add),
                          reads=["tt1", "ps6"], writes=["prev"])
                    sy.op("act", lambda e: e.activation(out=prevb[:], in_=prev[:], func=AF.Copy),
                          reads=["prev"], writes=["prevb"])
                    if float(os.environ.get('SSD_DBG', '99')) < 5:
                        continue
                    sy.op("pool", lambda e: e.tensor_tensor(yd[:], xs_t[:],
                                                            Dfull[:, 4 * g:4 * g + 4, :].rearrange("p h e -> p (h e)"), ALU.mult),
                          reads=["xs_t", "Dfull"], writes=["yd"])
                    sy.op("dve", lambda e: e.tensor_tensor(y1[:], self.ps[5][:, 0:256], yd[:], ALU.add),
                          reads=["ps5", "yd"], writes=["y1"])
                    sy.op("dve", lambda e: e.tensor_tensor(y1[:], y1[:], szb[:], ALU.mult),
                          reads=["y1", szk], writes=["y1"])
                    nsz += 1
                    sy.op("dve", lambda e: e.memset(ss[:], 0.0), writes=["ss"])
                    sy.op("act", lambda e: e.activation(out=junk[:], in_=y1[:], func=AF.Square, accum_out=ss[:]),
                          reads=["y1", "ss"], writes=["junk", "ss"])
                    sy.op("act", lambda e: e.activation(out=rstd[:], in_=ss[:], func=AF.Sqrt, bias=self.epsr[:, 0:1], scale=1.0 / 256.0),
                          reads=["ss"], writes=["rstd"])
                    sy.op("dve", lambda e: e.reciprocal(rstd[:], rstd[:]), reads=["rstd"], writes=["rstd"])
                    obb = ob[c % 2]
                    obk = "ob%d" % (c % 2)
                    sy.op("dve", lambda e: e.scalar_tensor_tensor(obb[:], y1[:], rstd[:, 0:1], normw[:, g * 256:(g + 1) * 256],
                                                                  ALU.mult, ALU.mult),
                          reads=["y1", "rstd", "normw"], writes=[obk])
                    sy.dma("pool", self.mix[c * 128:(c + 1) * 128, AW + g * 256:AW + (g + 1) * 256], obb[:], obk, reads=[obk])

    def proj_ln(self, l, KA, load_A, w_d, wkey, resid, gate_off, lng_in, lnb_in, dst, T, KH=1, KS=16):
        sy, cfg = self.sy, self.cfg
        D, S, KC = cfg.D, cfg.S, cfg.KC
        KAH = KA // KH
        nD = (D + 511) // 512
        NS = T // 128
        with ExitStack() as es:
            AT = self.sb(es, "plA", [128, KAH, T], BF16)
            wb = [self.sb(es, "plw%d" % i, [128, KS, 512], BF16) for i in range(2)]
            res = [self.sb(es, "plres%d" % i, [128, D], F32) for i in range(NS)]
            gate_b = self.sb(es, "plgate", [128, D], F32)
            lng = self.sb(es, "pllng", [128, D], F32)
            lnb = self.sb(es, "pllnb", [128, D], F32)
            tmp = [self.sb(es, "pltmp%d" % i, [128, 512], F32) for i in range(2)]
            st = self.sb(es, "plst", [128, nD, 6], F32)
            mv = self.sb(es, "plmv", [128, 2], F32)
            rstd = self.sb(es, "plrstd", [128, 1], F32)
            nmr = self.sb(es, "plnmr", [128, 1], F32)
            self.bcast_load(lng, lng_in, D, "pllng", "lng")
            self.bcast_load(lnb, lnb_in, D, "pllnb", "lnb")
            dg = self.sb(es, "pldg", [128, 128], F32)
            gate_col0 = gate_off // 128
            for k in range(KC):
                sy.op("dve", lambda e: e.tensor_scalar_mul(dg[:], self.identf[:], self.modT[l][:, gate_col0 + k:gate_col0 + k + 1]),
                      reads=["identf"], writes=["dg"])
                sy.op("pe", lambda e: e.matmul(self.ps[7][:, 0:128], self.onesf[:], dg[:], start=True, stop=True),
                      reads=["dg", "onesf"], writes=["ps7"])
                sy.op("act", lambda e: e.activation(out=gate_b[:, k * 128:(k + 1) * 128], in_=self.ps[7][:, 0:128], func=AF.Copy),
                      reads=["ps7"], writes=["gate_b"])
            nw = 0
            nt = 0
            for tt in range(S // T):
                t0 = tt * T
                for sub in range(NS):
                    sy.dma("sp", res[sub][:], resid[t0 + sub * 128:t0 + (sub + 1) * 128, :], "plres%d" % sub,
                           writes=["res%d" % sub])
                for hf in range(KH):
                    kbase = hf * KAH
                    load_A(AT, t0, T, es, kbase, KAH)
                    kslabs = _chunks(KAH, KS)
                    for c0, n in _chunks(D, 512):
                        for si, (k0, kn) in enumerate(kslabs):
                            w = wb[nw % 2]
                            wk = "plw%d" % (nw % 2)
                            nw += 1
                            sy.dma("sp", w[:, 0:kn, 0:n],
                                   w_d[(kbase + k0) * 128:(kbase + k0 + kn) * 128, c0:c0 + n].rearrange("(k p) n -> p k n", p=128),
                                   wk, writes=[wk])
                            for sub in range(NS):
                                bank = self.ps[sub]
                                for kk in range(kn):
                                    k = k0 + kk
                                    sy.op("pe", lambda e: e.matmul(bank[:, 0:n], AT[:, k, sub * 128:(sub + 1) * 128], w[:, kk, 0:n],
                                                                   start=(k == 0), stop=(k == KAH - 1)),
                                          reads=[wk, "plA"], writes=["ps%d" % sub],
                                          signal=(kk == kn - 1))
                        for sub in range(NS):
                            bank = self.ps[sub]
                            tb = tmp[nt % 2]
                            tk = "pltmp%d" % (nt % 2)
                            nt += 1
                            sy.op("dve", lambda e: e.tensor_tensor(tb[:, 0:n], bank[:, 0:n], gate_b[:, c0:c0 + n], ALU.mult),
                                  reads=["ps%d" % sub, "gate_b"], writes=[tk])
                            if hf == 0:
                                sy.op("dve", lambda e: e.scalar_tensor_tensor(res[sub][:, c0:c0 + n], res[sub][:, c0:c0 + n],
                                                                              cfg.alpha, tb[:, 0:n], ALU.mult, ALU.add),
                                      reads=[tk, "res%d" % sub], writes=["res%d" % sub])
                            else:
                                sy.op("pool", lambda e: e.tensor_tensor(res[sub][:, c0:c0 + n], res[sub][:, c0:c0 + n],
                                                                        tb[:, 0:n], ALU.add),
                                      reads=[tk, "res%d" % sub], writes=["res%d" % sub])
                for sub in range(NS):
                    r = res[sub]
                    rk = "res%d" % sub
                    for i, (c0, n) in enumerate(_chunks(D, 512)):
                        sy.op("dve", lambda e: e.bn_stats(st[:, i, :], r[:, c0:c0 + n]), reads=[rk], writes=["plst"])
                    sy.op("dve", lambda e: e.bn_aggr(mv[:], st[:].rearrange("p a b -> p (a b)")), reads=["plst"], writes=["mv"])
                    sy.op("act", lambda e: e.activation(out=rstd[:], in_=mv[:, 1:2], func=AF.Sqrt, bias=self.epsl[:, 0:1], scale=1.0),
                          reads=["mv"], writes=["rstd"])
                    sy.op("dve", lambda e: e.reciprocal(rstd[:], rstd[:]), reads=["rstd"], writes=["rstd"])
                    sy.op("dve", lambda e: e.scalar_tensor_tensor(nmr[:], mv[:, 0:1], -1.0, rstd[:], ALU.mult, ALU.mult),
                          reads=["mv", "rstd"], writes=["nmr"])
                    sy.op("act", lambda e: e.activation(out=r[:], in_=r[:], func=AF.Identity, bias=nmr[:, 0:1], scale=rstd[:, 0:1]),
                          reads=[rk, "nmr", "rstd"], writes=[rk])
                    sy.op("dve", lambda e: e.tensor_tensor(r[:], r[:], lng[:], ALU.mult), reads=[rk, "lng"], writes=[rk])
                    sy.op("pool", lambda e: e.tensor_tensor(r[:], r[:], lnb[:], ALU.add), reads=[rk, "lnb"], writes=[rk])
                    sy.dma("pool", dst[t0 + sub * 128:t0 + (sub + 1) * 128, :], r[:], "plsto%d" % sub, reads=[rk])

    def phase_outproj(self, l, xin, xout):
        sy, cfg = self.sy, self.cfg
        KA = cfg.DMIX // 128
        T = min(256, cfg.S)
        holder = {}

        def load_A(AT, t0, T_, es, kbase, kn_):
            if "mst" not in holder:
                holder["mst"] = [self.sb(es, "mst%d" % i, [128, cfg.DMIX], BF16) for i in range(2)]
                holder["n"] = 0
            for sub in range(T_ // 128):
                m = holder["mst"][sub % 2]
                mk = "mst%d" % (sub % 2)
                sy.dma("sp", m[:], self.mix[t0 + sub * 128:t0 + (sub + 1) * 128, :], mk, writes=[mk])
                for g0, gn in _chunks(KA, 8):
                    tb = 4 + holder["n"] % 2
                    holder["n"] += 1
                    pbb = self.psb(tb)
                    for kk in range(gn):
                        k = g0 + kk
                        sy.op("pe", lambda e: e.transpose(out=pbb[:, kk * 128:(kk + 1) * 128], in_=m[:, k * 128:(k + 1) * 128],
                                                          identity=self.identb[:]),
                              reads=[mk], writes=["ps%d" % tb], signal=(kk == gn - 1))
                    for kk in range(gn):
                        k = g0 + kk
                        if kk % 2 == 0:
                            sy.op("act", lambda e: e.activation(out=AT[:, k, sub * 128:(sub + 1) * 128],
                                                                in_=pbb[:, kk * 128:(kk + 1) * 128], func=AF.Copy),
                                  reads=["ps%d" % tb], writes=["plA"])
                        else:
                            sy.op("dve", lambda e: e.tensor_copy(AT[:, k, sub * 128:(sub + 1) * 128], pbb[:, kk * 128:(kk + 1) * 128]),
                                  reads=["ps%d" % tb], writes=["plA"])

        self.proj_ln(l, KA, load_A, self.w_out[l], "wc_out%d" % l, xin, 2 * cfg.D, self.i_ln1g[l], self.i_ln1b[l], self.x1, T)

    def phase_ffn1(self, l, xin, xout):
        sy, cfg = self.sy, self.cfg
        KC, D, S, DFF = cfg.KC, cfg.D, cfg.S, cfg.DFF
        T = min(512, S)
        NFC = DFF // 128
        with ExitStack() as es:
            hT = self.sb(es, "h2T", [128, KC, T], BF16)
            wg = [self.sb(es, "wg%d" % i, [128, KC, 256], BF16) for i in range(2)]
            wu = [self.sb(es, "wu%d" % i, [128, KC, 256], BF16) for i in range(2)]
            xst = [self.sb(es, "xst%d" % i, [128, D], F32) for i in range(2)]
            gpad = [self.sb(es, "gpad%d" % i, [128, 2 + 512], F32) for i in range(2)]
            cvt = [self.sb(es, "cvt%d" % i, [128, 512], F32) for i in range(2)]
            sgt = [self.sb(es, "sgt%d" % i, [128, 512], F32) for i in range(2)]
            obf = [self.sb(es, "obf%d" % i, [128, 512], BF16) for i in range(3)]
            halo = self.sb(es, "fhalo", [128, NFC, 2], F32)
            cw = self.sb(es, "fcw", [128, 3 * NFC], F32)
            cb = self.sb(es, "fcb", [128, NFC], F32)
            self.load_cols(es, cw, "cw", self.i_fcw[l], 3 * NFC, "fcw")
            self.load_cols(es, cb, "cb", self.i_fcb[l], NFC, "fcb")
            sy.op("pool", lambda e: e.memset(halo[:], 0.0), writes=["halo%d" % i for i in range(NFC)])
            scale_ap = self.sc1p[l][:, KC:2 * KC]
            shift_ap = self.modT[l][:, 3 * KC:4 * KC]
            blocks = _chunks(DFF, 256)
            nev = 0
            nst = 0
            for tt in range(S // T):
                t0 = tt * T
                self.build_hT(self.x1, t0, T, hT, xst, scale_ap, shift_ap, "hT")
                for bi, (c0, bn) in enumerate(blocks):
                    gi = tt * len(blocks) + bi
                    wgb, wub = wg[gi % 2], wu[gi % 2]
                    wgk, wuk = "wg%d" % (gi % 2), "wu%d" % (gi % 2)
                    sy.dma("sp", wgb[:, :, 0:bn], self.w_g[l, :, c0:c0 + bn].rearrange("(k p) n -> p k n", p=128), wgk, writes=[wgk])
                    sy.dma("sp", wub[:, :, 0:bn], self.w_u[l, :, c0:c0 + bn].rearrange("(k p) n -> p k n", p=128), wuk, writes=[wuk])
                    for oc in range(bn // 128):
                        fc = c0 // 128 + oc
                        pg = 2 + (nev % 3)
                        pu = 5 + (nev % 3)
                        nev += 1
                        bg, bu = self.ps[pg], self.ps[pu]
                        for k in range(KC):
                            sy.op("pe", lambda e: e.matmul(bg[:, 0:T], wgb[:, k, oc * 128:(oc + 1) * 128], hT[:, k, :],
                                                           start=(k == 0), stop=(k == KC - 1)),
                                  reads=[wgk, "hT"], writes=["ps%d" % pg], signal=(k == KC - 1))
                        for k in range(KC):
                            sy.op("pe", lambda e: e.matmul(bu[:, 0:T], wub[:, k, oc * 128:(oc + 1) * 128], hT[:, k, :],
                                                           start=(k == 0), stop=(k == KC - 1)),
                                  reads=[wuk, "hT"], writes=["ps%d" % pu], signal=(k == KC - 1))
                        gp = gpad[fc % 2]
                        gpk = "gpad%d" % (fc % 2)
                        cv = cvt[fc % 2]
                        cvk = "cvt%d" % (fc % 2)
                        sg = sgt[fc % 2]
                        sgk = "sgt%d" % (fc % 2)
                        sy.op("act", lambda e: e.activation(out=gp[:, 2:2 + T], in_=bg[:, 0:T], func=AF.Copy),
                              reads=["ps%d" % pg], writes=[gpk + "m"])
                        sy.op("pool", lambda e: e.tensor_copy(gp[:, 0:2], halo[:, fc, :]),
                              reads=["halo%d" % fc], writes=[gpk + "h"])
                        sy.op("dve", lambda e: e.tensor_scalar(cv[:, 0:T], gp[:, 2:2 + T], cw[:, 2 * NFC + fc:2 * NFC + fc + 1],
                                                               cb[:, fc:fc + 1], ALU.mult, ALU.add),
                              reads=[gpk + "m", gpk + "h", "cw", "cb"], writes=[cvk])
                        for j in range(2):
                            sy.op("dve", lambda e: e.scalar_tensor_tensor(cv[:, 0:T], gp[:, j:j + T],
                                                                          cw[:, j * NFC + fc:j * NFC + fc + 1], cv[:, 0:T],
                                                                          ALU.mult, ALU.add),
                                  reads=[gpk + "m", gpk + "h", cvk], writes=[cvk])
                        sy.op("pool", lambda e: e.tensor_copy(halo[:, fc, :], gp[:, T:T + 2]),
                              reads=[gpk + "m"], writes=["halo%d" % fc])
                        sy.op("act", lambda e: e.activation(out=sg[:, 0:T], in_=cv[:, 0:T], func=AF.Silu),
                              reads=[cvk], writes=[sgk])
                        si = nst % 3
                        nst += 1
                        sy.op("dve", lambda e: e.tensor_tensor(obf[si][:, 0:T], sg[:, 0:T], bu[:, 0:T], ALU.mult),
                              reads=[sgk, "ps%d" % pu], writes=["obf%d" % si])
                        sy.dma("pool", self.actT[fc * 128:(fc + 1) * 128, t0:t0 + T], obf[si][:, 0:T], "obf%d" % si,
                               reads=["obf%d" % si])

    def phase_ffn2(self, l, xin, xout):
        sy, cfg = self.sy, self.cfg
        KA = cfg.DFF // 128
        T = min(512, cfg.S)
        KH = 2 if KA % 2 == 0 else 1

        def load_A(AT, t0, T_, es, kbase, kn_):
            for k0, kn in _chunks(kn_, 32):
                sy.dma("sp", AT[:, k0:k0 + kn, :],
                       self.actT[(kbase + k0) * 128:(kbase + k0 + kn) * 128, t0:t0 + T_].rearrange("(k p) t -> p k t", p=128),
                       "plA", writes=["plA"])

        self.proj_ln(l, KA, load_A, self.w_d[l], "wc_d%d" % l, self.x1, 5 * cfg.D, self.i_ln2g[l], self.i_ln2b[l], xout, T,
                     KH=KH, KS=8)


def make_in_map(cfg, inputs, b):
    f = lambda a: np.ascontiguousarray(a, dtype=np.float32)
    L = cfg.L
    return {
        "x": f(inputs["x"][b]),
        "c": f(inputs["c"][b]).reshape(cfg.KC, 128),
        "w_mod": f(inputs["w_mod"]),
        "b_mod": f(inputs["b_mod"]).reshape(L, 6 * cfg.KC, 128),
        "w_in": f(inputs["w_in"]),
        "diff_lambda": f(inputs["diff_lambda"]).reshape(L, 1, 512),
        "diff_norm_w": f(inputs["diff_norm_w"]).reshape(L, 1, 256),
        "ssd_conv_w": f(inputs["ssd_conv_w"]).reshape(L, 4 * cfg.XBC // 128, 128),
        "ssd_conv_b": f(inputs["ssd_conv_b"]).reshape(L, cfg.XBC // 128, 128),
        "ssd_dt_bias": f(inputs["ssd_dt_bias"]).reshape(L, 1, cfg.NHS),
        "ssd_a_log": f(inputs["ssd_a_log"]).reshape(L, 1, cfg.NHS),
        "ssd_d": f(inputs["ssd_d"]).reshape(L, 1, cfg.NHS),
        "ssd_norm_w": f(inputs["ssd_norm_w"]).reshape(L, 1, cfg.SW),
        "w_out": f(inputs["w_out"]),
        "ln1_g": f(inputs["ln1_g"]).reshape(L, 1, cfg.D),
        "ln1_b": f(inputs["ln1_b"]).reshape(L, 1, cfg.D),
        "w_gate": f(inputs["w_gate"]),
        "w_up": f(inputs["w_up"]),
        "ffn_conv_w": f(inputs["ffn_conv_w"]).reshape(L, 3 * cfg.DFF // 128, 128),
        "ffn_conv_b": f(inputs["ffn_conv_b"]).reshape(L, cfg.DFF // 128, 128),
        "w_down": f(inputs["w_down"]),
        "ln2_g": f(inputs["ln2_g"]).reshape(L, 1, cfg.D),
        "ln2_b": f(inputs["ln2_b"]).reshape(L, 1, cfg.D),
    }


def kernel(**inputs):
    cfg = Cfg()
    B = inputs["x"].shape[0]
    prog = Prog(cfg)
    nc = prog.build()
    in_maps = [make_in_map(cfg, inputs, b) for b in range(B)]
    res = run_bass_kernel_spmd(nc, in_maps, core_ids=list(range(B)))
    return np.stack([np.asarray(r["out"], dtype=np.float32) for r in res.results], axis=0)
```

```python
import math
import os
from contextlib import ExitStack

import numpy as np
import concourse.bass as bass
import concourse.mybir as mybir
from concourse.bass_utils import run_bass_kernel_spmd

F32 = mybir.dt.float32
BF16 = mybir.dt.bfloat16
AF = mybir.ActivationFunctionType
ALU = mybir.AluOpType
AX = mybir.AxisListType

LN_EPS = 1e-5
RMS_EPS = 1e-5


class Cfg:
    def __init__(self, D=4096, S=4096, NH=8, NG=8, DFF=11008, L=2):
        self.D, self.S, self.NH, self.NG, self.DFF, self.L = D, S, NH, NG, DFF, L
        self.AW = NH * 256
        self.SW = NG * 256
        self.NHS = NG * 4
        self.DMIX = self.AW + self.SW
        self.XBC = self.SW + 2 * NG * 128
        self.NIN = 3 * self.AW + self.SW + self.XBC + self.NHS
        self.KC = D // 128
        self.alpha = (2.0 * L) ** 0.25


class Sy:
    def __init__(self, nc, es):
        self.nc = nc
        self.es = es
        self.engs = {"pe": nc.tensor, "dve": nc.vector, "act": nc.scalar,
                     "pool": nc.gpsimd, "sp": nc.sync}
        self.sem = {}
        self.cnt = {}
        for e in ("pe", "dve", "act", "pool"):
            self.sem[e] = es.enter_context(nc.semaphore("s_" + e))
            self.cnt[e] = 0
        self.waited = {}
        self.st = {}
        self.pend_r = []
        self.pend_w = []
        self.nwait = 0

    def _lane(self, name):
        if name not in self.sem:
            self.sem[name] = self.es.enter_context(self.nc.semaphore("l_" + name))
            self.cnt[name] = 0
        return name

    def _wait(self, e, tok):
        sk, val = tok
        if e == "pe" and sk == "pe":
            return
        k = (e, sk)
        if self.waited.get(k, 0) >= val:
            return
        self.waited[k] = val
        self.engs[e].wait_ge(self.sem[sk], val)
        self.nwait += 1

    def _deps(self, e, reads, writes, own=None):
        for k in reads:
            s = self.st.get(k)
            if s and s[0] and s[0][0] != own:
                self._wait(e, s[0])
        for k in writes:
            s = self.st.get(k)
            if s:
                if s[0] and s[0][0] != own:
                    self._wait(e, s[0])
                for t in s[1]:
                    if t[0] != own:
                        self._wait(e, t)

    def _commit(self, tok, reads, writes):
        for k in reads:
            s = self.st.setdefault(k, [None, []])
            s[1] = [t for t in s[1] if t[0] != tok[0]] + [tok]
        for k in writes:
            self.st[k] = [tok, []]

    def op(self, e, fn, reads=(), writes=(), signal=True):
        self._deps(e, reads, writes)
        ins = fn(self.engs[e])
        if e == "pe" and not signal:
            self.pend_r += list(reads)
            self.pend_w += list(writes)
            return ins
        self.cnt[e] += 1
        ins.then_inc(self.sem[e], 1)
        tok = (e, self.cnt[e])
        if e == "pe":
            reads = list(reads) + self.pend_r
            writes = list(writes) + self.pend_w
            self.pend_r, self.pend_w = [], []
        self._commit(tok, reads, writes)
        return ins

    def dma(self, q, out, in_, lane, reads=(), writes=()):
        self._lane(lane)
        self._deps(q, reads, writes, own=lane)
        ins = self.engs[q].dma_start(out=out, in_=in_)
        self.cnt[lane] += 16
        ins.then_inc(self.sem[lane], 16)
        self._commit((lane, self.cnt[lane]), reads, writes)
        return ins

    def barrier(self):
        for e in ("pe", "dve", "act", "pool", "sp"):
            for sk, c in self.cnt.items():
                if c > 0 and sk != e:
                    self._wait(e, (sk, c))
        self.st = {}


def _chunks(n, c):
    return [(i, min(c, n - i)) for i in range(0, n, c)]


class Prog:
    def __init__(self, cfg, debug=False, stop_after=None):
        self.cfg = cfg
        self.debug = debug
        self.stop_after = stop_after
        self.nc = bass.Bass("TRN2", target_bir_lowering=False)
        self.dbg_outs = []

    def din(self, name, shape, dt=F32):
        return self.nc.dram_tensor(name, list(shape), dt, kind="ExternalInput").ap()

    def dscr(self, name, shape, dt):
        kind = "ExternalOutput" if self.debug else "Internal"
        if self.debug:
            self.dbg_outs.append(name)
        return self.nc.dram_tensor(name, list(shape), dt, kind=kind).ap()

    def sb(self, es, name, shape, dt):
        self._nsb = getattr(self, "_nsb", 0) + 1
        return es.enter_context(self.nc.sbuf_tensor("%s_%d" % (name, self._nsb), list(shape), dt))

    def build(self):
        cfg, nc = self.cfg, self.nc
        D, S, L = cfg.D, cfg.S, cfg.L
        self.i_x = self.din("x", [S, D])
        self.i_c = self.din("c", [cfg.KC, 128])
        self.i_wmod = self.din("w_mod", [L, D, 6 * D])
        self.i_bmod = self.din("b_mod", [L, 6 * cfg.KC, 128])
        self.i_win = self.din("w_in", [L, D, cfg.NIN])
        self.i_lam = self.din("diff_lambda", [L, 1, 512])
        self.i_dnw = self.din("diff_norm_w", [L, 1, 256])
        self.i_scw = self.din("ssd_conv_w", [L, 4 * cfg.XBC // 128, 128])
        self.i_scb = self.din("ssd_conv_b", [L, cfg.XBC // 128, 128])
        self.i_dtb = self.din("ssd_dt_bias", [L, 1, cfg.NHS])
        self.i_alog = self.din("ssd_a_log", [L, 1, cfg.NHS])
        self.i_sd = self.din("ssd_d", [L, 1, cfg.NHS])
        self.i_snw = self.din("ssd_norm_w", [L, 1, cfg.SW])
        self.i_wout = self.din("w_out", [L, cfg.DMIX, D])
        self.i_ln1g = self.din("ln1_g", [L, 1, D])
        self.i_ln1b = self.din("ln1_b", [L, 1, D])
        self.i_wg = self.din("w_gate", [L, D, cfg.DFF])
        self.i_wu = self.din("w_up", [L, D, cfg.DFF])
        self.i_fcw = self.din("ffn_conv_w", [L, 3 * cfg.DFF // 128, 128])
        self.i_fcb = self.din("ffn_conv_b", [L, cfg.DFF // 128, 128])
        self.i_wd = self.din("w_down", [L, cfg.DFF, D])
        self.i_ln2g = self.din("ln2_g", [L, 1, D])
        self.i_ln2b = self.din("ln2_b", [L, 1, D])
        self.o_out = nc.dram_tensor("out", [S, D], F32, kind="ExternalOutput").ap()

        self.w_in = self.dscr("wb_in", [L, D, cfg.NIN], BF16)
        self.w_out = self.dscr("wb_out", [L, cfg.DMIX, D], BF16)
        self.w_g = self.dscr("wb_g", [L, D, cfg.DFF], BF16)
        self.w_u = self.dscr("wb_u", [L, D, cfg.DFF], BF16)
        self.w_d = self.dscr("wb_d", [L, cfg.DFF, D], BF16)
        self.qT = self.dscr("s_qT", [cfg.AW, S], BF16)
        self.kT = self.dscr("s_kT", [cfg.AW, S], BF16)
        self.v = self.dscr("s_v", [S, cfg.AW], BF16)
        self.sz = self.dscr("s_sz", [S, cfg.SW], F32)
        self.xsT = self.dscr("s_xsT", [cfg.SW, S], F32)
        self.BT = self.dscr("s_BT", [cfg.NG * 128, S], BF16)
        self.CT = self.dscr("s_CT", [cfg.NG * 128, S], BF16)
        self.dt = self.dscr("s_dt", [S, cfg.NHS], F32)
        self.mix = self.dscr("s_mix", [S, cfg.DMIX], BF16)
        self.x1 = self.dscr("s_x1", [S, D], F32)
        self.actT = self.dscr("s_actT", [cfg.DFF, S], BF16)
        self.xl = self.dscr("s_xl", [S, D], F32)

        with ExitStack() as es:
            self.sy = Sy(nc, es)
            self.ps = [es.enter_context(nc.psum_tensor("ps%d" % i, [128, 512], F32))
                       for i in range(8)]
            self.consts(es)
            self.cast_weights()
            for l in range(L):
                self.phase_mod(l)
            self.sy.barrier()
            for l in range(L):
                xin = self.i_x if l == 0 else self.xl
                xout = self.o_out if l == L - 1 else self.xl
                for pi, ph in enumerate((self.phase_inproj, self.phase_attn, self.phase_ssd,
                                         self.phase_outproj, self.phase_ffn1, self.phase_ffn2)):
                    if self.stop_after is not None and (l, pi) > self.stop_after:
                        continue
                    ph(l, xin, xout)
                    self.sy.barrier()
            self.sy.barrier()
        return nc

    def consts(self, es):
        sy, nc, cfg = self.sy, self.nc, self.cfg
        self.identf = self.sb(es, "identf", [128, 128], F32)
        self.identb = self.sb(es, "identb", [128, 128], BF16)
        self.onesf = self.sb(es, "onesf", [128, 128], F32)
        self.tri = self.sb(es, "tri", [128, 128], F32)
        self.ltri = self.sb(es, "ltri", [128, 128], F32)
        self.modT = [self.sb(es, "modT%d" % l, [128, 6 * cfg.KC], F32) for l in range(cfg.L)]
        self.sc1p = [self.sb(es, "sc1p%d" % l, [128, 2 * cfg.KC], F32) for l in range(cfg.L)]
        sy.op("pool", lambda e: e.memset(self.onesf[:], 1.0), writes=["onesf"])
        sy.op("pool", lambda e: e.affine_select(
            out=self.identf[:], in_=self.onesf[:], pattern=[[-1, 128]],
            compare_op=ALU.is_equal, fill=0.0, base=0, channel_multiplier=1),
            reads=["onesf"], writes=["identf"])
        sy.op("pool", lambda e: e.affine_select(
            out=self.tri[:], in_=self.onesf[:], pattern=[[1, 128]],
            compare_op=ALU.is_ge, fill=0.0, base=0, channel_multiplier=-1),
            reads=["onesf"], writes=["tri"])
        sy.op("pool", lambda e: e.affine_select(
            out=self.ltri[:], in_=self.onesf[:], pattern=[[-1, 128]],
            compare_op=ALU.is_gt, fill=0.0, base=0, channel_multiplier=1),
            reads=["onesf"], writes=["ltri"])
        sy.op("dve", lambda e: e.tensor_copy(self.identb[:], self.identf[:]),
              reads=["identf"], writes=["identb"])
        self.epsr = self.sb(es, "epsr", [128, 1], F32)
        self.epsl = self.sb(es, "epsl", [128, 1], F32)
        self.zeroc = self.sb(es, "zeroc", [128, 1], F32)
        sy.op("pool", lambda e: e.memset(self.zeroc[:], 0.0), writes=["zeroc"])
        sy.op("pool", lambda e: e.memset(self.epsr[:], RMS_EPS), writes=["epsr"])
        sy.op("pool", lambda e: e.memset(self.epsl[:], LN_EPS), writes=["epsl"])

    def cast_weights(self):
        sy, cfg = self.sy, self.cfg
        n = 0
        for l in range(cfg.L):
            for src, dst, rows in ((self.i_win, self.w_in, cfg.D), (self.i_wout, self.w_out, cfg.DMIX),
                                   (self.i_wg, self.w_g, cfg.D), (self.i_wu, self.w_u, cfg.D),
                                   (self.i_wd, self.w_d, cfg.DFF)):
                for r0, nr in _chunks(rows, 1024):
                    sy.dma("pool", dst[l, r0:r0 + nr, :], src[l, r0:r0 + nr, :], "wcast%d" % (n % 4))
                    n += 1

    def load_cols(self, es, dst, dst_key, src_rows, n, tag):
        sy = self.sy
        tmp = self.sb(es, "lc_" + tag, [128, 128], F32)
        for j0, nj in _chunks(n, 128):
            sy.dma("sp", tmp[0:nj, :], src_rows[j0:j0 + nj, :], "lc", writes=["lc_tmp"])
            bank = self.ps[7]
            sy.op("pe", lambda e: e.transpose(out=bank[:, 0:nj], in_=tmp[0:nj, :], identity=self.identf[0:nj, 0:nj]),
                  reads=["lc_tmp", "identf"], writes=["ps7"])
            sy.op("dve", lambda e: e.tensor_copy(dst[:, j0:j0 + nj], bank[:, 0:nj]),
                  reads=["ps7"], writes=[dst_key])

    def phase_mod(self, l):
        sy, cfg = self.sy, self.cfg
        KC, D = cfg.KC, cfg.D
        NJ = 6 * KC
        with ExitStack() as es:
            ca = self.sb(es, "ca", [128, KC], F32)
            cas = self.sb(es, "cas", [128, KC], F32)
            bm = self.sb(es, "bm", [128, NJ], F32)
            wsl = [self.sb(es, "wms%d" % i, [128, KC, 256], F32) for i in range(2)]
            self.load_cols(es, ca, "ca", self.i_c, KC, "c")
            self.load_cols(es, bm, "bm", self.i_bmod[l], NJ, "b")
            sy.op("act", lambda e: e.activation(out=cas[:], in_=ca[:], func=AF.Silu),
                  reads=["ca"], writes=["cas"])
            psm = self.ps[6]
            blocks = _chunks(6 * D, 256)
            for bi, (c0, n) in enumerate(blocks):
                w = wsl[bi % 2]
                sy.dma("sp", w[:, :, 0:n],
                       self.i_wmod[l, :, c0:c0 + n].rearrange("(k p) n -> p k n", p=128),
                       "wms%d" % (bi % 2), writes=["wms%d" % (bi % 2)])
                for jj in range(n // 128):
                    j = c0 // 128 + jj
                    for k in range(KC):
                        sy.op("pe", lambda e: e.matmul(psm[:, j:j + 1], w[:, k, jj * 128:(jj + 1) * 128],
                                                       cas[:, k:k + 1], start=(k == 0), stop=(k == KC - 1)),
                              reads=["wms%d" % (bi % 2), "cas"], writes=["ps6"],
                              signal=(k == KC - 1))
            sy.op("dve", lambda e: e.tensor_tensor(self.modT[l][:], psm[:, 0:NJ], bm[:], ALU.add),
                  reads=["ps6", "bm"], writes=["modT%d" % l])
            sy.op("dve", lambda e: e.tensor_scalar_add(self.sc1p[l][:, 0:KC], self.modT[l][:, KC:2 * KC], 1.0),
                  reads=["modT%d" % l], writes=["sc1pa%d" % l])
            sy.op("dve", lambda e: e.tensor_scalar_add(self.sc1p[l][:, KC:2 * KC], self.modT[l][:, 4 * KC:5 * KC], 1.0),
                  reads=["modT%d" % l], writes=["sc1pb%d" % l])
            sy.barrier()

    def build_hT(self, xin, t0, T, hT, xst, scale_ap, shift_ap, tag):
        sy, cfg = self.sy, self.cfg
        KC = cfg.KC
        G = min(4, KC)
        nb = 0
        for sub in range(T // 128):
            xs = xst[sub % 2]
            xk = "xst%d" % (sub % 2)
            sy.dma("sp", xs[:], xin[t0 + sub * 128:t0 + (sub + 1) * 128, :], xk, writes=[xk])
            for kg in range(KC // G):
                bi = nb % 2
                nb += 1
                bank = self.ps[bi]
                for kk in range(G):
                    k = kg * G + kk
                    sy.op("pe", lambda e: e.transpose(out=bank[:, kk * 128:(kk + 1) * 128],
                                                      in_=xs[:, k * 128:(k + 1) * 128], identity=self.identf[:]),
                          reads=[xk], writes=["ps%d" % bi], signal=(kk == G - 1))
                for kk in range(G):
                    k = kg * G + kk
                    sy.op("act", lambda e: e.activation(out=hT[:, k, sub * 128:(sub + 1) * 128],
                                                        in_=bank[:, kk * 128:(kk + 1) * 128], func=AF.Identity,
                                                        bias=shift_ap[:, k:k + 1], scale=scale_ap[:, k:k + 1]),
                          reads=["ps%d" % bi], writes=[tag])

    def phase_inproj(self, l, xin, xout):
        sy, cfg = self.sy, self.cfg
        KC, D, S = cfg.KC, cfg.D, cfg.S
        T = min(512, S)
        NXC = cfg.XBC // 128
        AW, SW = cfg.AW, cfg.SW
        segs = [("q", 0, AW, "a"), ("k", AW, AW, "a"), ("v", 2 * AW, AW, "b"),
                ("z", 3 * AW, SW, "b"), ("xbc", 3 * AW + SW, cfg.XBC, "a"),
                ("dt", 3 * AW + SW + cfg.XBC, cfg.NHS, "b")]
        blocks = []
        for name, c0, n, form in segs:
            for b0, bn in _chunks(n, 512):
                blocks.append((name, c0, b0, bn, form))
        with ExitStack() as es:
            hT = self.sb(es, "hT", [128, KC, T], BF16)
            wb = [self.sb(es, "wb%d" % i, [128, KC, 512], BF16) for i in range(2)]
            xst = [self.sb(es, "xst%d" % i, [128, D], F32) for i in range(2)]
            ost = [self.sb(es, "ost%d" % i, [128, 512], F32) for i in range(3)]
            obf = [self.sb(es, "obf%d" % i, [128, 512], BF16) for i in range(3)]
            xpad = [self.sb(es, "xpad%d" % i, [128, 3 + 512], F32) for i in range(2)]
            cvt = [self.sb(es, "cvt%d" % i, [128, 512], F32) for i in range(2)]
            halo = self.sb(es, "halo", [128, NXC, 3], F32)
            cw = self.sb(es, "cw", [128, 4 * NXC], F32)
            cb = self.sb(es, "cb", [128, NXC], F32)
            dtb = self.sb(es, "dtb", [128, cfg.NHS], F32)
            sp1 = self.sb(es, "sp1", [128, cfg.NHS], F32)
            sp2 = self.sb(es, "sp2", [128, cfg.NHS], F32)
            sp3 = self.sb(es, "sp3", [128, cfg.NHS], F32)
            self.load_cols(es, cw, "cw", self.i_scw[l], 4 * NXC, "cw")
            self.load_cols(es, cb, "cb", self.i_scb[l], NXC, "cb")
            sy.dma("sp", dtb[:], self.i_dtb[l].to_broadcast([128, cfg.NHS]), "dtb", writes=["dtb"])
            sy.op("pool", lambda e: e.memset(halo[:], 0.0), writes=["halo%d" % i for i in range(NXC)])
            scale_ap = self.sc1p[l][:, 0:KC]
            shift_ap = self.modT[l][:, 0:KC]
            nev = 0
            nst = 0
            for tt in range(S // T):
                t0 = tt * T
                self.build_hT(xin, t0, T, hT, xst, scale_ap, shift_ap, "hT")
                for bi, (name, c0, b0, bn, form) in enumerate(blocks):
                    gi = tt * len(blocks) + bi
                    w = wb[gi % 2]
                    wk = "wb%d" % (gi % 2)
                    sy.dma("sp", w[:, :, 0:bn],
                           self.w_in[l, :, c0 + b0:c0 + b0 + bn].rearrange("(k p) n -> p k n", p=128),
                           wk, writes=[wk])
                    if form == "a":
                        for oc in range(bn // 128):
                            pb = 2 + nev % 4
                            nev += 1
                            bank = self.ps[pb]
                            pk = "ps%d" % pb
                            for k in range(KC):
                                sy.op("pe", lambda e: e.matmul(bank[:, 0:T], w[:, k, oc * 128:(oc + 1) * 128],
                                                               hT[:, k, :], start=(k == 0), stop=(k == KC - 1)),
                                      reads=[wk, "hT"], writes=[pk], signal=(k == KC - 1))
                            fc = (b0 + oc * 128) // 128
                            si = nst % 3
                            nst += 1
                            if name == "q":
                                sy.op("act", lambda e: e.activation(out=obf[si][:, 0:T], in_=bank[:, 0:T],
                                                                    func=AF.Copy, scale=1.0 / math.sqrt(128.0)),
                                      reads=[pk], writes=["obf%d" % si])
                                sy.dma("pool", self.qT[fc * 128:(fc + 1) * 128, t0:t0 + T], obf[si][:, 0:T],
                                       "obf%d" % si, reads=["obf%d" % si])
                            elif name == "k":
                                sy.op("dve", lambda e: e.tensor_copy(obf[si][:, 0:T], bank[:, 0:T]),
                                      reads=[pk], writes=["obf%d" % si])
                                sy.dma("pool", self.kT[fc * 128:(fc + 1) * 128, t0:t0 + T], obf[si][:, 0:T],
                                       "obf%d" % si, reads=["obf%d" % si])
                            else:
                                xp = xpad[fc % 2]
                                xpk = "xpad%d" % (fc % 2)
                                cv = cvt[fc % 2]
                                cvk = "cvt%d" % (fc % 2)
                                sy.op("act", lambda e: e.activation(out=xp[:, 3:3 + T], in_=bank[:, 0:T], func=AF.Copy),
                                      reads=[pk], writes=[xpk + "m"])
                                sy.op("pool", lambda e: e.tensor_copy(xp[:, 0:3], halo[:, fc, :]),
                                      reads=["halo%d" % fc], writes=[xpk + "h"])
                                sy.op("dve", lambda e: e.tensor_scalar(cv[:, 0:T], xp[:, 3:3 + T],
                                                                       cw[:, 3 * NXC + fc:3 * NXC + fc + 1],
                                                                       cb[:, fc:fc + 1], ALU.mult, ALU.add),
                                      reads=[xpk + "m", xpk + "h", "cw", "cb"], writes=[cvk])
                                for j in range(3):
                                    sy.op("dve", lambda e: e.scalar_tensor_tensor(
                                        cv[:, 0:T], xp[:, j:j + T], cw[:, j * NXC + fc:j * NXC + fc + 1],
                                        cv[:, 0:T], ALU.mult, ALU.add),
                                        reads=[xpk + "m", xpk + "h", cvk], writes=[cvk])
                                sy.op("pool", lambda e: e.tensor_copy(halo[:, fc, :], xp[:, T:T + 3]),
                                      reads=[xpk + "m"], writes=["halo%d" % fc])
                                if fc < SW // 128:
                                    sy.op("act", lambda e: e.activation(out=ost[si][:, 0:T], in_=cv[:, 0:T], func=AF.Silu),
                                          reads=[cvk], writes=["ost%d" % si])
                                    sy.dma("pool", self.xsT[fc * 128:(fc + 1) * 128, t0:t0 + T], ost[si][:, 0:T],
                                           "ost%d" % si, reads=["ost%d" % si])
                                else:
                                    sy.op("act", lambda e: e.activation(out=obf[si][:, 0:T], in_=cv[:, 0:T], func=AF.Silu),
                                          reads=[cvk], writes=["obf%d" % si])
                                    g = fc - SW // 128
                                    dst = self.BT if g < cfg.NG else self.CT
                                    g = g % cfg.NG
                                    sy.dma("pool", dst[g * 128:(g + 1) * 128, t0:t0 + T], obf[si][:, 0:T],
                                           "obf%d" % si, reads=["obf%d" % si])
                    else:
                        for sub in range(T // 128):
                            pb = 2 + nev % 4
                            nev += 1
                            bank = self.ps[pb]
                            pk = "ps%d" % pb
                            for k in range(KC):
                                sy.op("pe", lambda e: e.matmul(bank[:, 0:bn], hT[:, k, sub * 128:(sub + 1) * 128],
                                                               w[:, k, 0:bn], start=(k == 0), stop=(k == KC - 1)),
                                      reads=[wk, "hT"], writes=[pk], signal=(k == KC - 1))
                            r0 = t0 + sub * 128
                            si = nst % 3
                            nst += 1
                            if name == "v":
                                sy.op("dve", lambda e: e.tensor_copy(obf[si][:, 0:bn], bank[:, 0:bn]),
                                      reads=[pk], writes=["obf%d" % si])
                                sy.dma("pool", self.v[r0:r0 + 128, b0:b0 + bn], obf[si][:, 0:bn],
                                       "obf%d" % si, reads=["obf%d" % si])
                            elif name == "z":
                                sy.op("act", lambda e: e.activation(out=ost[si][:, 0:bn], in_=bank[:, 0:bn], func=AF.Silu),
                                      reads=[pk], writes=["ost%d" % si])
                                sy.dma("pool", self.sz[r0:r0 + 128, b0:b0 + bn], ost[si][:, 0:bn],
                                       "ost%d" % si, reads=["ost%d" % si])
                            else:
                                n = bn
                                sy.op("dve", lambda e: e.tensor_tensor(sp1[:, 0:n], bank[:, 0:n], dtb[:, 0:n], ALU.add),
                                      reads=[pk, "dtb"], writes=["sp1"])
                                sy.op("dve", lambda e: e.scalar_tensor_tensor(sp2[:, 0:n], sp1[:, 0:n], -1.0, sp1[:, 0:n], ALU.mult, ALU.max),
                                      reads=["sp1"], writes=["sp2"])
                                sy.op("act", lambda e: e.activation(out=sp3[:, 0:n], in_=sp2[:, 0:n], func=AF.Exp, scale=-1.0),
                                      reads=["sp2"], writes=["sp3"])
                                sy.op("act", lambda e: e.activation(out=sp2[:, 0:n], in_=sp3[:, 0:n], func=AF.Ln, bias=1.0),
                                      reads=["sp3"], writes=["sp2"])
                                sy.op("dve", lambda e: e.scalar_tensor_tensor(ost[si][:, 0:n], sp1[:, 0:n], 0.0, sp2[:, 0:n],
                                                                              ALU.max, ALU.add),
                                      reads=["sp1", "sp2"], writes=["ost%d" % si])
                                sy.dma("pool", self.dt[r0:r0 + 128, :], ost[si][:, 0:n],
                                       "ost%d" % si, reads=["ost%d" % si])

    def bcast_load(self, dst, src_row, n, lane, key):
        self.sy.dma("sp", dst[:, 0:n], src_row.to_broadcast([128, n]), lane, writes=[key])

    def psb(self, i):
        return self.ps[i][:].bitcast(BF16)

    def phase_attn(self, l, xin, xout):
        sy, cfg = self.sy, self.cfg
        S, NH = cfg.S, cfg.NH
        NB = S // 128
        NSTR = 2
        lam_init = 0.8 - 0.6 * math.exp(-0.3 * l)
        with ExitStack() as es:
            Dm = self.sb(es, "Dm", [128, S], F32)
            M2 = self.sb(es, "M2", [128, 128], F32)
            lamb = self.sb(es, "lamb", [128, 512], F32)
            lt = self.sb(es, "lt", [128, 256], F32)
            lsm = self.sb(es, "lsm", [128, 2], F32)
            lex = self.sb(es, "lex", [128, 2], F32)
            nlam = self.sb(es, "nlam", [128, 1], F32)
            nw = self.sb(es, "nw", [128, 256], F32)
            nws = self.sb(es, "nws", [128, 256], F32)
            kTs = [self.sb(es, "kTs%d" % j, [128, S], BF16) for j in range(2)]
            qTs = [self.sb(es, "qTs%d" % j, [128, S], BF16) for j in range(2)]
            V = self.sb(es, "V", [128, NB, 256], BF16)
            ob = [self.sb(es, "ob%d" % i, [128, 256], BF16) for i in range(2)]
            st = []
            for t in range(NSTR):
                d = {}
                d["sc"] = [self.sb(es, "sc%d" % j, [128, S], F32) for j in range(2)]
                d["Ab"] = self.sb(es, "Ab", [128, S], BF16)
                d["AT"] = self.sb(es, "AT", [128, NB, 128], BF16)
                for nm, w in (("mx", 2), ("nmx", 2), ("sm", 2), ("rs", 2), ("t1", 1), ("cc", 1), ("ss", 1), ("rstd", 1)):
                    d[nm] = self.sb(es, nm, [128, w], F32)
                d["o"] = self.sb(es, "o", [128, 256], F32)
                d["junk"] = self.sb(es, "junk", [128, 256], F32)
                st.append(d)
            sy.op("pool", lambda e: e.iota(Dm[:], pattern=[[-1, S]], base=S - 128, channel_multiplier=1,
                                           allow_small_or_imprecise_dtypes=True), writes=["Dm"])
            sy.op("dve", lambda e: e.scalar_tensor_tensor(Dm[:], Dm[:], -1.0, Dm[:], ALU.mult, ALU.min),
                  reads=["Dm"], writes=["Dm"])
            sy.op("pool", lambda e: e.memset(M2[:], 0.0), writes=["M2"])
            sy.op("pool", lambda e: e.memset(M2[0:64, 64:128], -30000.0), reads=["M2"], writes=["M2"])
            self.bcast_load(lamb, self.i_lam[l], 512, "lamb", "lamb")
            self.bcast_load(nw, self.i_dnw[l], 256, "nw", "nw")
            sy.op("dve", lambda e: e.tensor_tensor(lt[:, 0:128], lamb[:, 0:128], lamb[:, 128:256], ALU.mult),
                  reads=["lamb"], writes=["lt"])
            sy.op("dve", lambda e: e.tensor_tensor(lt[:, 128:256], lamb[:, 256:384], lamb[:, 384:512], ALU.mult),
                  reads=["lamb", "lt"], writes=["lt"])
            sy.op("dve", lambda e: e.reduce_sum(lsm[:, 0:1], lt[:, 0:128], AX.X), reads=["lt"], writes=["lsm"])
            sy.op("dve", lambda e: e.reduce_sum(lsm[:, 1:2], lt[:, 128:256], AX.X), reads=["lt", "lsm"], writes=["lsm"])
            sy.op("act", lambda e: e.activation(out=lex[:], in_=lsm[:], func=AF.Exp), reads=["lsm"], writes=["lex"])
            sy.op("dve", lambda e: e.scalar_tensor_tensor(nlam[:], lex[:, 1:2], -lam_init, lex[:, 0:1], ALU.add, ALU.subtract),
                  reads=["lex"], writes=["nlam"])
            sy.op("dve", lambda e: e.tensor_scalar_mul(nws[:], nw[:], 1.0 - lam_init), reads=["nw"], writes=["nws"])
            cnt = {"sb": 0, "ob": 0}

            def qblock(t, h, i, slope):
                d = st[t]
                sc, Ab, AT = d["sc"], d["Ab"], d["AT"]
                K_ = lambda nm: "%s_%d" % (nm, t)
                q0 = i * 128
                kend = q0 + 128
                off = S - 128 - q0
                for j in range(2):
                    sk = K_("sc%d" % j)
                    for c0, n in _chunks(kend, 512):
                        pb = 2 * t + (cnt["sb"] % 2)
                        cnt["sb"] += 1
                        bank = self.ps[pb]
                        sy.op("pe", lambda e: e.matmul(bank[:, 0:n], qTs[j][:, q0:q0 + 128], kTs[j][:, c0:c0 + n],
                                                       start=True, stop=True),
                              reads=["qTs%d" % j, "kTs%d" % j], writes=["ps%d" % pb])
                        sy.op("dve", lambda e: e.scalar_tensor_tensor(sc[j][:, c0:c0 + n], Dm[:, off + c0:off + c0 + n],
                                                                      slope, bank[:, 0:n], ALU.mult, ALU.add),
                              reads=["ps%d" % pb, "Dm"], writes=[sk])
                    sy.op("dve", lambda e: e.tensor_tensor(sc[j][:, q0:kend], sc[j][:, q0:kend], M2[:], ALU.add),
                          reads=[sk, "M2"], writes=[sk])
                    sy.op("dve", lambda e: e.reduce_max(d["mx"][:, j:j + 1], sc[j][:, 0:kend], AX.X),
                          reads=[sk], writes=[K_("mx%d" % j)])
                    yield
                sy.op("dve", lambda e: e.tensor_scalar_mul(d["nmx"][:], d["mx"][:], -1.0),
                      reads=[K_("mx0"), K_("mx1")], writes=[K_("nmx")])
                sy.op("dve", lambda e: e.memset(d["sm"][:], 0.0), writes=[K_("sm")])
                for j in range(2):
                    sk = K_("sc%d" % j)
                    sy.op("act", lambda e: e.activation(out=sc[j][:, 0:kend], in_=sc[j][:, 0:kend], func=AF.Exp,
                                                        bias=d["nmx"][:, j:j + 1], scale=1.0, accum_out=d["sm"][:, j:j + 1]),
                          reads=[sk, K_("nmx"), K_("sm")], writes=[sk, K_("sm")])
                yield
                sy.op("dve", lambda e: e.reciprocal(d["rs"][:], d["sm"][:]), reads=[K_("sm")], writes=[K_("rs")])
                sy.op("dve", lambda e: e.tensor_tensor(d["t1"][:], d["sm"][:, 0:1], d["rs"][:, 1:2], ALU.mult),
                      reads=[K_("sm"), K_("rs")], writes=[K_("t1")])
                sy.op("dve", lambda e: e.tensor_tensor(d["cc"][:], d["t1"][:], nlam[:], ALU.mult),
                      reads=[K_("t1"), "nlam"], writes=[K_("cc")])
                sy.op("dve", lambda e: e.scalar_tensor_tensor(Ab[:, 0:kend], sc[1][:, 0:kend], d["cc"][:, 0:1],
                                                              sc[0][:, 0:kend], ALU.mult, ALU.add),
                      reads=[K_("sc0"), K_("sc1"), K_("cc")], writes=[K_("Ab")])
                yield
                for g0, gn in _chunks(i + 1, 8):
                    tb = 4 + t
                    pbb = self.psb(tb)
                    for kk in range(gn):
                        kb = g0 + kk
                        sy.op("pe", lambda e: e.transpose(out=pbb[:, kk * 128:(kk + 1) * 128],
                                                          in_=Ab[:, kb * 128:(kb + 1) * 128], identity=self.identb[:]),
                              reads=[K_("Ab")], writes=["ps%d" % tb], signal=(kk == gn - 1))
                    sy.op("act", lambda e: e.activation(out=AT[:, g0:g0 + gn, :].rearrange("p a b -> p (a b)"),
                                                        in_=pbb[:, 0:gn * 128], func=AF.Copy),
                          reads=["ps%d" % tb], writes=[K_("AT")])
                    yield
                ob_i = 6 + t
                po = self.ps[ob_i]
                for kb in range(i + 1):
                    sy.op("pe", lambda e: e.matmul(po[:, 0:256], AT[:, kb, :], V[:, kb, :],
                                                   start=(kb == 0), stop=(kb == i)),
                          reads=[K_("AT"), "V"], writes=["ps%d" % ob_i], signal=(kb == i))
                yield
                sy.op("dve", lambda e: e.tensor_scalar_mul(d["o"][:], po[:, 0:256], d["rs"][:, 0:1]),
                      reads=["ps%d" % ob_i, K_("rs")], writes=[K_("o")])
                sy.op("dve", lambda e: e.memset(d["ss"][:], 0.0), writes=[K_("ss")])
                sy.op("act", lambda e: e.activation(out=d["junk"][:], in_=d["o"][:], func=AF.Square, accum_out=d["ss"][:]),
                      reads=[K_("o"), K_("ss")], writes=[K_("junk"), K_("ss")])
                sy.op("act", lambda e: e.activation(out=d["rstd"][:], in_=d["ss"][:], func=AF.Sqrt, bias=self.epsr[:, 0:1],
                                                    scale=1.0 / 256.0),
                      reads=[K_("ss")], writes=[K_("rstd")])
                sy.op("dve", lambda e: e.reciprocal(d["rstd"][:], d["rstd"][:]), reads=[K_("rstd")], writes=[K_("rstd")])
                obb = ob[cnt["ob"] % 2]
                obk = "ob%d" % (cnt["ob"] % 2)
                cnt["ob"] += 1
                sy.op("dve", lambda e: e.scalar_tensor_tensor(obb[:], d["o"][:], d["rstd"][:, 0:1], nws[:], ALU.mult, ALU.mult),
                      reads=[K_("o"), K_("rstd"), "nws"], writes=[obk])
                sy.dma("pool", self.mix[q0:q0 + 128, h * 256:(h + 1) * 256], obb[:], obk, reads=[obk])

            for h in range(NH):
                slope = 2.0 ** (-8.0 * (h + 1) / NH)
                for j in range(2):
                    r0 = (2 * h + j) * 128
                    sy.dma("sp", kTs[j][:], self.kT[r0:r0 + 128, :], "kTs%d" % j, writes=["kTs%d" % j])
                    sy.dma("sp", qTs[j][:], self.qT[r0:r0 + 128, :], "qTs%d" % j, writes=["qTs%d" % j])
                sy.dma("sp", V[:], self.v[:, h * 256:(h + 1) * 256].rearrange("(b p) e -> p b e", p=128),
                       "V", writes=["V"])
                gens = []
                free = list(range(NSTR))
                nxt = 0
                while nxt < NB or gens:
                    if nxt < NB and free:
                        t_ = free.pop(0)
                        gens.append((t_, qblock(t_, h, nxt, slope)))
                        nxt += 1
                    for item in list(gens):
                        try:
                            next(item[1])
                        except StopIteration:
                            gens.remove(item)
                            free.append(item[0])

    def phase_ssd(self, l, xin, xout):
        sy, cfg = self.sy, self.cfg
        S, NG, NHS, SW, AW = cfg.S, cfg.NG, cfg.NHS, cfg.SW, cfg.AW
        NB = S // 128
        with ExitStack() as es:
            xsTg = self.sb(es, "xsTg", [128, 2, S], F32)
            BTg = self.sb(es, "BTg", [128, S], BF16)
            CTg = self.sb(es, "CTg", [128, S], BF16)
            dta = self.sb(es, "dta", [128, NB, NHS], F32)
            Arow = self.sb(es, "Arow", [128, NHS], F32)
            Drow = self.sb(es, "Drow", [128, NHS], F32)
            Dfull = self.sb(es, "Dfull", [128, NHS, 64], F32)
            normw = self.sb(es, "normw", [128, SW], F32)
            prev = self.sb(es, "prev", [128, 256], F32)
            prevb = self.sb(es, "prevb", [128, 256], BF16)
            P = []
            for par in range(2):
                d = {}
                d["xs_t"] = self.sb(es, "xs_t", [128, 256], F32)
                d["Bt"] = self.sb(es, "Bt", [128, 128], BF16)
                for nm in ("da", "nacs", "dst", "w2"):
                    d[nm] = self.sb(es, nm, [128, 4], F32)
                for nm in ("daB", "eRow", "E", "E2"):
                    d[nm] = self.sb(es, nm, [128, 4, 128], F32)
                d["CBm"] = self.sb(es, "CBm", [128, 128], F32)
                d["LT"] = self.sb(es, "LT", [128, 4, 128], BF16)
                d["CeT"] = self.sb(es, "CeT", [128, 4, 128], BF16)
                d["xdt"] = self.sb(es, "xdt", [128, 4, 64], BF16)
                d["xdd"] = self.sb(es, "xdd", [128, 4, 64], BF16)
                P.append(d)
            tt1 = self.sb(es, "tt1", [128, 256], F32)
            yd = self.sb(es, "yd", [128, 256], F32)
            y1 = self.sb(es, "y1", [128, 256], F32)
            junk = self.sb(es, "junk", [128, 256], F32)
            szt = [self.sb(es, "szt%d" % i, [128, 256], F32) for i in range(2)]
            ss = self.sb(es, "ss", [128, 1], F32)
            rstd = self.sb(es, "rstd", [128, 1], F32)
            ob = [self.sb(es, "ob%d" % i, [128, 256], BF16) for i in range(2)]
            self.bcast_load(Arow, self.i_alog[l], NHS, "Arow", "Arow")
            sy.op("act", lambda e: e.activation(out=Arow[:], in_=Arow[:], func=AF.Exp), reads=["Arow"], writes=["Arow"])
            sy.op("dve", lambda e: e.tensor_scalar_mul(Arow[:], Arow[:], -1.0), reads=["Arow"], writes=["Arow"])
            self.bcast_load(Drow, self.i_sd[l], NHS, "Drow", "Drow")
            sy.op("dve", lambda e: e.tensor_copy(Dfull[:], Drow[:].unsqueeze(2).to_broadcast([128, NHS, 64])),
                  reads=["Drow"], writes=["Dfull"])
            self.bcast_load(normw, self.i_snw[l], SW, "normw", "normw")
            sy.dma("sp", dta[:], self.dt.rearrange("(b p) h -> p b h", p=128), "dta", writes=["dta"])

            def stage1(g, c):
                d = P[c % 2]
                K_ = lambda nm: "%s_%d" % (nm, c % 2)
                cs = slice(c * 128, (c + 1) * 128)
                dtc = dta[:, c, 4 * g:4 * g + 4]
                xs_t, Bt, da, daB, nacs, dst, w2 = d["xs_t"], d["Bt"], d["da"], d["daB"], d["nacs"], d["dst"], d["w2"]
                eRow, E, E2, CBm, LT, CeT, xdt, xdd = d["eRow"], d["E"], d["E2"], d["CBm"], d["LT"], d["CeT"], d["xdt"], d["xdd"]
                for f in range(2):
                    sy.op("pe", lambda e: e.transpose(out=self.ps[0][:, f * 128:(f + 1) * 128], in_=xsTg[:, f, cs],
                                                      identity=self.identf[:]),
                          reads=["xsTg"], writes=["ps0"], signal=(f == 1))
                sy.op("act", lambda e: e.activation(out=xs_t[:], in_=self.ps[0][:, 0:256], func=AF.Copy),
                      reads=["ps0"], writes=[K_("xs_t")])
                pb1 = self.psb(1)
                sy.op("pe", lambda e: e.transpose(out=pb1[:, 0:128], in_=BTg[:, cs], identity=self.identb[:]),
                      reads=["BTg"], writes=["ps1"])
                sy.op("dve", lambda e: e.tensor_copy(Bt[:], pb1[:, 0:128]), reads=["ps1"], writes=[K_("Bt")])
                sy.op("dve", lambda e: e.tensor_tensor(da[:], dtc, Arow[:, 4 * g:4 * g + 4], ALU.mult),
                      reads=["dta", "Arow"], writes=[K_("da")])
                sy.op("dve", lambda e: e.tensor_copy(daB[:], da[:].unsqueeze(2).to_broadcast([128, 4, 128])),
                      reads=[K_("da")], writes=[K_("daB")])
                for h in range(4):
                    sy.op("pe", lambda e: e.matmul(self.ps[2][:, h * 128:(h + 1) * 128], daB[:, h, :], self.tri[:],
                                                   start=True, stop=True),
                          reads=[K_("daB"), "tri"], writes=["ps2"], signal=(h == 3))
                sy.op("pe", lambda e: e.matmul(self.ps[3][:, 0:4], self.tri[:], da[:], start=True, stop=True),
                      reads=[K_("da"), "tri"], writes=["ps3"], signal=False)
                sy.op("pe", lambda e: e.matmul(self.ps[3][:, 4:8], self.ltri[:], da[:], start=True, stop=True),
                      reads=[K_("da"), "ltri"], writes=["ps3"])
                sy.op("pe", lambda e: e.matmul(self.ps[4][:, 0:128], BTg[:, cs], CTg[:, cs], start=True, stop=True),
                      reads=["BTg", "CTg"], writes=["ps4"])
                sy.op("dve", lambda e: e.tensor_scalar_mul(nacs[:], self.ps[3][:, 0:4], -1.0),
                      reads=["ps3"], writes=[K_("nacs")])
                sy.op("act", lambda e: e.activation(out=dst[:], in_=self.ps[3][:, 4:8], func=AF.Exp),
                      reads=["ps3"], writes=[K_("dst")])
                sy.op("act", lambda e: e.activation(out=eRow[:].rearrange("p a b -> p (a b)"), in_=self.ps[2][:, 0:512],
                                                    func=AF.Exp),
                      reads=["ps2"], writes=[K_("eRow")])
                for h in range(4):
                    sy.op("act", lambda e: e.activation(out=E[:, h, :], in_=self.ps[2][:, h * 128:(h + 1) * 128], func=AF.Exp,
                                                        bias=nacs[:, h:h + 1], scale=1.0),
                          reads=["ps2", K_("nacs")], writes=[K_("E")])
                sy.op("dve", lambda e: e.tensor_scalar_min(E2[:].rearrange("p a b -> p (a b)"),
                                                           E[:].rearrange("p a b -> p (a b)"), 1.0),
                      reads=[K_("E")], writes=[K_("E2")])
                sy.op("dve", lambda e: e.tensor_tensor(CBm[:], self.ps[4][:, 0:128], self.tri[:], ALU.mult),
                      reads=["ps4", "tri"], writes=[K_("CBm")])
                sy.op("dve", lambda e: e.tensor_tensor(LT[:], E2[:], CBm[:].unsqueeze(1).to_broadcast([128, 4, 128]), ALU.mult),
                      reads=[K_("E2"), K_("CBm")], writes=[K_("LT")])
                sy.op("pool", lambda e: e.tensor_tensor(CeT[:], eRow[:], CTg[:, cs].unsqueeze(1).to_broadcast([128, 4, 128]),
                                                        ALU.mult),
                      reads=[K_("eRow"), "CTg"], writes=[K_("CeT")])
                xv = xs_t[:].rearrange("p (h e) -> p h e", h=4)
                sy.op("dve", lambda e: e.tensor_tensor(xdt[:], xv, dtc.unsqueeze(2).to_broadcast([128, 4, 64]), ALU.mult),
                      reads=[K_("xs_t"), "dta"], writes=[K_("xdt")])
                sy.op("dve", lambda e: e.tensor_tensor(w2[:], dtc, dst[:], ALU.mult), reads=["dta", K_("dst")], writes=[K_("w2")])
                sy.op("dve", lambda e: e.tensor_tensor(xdd[:], xv, w2[:].unsqueeze(2).to_broadcast([128, 4, 64]), ALU.mult),
                      reads=[K_("xs_t"), K_("w2")], writes=[K_("xdd")])

            def stage2(g, c):
                d = P[c % 2]
                K_ = lambda nm: "%s_%d" % (nm, c % 2)
                xs_t, Bt, eRow, LT, CeT, xdt, xdd = d["xs_t"], d["Bt"], d["eRow"], d["LT"], d["CeT"], d["xdt"], d["xdd"]
                szb = szt[c % 2]
                szk = "szt%d" % (c % 2)
                sy.dma("sp", szb[:], self.sz[c * 128:(c + 1) * 128, g * 256:(g + 1) * 256], szk, writes=[szk])
                for h in range(4):
                    sy.op("pe", lambda e: e.matmul(self.ps[5][:, h * 64:(h + 1) * 64], LT[:, h, :], xdt[:, h, :],
                                                   start=True, stop=False),
                          reads=[K_("LT"), K_("xdt")], writes=["ps5"], signal=False)
                    sy.op("pe", lambda e: e.matmul(self.ps[5][:, h * 64:(h + 1) * 64], CeT[:, h, :],
                                                   prevb[:, h * 64:(h + 1) * 64], start=False, stop=True),
                          reads=[K_("CeT"), "prevb"], writes=["ps5"], signal=(h == 3))
                sy.op("pe", lambda e: e.matmul(self.ps[6][:, 0:256], Bt[:], xdd[:].rearrange("p h e -> p (h e)"),
                                               start=True, stop=True),
                      reads=[K_("Bt"), K_("xdd")], writes=["ps6"])
                sy.op("dve", lambda e: e.tensor_tensor(tt1[:].rearrange("p (h e) -> p h e", h=4),
                                                       prev[:].rearrange("p (h e) -> p h e", h=4),
                                                       eRow[:, :, 127:128].to_broadcast([128, 4, 64]), ALU.mult),
                      reads=["prev", K_("eRow")], writes=["tt1"])
                sy.op("dve", lambda e: e.tensor_tensor(prev[:], tt1[:], self.ps[6][:, 0:256], ALU.add),
                      reads=["tt1", "ps6"], writes=["prev"])
                sy.op("act", lambda e: e.activation(out=prevb[:], in_=prev[:], func=AF.Copy),
                      reads=["prev"], writes=["prevb"])
                sy.op("pool", lambda e: e.tensor_tensor(yd[:], xs_t[:],
                                                        Dfull[:, 4 * g:4 * g + 4, :].rearrange("p h e -> p (h e)"), ALU.mult),
                      reads=[K_("xs_t"), "Dfull"], writes=["yd"])
                sy.op("dve", lambda e: e.tensor_tensor(y1[:], self.ps[5][:, 0:256], yd[:], ALU.add),
                      reads=["ps5", "yd"], writes=["y1"])
                sy.op("dve", lambda e: e.tensor_tensor(y1[:], y1[:], szb[:], ALU.mult),
                      reads=["y1", szk], writes=["y1"])
                sy.op("dve", lambda e: e.memset(ss[:], 0.0), writes=["ss"])
                sy.op("act", lambda e: e.activation(out=junk[:], in_=y1[:], func=AF.Square, accum_out=ss[:]),
                      reads=["y1", "ss"], writes=["junk", "ss"])
                sy.op("act", lambda e: e.activation(out=rstd[:], in_=ss[:], func=AF.Sqrt, bias=self.epsr[:, 0:1], scale=1.0 / 256.0),
                      reads=["ss"], writes=["rstd"])
                sy.op("dve", lambda e: e.reciprocal(rstd[:], rstd[:]), reads=["rstd"], writes=["rstd"])
                obb = ob[c % 2]
                obk = "ob%d" % (c % 2)
                sy.op("dve", lambda e: e.scalar_tensor_tensor(obb[:], y1[:], rstd[:, 0:1], normw[:, g * 256:(g + 1) * 256],
                                                              ALU.mult, ALU.mult),
                      reads=["y1", "rstd", "normw"], writes=[obk])
                sy.dma("pool", self.mix[c * 128:(c + 1) * 128, AW + g * 256:AW + (g + 1) * 256], obb[:], obk, reads=[obk])

            for g in range(NG):
                for f in range(2):
                    sy.dma("sp", xsTg[:, f, :], self.xsT[(2 * g + f) * 128:(2 * g + f + 1) * 128, :], "xsTg", writes=["xsTg"])
                sy.dma("sp", BTg[:], self.BT[g * 128:(g + 1) * 128, :], "BTg", writes=["BTg"])
                sy.dma("sp", CTg[:], self.CT[g * 128:(g + 1) * 128, :], "CTg", writes=["CTg"])
                sy.op("dve", lambda e: e.memset(prev[:], 0.0), writes=["prev"])
                sy.op("dve", lambda e: e.memset(prevb[:], 0.0), writes=["prevb"])
                stage1(g, 0)
                for c in range(NB):
                    if c + 1 < NB:
                        stage1(g, c + 1)
                    stage2(g, c)

    def proj_ln(self, l, KA, load_A, w_d, wkey, resid, gate_off, lng_in, lnb_in, dst, T, KH=1, KS=16):
        sy, cfg = self.sy, self.cfg
        D, S, KC = cfg.D, cfg.S, cfg.KC
        KAH = KA // KH
        nD = (D + 511) // 512
        NS = T // 128
        with ExitStack() as es:
            AT = self.sb(es, "plA", [128, KAH, T], BF16)
            wb = [self.sb(es, "plw%d" % i, [128, KS, 512], BF16) for i in range(2)]
            res = [self.sb(es, "plres%d" % i, [128, D], F32) for i in range(NS)]
            gate_b = self.sb(es, "plgate", [128, D], F32)
            lng = self.sb(es, "pllng", [128, D], F32)
            lnb = self.sb(es, "pllnb", [128, D], F32)
            tmp = [self.sb(es, "pltmp%d" % i, [128, 512], F32) for i in range(2)]
            st = self.sb(es, "plst", [128, nD, 6], F32)
            mv = self.sb(es, "plmv", [128, 2], F32)
            rstd = self.sb(es, "plrstd", [128, 1], F32)
            nmr = self.sb(es, "plnmr", [128, 1], F32)
            self.bcast_load(lng, lng_in, D, "pllng", "lng")
            self.bcast_load(lnb, lnb_in, D, "pllnb", "lnb")
            dg = self.sb(es, "pldg", [128, 128], F32)
            gate_col0 = gate_off // 128
            for k in range(KC):
                sy.op("dve", lambda e: e.tensor_scalar_mul(dg[:], self.identf[:], self.modT[l][:, gate_col0 + k:gate_col0 + k + 1]),
                      reads=["identf"], writes=["dg"])
                sy.op("pe", lambda e: e.matmul(self.ps[7][:, 0:128], self.onesf[:], dg[:], start=True, stop=True),
                      reads=["dg", "onesf"], writes=["ps7"])
                sy.op("act", lambda e: e.activation(out=gate_b[:, k * 128:(k + 1) * 128], in_=self.ps[7][:, 0:128], func=AF.Copy),
                      reads=["ps7"], writes=["gate_b"])
            nw = 0
            nt = 0
            for tt in range(S // T):
                t0 = tt * T
                for sub in range(NS):
                    sy.dma("sp", res[sub][:], resid[t0 + sub * 128:t0 + (sub + 1) * 128, :], "plres%d" % sub,
                           writes=["res%d" % sub])
                for hf in range(KH):
                    kbase = hf * KAH
                    load_A(AT, t0, T, es, kbase, KAH)
                    kslabs = _chunks(KAH, KS)
                    for c0, n in _chunks(D, 512):
                        for si, (k0, kn) in enumerate(kslabs):
                            w = wb[nw % 2]
                            wk = "plw%d" % (nw % 2)
                            nw += 1
                            sy.dma("sp", w[:, 0:kn, 0:n],
                                   w_d[(kbase + k0) * 128:(kbase + k0 + kn) * 128, c0:c0 + n].rearrange("(k p) n -> p k n", p=128),
                                   wk, writes=[wk])
                            for sub in range(NS):
                                bank = self.ps[sub]
                                for kk in range(kn):
                                    k = k0 + kk
                                    sy.op("pe", lambda e: e.matmul(bank[:, 0:n], AT[:, k, sub * 128:(sub + 1) * 128], w[:, kk, 0:n],
                                                                   start=(k == 0), stop=(k == KAH - 1)),
                                          reads=[wk, "plA"], writes=["ps%d" % sub],
                                          signal=(kk == kn - 1))
                        for sub in range(NS):
                            bank = self.ps[sub]
                            tb = tmp[nt % 2]
                            tk = "pltmp%d" % (nt % 2)
                            nt += 1
                            sy.op("dve", lambda e: e.tensor_tensor(tb[:, 0:n], bank[:, 0:n], gate_b[:, c0:c0 + n], ALU.mult),
                                  reads=["ps%d" % sub, "gate_b"], writes=[tk])
                            if hf == 0:
                                sy.op("dve", lambda e: e.scalar_tensor_tensor(res[sub][:, c0:c0 + n], res[sub][:, c0:c0 + n],
                                                                              cfg.alpha, tb[:, 0:n], ALU.mult, ALU.add),
                                      reads=[tk, "res%d" % sub], writes=["res%d" % sub])
                            else:
                                sy.op("pool", lambda e: e.tensor_tensor(res[sub][:, c0:c0 + n], res[sub][:, c0:c0 + n],
                                                                        tb[:, 0:n], ALU.add),
                                      reads=[tk, "res%d" % sub], writes=["res%d" % sub])
                for sub in range(NS):
                    r = res[sub]
                    rk = "res%d" % sub
                    for i, (c0, n) in enumerate(_chunks(D, 512)):
                        sy.op("dve", lambda e: e.bn_stats(st[:, i, :], r[:, c0:c0 + n]), reads=[rk], writes=["plst"])
                    sy.op("dve", lambda e: e.bn_aggr(mv[:], st[:].rearrange("p a b -> p (a b)")), reads=["plst"], writes=["mv"])
                    sy.op("act", lambda e: e.activation(out=rstd[:], in_=mv[:, 1:2], func=AF.Sqrt, bias=self.epsl[:, 0:1], scale=1.0),
                          reads=["mv"], writes=["rstd"])
                    sy.op("dve", lambda e: e.reciprocal(rstd[:], rstd[:]), reads=["rstd"], writes=["rstd"])
                    sy.op("dve", lambda e: e.scalar_tensor_tensor(nmr[:], mv[:, 0:1], -1.0, rstd[:], ALU.mult, ALU.mult),
                          reads=["mv", "rstd"], writes=["nmr"])
                    sy.op("act", lambda e: e.activation(out=r[:], in_=r[:], func=AF.Identity, bias=nmr[:, 0:1], scale=rstd[:, 0:1]),
                          reads=[rk, "nmr", "rstd"], writes=[rk])
                    sy.op("dve", lambda e: e.tensor_tensor(r[:], r[:], lng[:], ALU.mult), reads=[rk, "lng"], writes=[rk])
                    sy.op("pool", lambda e: e.tensor_tensor(r[:], r[:], lnb[:], ALU.add), reads=[rk, "lnb"], writes=[rk])
                    sy.dma("pool", dst[t0 + sub * 128:t0 + (sub + 1) * 128, :], r[:], "plsto%d" % sub, reads=[rk])

    def phase_outproj(self, l, xin, xout):
        sy, cfg = self.sy, self.cfg
        KA = cfg.DMIX // 128
        T = min(256, cfg.S)
        holder = {}

        def load_A(AT, t0, T_, es, kbase, kn_):
            if "mst" not in holder:
                holder["mst"] = [self.sb(es, "mst%d" % i, [128, cfg.DMIX], BF16) for i in range(2)]
                holder["n"] = 0
            for sub in range(T_ // 128):
                m = holder["mst"][sub % 2]
                mk = "mst%d" % (sub % 2)
                sy.dma("sp", m[:], self.mix[t0 + sub * 128:t0 + (sub + 1) * 128, :], mk, writes=[mk])
                for g0, gn in _chunks(KA, 8):
                    tb = 4 + holder["n"] % 2
                    holder["n"] += 1
                    pbb = self.psb(tb)
                    for kk in range(gn):
                        k = g0 + kk
                        sy.op("pe", lambda e: e.transpose(out=pbb[:, kk * 128:(kk + 1) * 128], in_=m[:, k * 128:(k + 1) * 128],
                                                          identity=self.identb[:]),
                              reads=[mk], writes=["ps%d" % tb], signal=(kk == gn - 1))
                    for kk in range(gn):
                        k = g0 + kk
                        if kk % 2 == 0:
                            sy.op("act", lambda e: e.activation(out=AT[:, k, sub * 128:(sub + 1) * 128],
                                                                in_=pbb[:, kk * 128:(kk + 1) * 128], func=AF.Copy),
                                  reads=["ps%d" % tb], writes=["plA"])
                        else:
                            sy.op("dve", lambda e: e.tensor_copy(AT[:, k, sub * 128:(sub + 1) * 128], pbb[:, kk * 128:(kk + 1) * 128]),
                                  reads=["ps%d" % tb], writes=["plA"])

        self.proj_ln(l, KA, load_A, self.w_out[l], "wc_out%d" % l, xin, 2 * cfg.D, self.i_ln1g[l], self.i_ln1b[l], self.x1, T)

    def phase_ffn1(self, l, xin, xout):
        sy, cfg = self.sy, self.cfg
        KC, D, S, DFF = cfg.KC, cfg.D, cfg.S, cfg.DFF
        T = min(512, S)
        NFC = DFF // 128
        with ExitStack() as es:
            hT = self.sb(es, "h2T", [128, KC, T], BF16)
            wg = [self.sb(es, "wg%d" % i, [128, KC, 256], BF16) for i in range(2)]
            wu = [self.sb(es, "wu%d" % i, [128, KC, 256], BF16) for i in range(2)]
            xst = [self.sb(es, "xst%d" % i, [128, D], F32) for i in range(2)]
            gpad = [self.sb(es, "gpad%d" % i, [128, 2 + 512], F32) for i in range(2)]
            cvt = [self.sb(es, "cvt%d" % i, [128, 512], F32) for i in range(2)]
            sgt = [self.sb(es, "sgt%d" % i, [128, 512], F32) for i in range(2)]
            obf = [self.sb(es, "obf%d" % i, [128, 512], BF16) for i in range(3)]
            halo = self.sb(es, "fhalo", [128, NFC, 2], F32)
            cw = self.sb(es, "fcw", [128, 3 * NFC], F32)
            cb = self.sb(es, "fcb", [128, NFC], F32)
            self.load_cols(es, cw, "cw", self.i_fcw[l], 3 * NFC, "fcw")
            self.load_cols(es, cb, "cb", self.i_fcb[l], NFC, "fcb")
            sy.op("pool", lambda e: e.memset(halo[:], 0.0), writes=["halo%d" % i for i in range(NFC)])
            scale_ap = self.sc1p[l][:, KC:2 * KC]
            shift_ap = self.modT[l][:, 3 * KC:4 * KC]
            blocks = _chunks(DFF, 256)
            nev = 0
            nst = 0
            for tt in range(S // T):
                t0 = tt * T
                self.build_hT(self.x1, t0, T, hT, xst, scale_ap, shift_ap, "hT")
                for bi, (c0, bn) in enumerate(blocks):
                    gi = tt * len(blocks) + bi
                    wgb, wub = wg[gi % 2], wu[gi % 2]
                    wgk, wuk = "wg%d" % (gi % 2), "wu%d" % (gi % 2)
                    sy.dma("sp", wgb[:, :, 0:bn], self.w_g[l, :, c0:c0 + bn].rearrange("(k p) n -> p k n", p=128), wgk, writes=[wgk])
                    sy.dma("sp", wub[:, :, 0:bn], self.w_u[l, :, c0:c0 + bn].rearrange("(k p) n -> p k n", p=128), wuk, writes=[wuk])
                    for oc in range(bn // 128):
                        fc = c0 // 128 + oc
                        pg = 2 + (nev % 3)
                        pu = 5 + (nev % 3)
                        nev += 1
                        bg, bu = self.ps[pg], self.ps[pu]
                        for k in range(KC):
                            sy.op("pe", lambda e: e.matmul(bg[:, 0:T], wgb[:, k, oc * 128:(oc + 1) * 128], hT[:, k, :],
                                                           start=(k == 0), stop=(k == KC - 1)),
                                  reads=[wgk, "hT"], writes=["ps%d" % pg], signal=(k == KC - 1))
                        for k in range(KC):
                            sy.op("pe", lambda e: e.matmul(bu[:, 0:T], wub[:, k, oc * 128:(oc + 1) * 128], hT[:, k, :],
                                                           start=(k == 0), stop=(k == KC - 1)),
                                  reads=[wuk, "hT"], writes=["ps%d" % pu], signal=(k == KC - 1))
                        gp = gpad[fc % 2]
                        gpk = "gpad%d" % (fc % 2)
                        cv = cvt[fc % 2]
                        cvk = "cvt%d" % (fc % 2)
                        sg = sgt[fc % 2]
                        sgk = "sgt%d" % (fc % 2)
                        sy.op("act", lambda e: e.activation(out=gp[:, 2:2 + T], in_=bg[:, 0:T], func=AF.Copy),
                              reads=["ps%d" % pg], writes=[gpk + "m"])
                        sy.op("pool", lambda e: e.tensor_copy(gp[:, 0:2], halo[:, fc, :]),
                              reads=["halo%d" % fc], writes=[gpk + "h"])
                        sy.op("dve", lambda e: e.tensor_scalar(cv[:, 0:T], gp[:, 2:2 + T], cw[:, 2 * NFC + fc:2 * NFC + fc + 1],
                                                               cb[:, fc:fc + 1], ALU.mult, ALU.add),
                              reads=[gpk + "m", gpk + "h", "cw", "cb"], writes=[cvk])
                        for j in range(2):
                            sy.op("dve", lambda e: e.scalar_tensor_tensor(cv[:, 0:T], gp[:, j:j + T],
                                                                          cw[:, j * NFC + fc:j * NFC + fc + 1], cv[:, 0:T],
                                                                          ALU.mult, ALU.add),
                                  reads=[gpk + "m", gpk + "h", cvk], writes=[cvk])
                        sy.op("pool", lambda e: e.tensor_copy(halo[:, fc, :], gp[:, T:T + 2]),
                              reads=[gpk + "m"], writes=["halo%d" % fc])
                        sy.op("act", lambda e: e.activation(out=sg[:, 0:T], in_=cv[:, 0:T], func=AF.Silu),
                              reads=[cvk], writes=[sgk])
                        si = nst % 3
                        nst += 1
                        sy.op("dve", lambda e: e.tensor_tensor(obf[si][:, 0:T], sg[:, 0:T], bu[:, 0:T], ALU.mult),
                              reads=[sgk, "ps%d" % pu], writes=["obf%d" % si])
                        sy.dma("pool", self.actT[fc * 128:(fc + 1) * 128, t0:t0 + T], obf[si][:, 0:T], "obf%d" % si,
                               reads=["obf%d" % si])

    def phase_ffn2(self, l, xin, xout):
        sy, cfg = self.sy, self.cfg
        KA = cfg.DFF // 128
        T = min(512, cfg.S)
        KH = 2 if KA % 2 == 0 else 1

        def load_A(AT, t0, T_, es, kbase, kn_):
            for k0, kn in _chunks(kn_, 32):
                sy.dma("sp", AT[:, k0:k0 + kn, :],
                       self.actT[(kbase + k0) * 128:(kbase + k0 + kn) * 128, t0:t0 + T_].rearrange("(k p) t -> p k t", p=128),
                       "plA", writes=["plA"])

        self.proj_ln(l, KA, load_A, self.w_d[l], "wc_d%d" % l, self.x1, 5 * cfg.D, self.i_ln2g[l], self.i_ln2b[l], xout, T,
                     KH=KH, KS=8)


def make_in_map(cfg, inputs, b):
    f = lambda a: np.ascontiguousarray(a, dtype=np.float32)
    L = cfg.L
    return {
        "x": f(inputs["x"][b]),
        "c": f(inputs["c"][b]).reshape(cfg.KC, 128),
        "w_mod": f(inputs["w_mod"]),
        "b_mod": f(inputs["b_mod"]).reshape(L, 6 * cfg.KC, 128),
        "w_in": f(inputs["w_in"]),
        "diff_lambda": f(inputs["diff_lambda"]).reshape(L, 1, 512),
        "diff_norm_w": f(inputs["diff_norm_w"]).reshape(L, 1, 256),
        "ssd_conv_w": f(inputs["ssd_conv_w"]).reshape(L, 4 * cfg.XBC // 128, 128),
        "ssd_conv_b": f(inputs["ssd_conv_b"]).reshape(L, cfg.XBC // 128, 128),
        "ssd_dt_bias": f(inputs["ssd_dt_bias"]).reshape(L, 1, cfg.NHS),
        "ssd_a_log": f(inputs["ssd_a_log"]).reshape(L, 1, cfg.NHS),
        "ssd_d": f(inputs["ssd_d"]).reshape(L, 1, cfg.NHS),
        "ssd_norm_w": f(inputs["ssd_norm_w"]).reshape(L, 1, cfg.SW),
        "w_out": f(inputs["w_out"]),
        "ln1_g": f(inputs["ln1_g"]).reshape(L, 1, cfg.D),
        "ln1_b": f(inputs["ln1_b"]).reshape(L, 1, cfg.D),
        "w_gate": f(inputs["w_gate"]),
        "w_up": f(inputs["w_up"]),
        "ffn_conv_w": f(inputs["ffn_conv_w"]).reshape(L, 3 * cfg.DFF // 128, 128),
        "ffn_conv_b": f(inputs["ffn_conv_b"]).reshape(L, cfg.DFF // 128, 128),
        "w_down": f(inputs["w_down"]),
        "ln2_g": f(inputs["ln2_g"]).reshape(L, 1, cfg.D),
        "ln2_b": f(inputs["ln2_b"]).reshape(L, 1, cfg.D),
    }


def kernel(**inputs):
    cfg = Cfg()
    B = inputs["x"].shape[0]
    prog = Prog(cfg)
    nc = prog.build()
    in_maps = [make_in_map(cfg, inputs, b) for b in range(B)]
    res = run_bass_kernel_spmd(nc, in_maps, core_ids=list(range(B)))
    return np.stack([np.asarray(r["out"], dtype=np.float32) for r in res.results], axis=0)
```
